# Optimizing a Trainium2 kernel written in Bass

```python
import jax, jax.numpy as jnp
from jax import lax
import numpy as np

D_MODEL = 4096
BATCH = 1
SEQ = 8192
DEPTH = 2

EPS = 1e-6
SC_WIDTH = 2048
SC_KERNEL = 3
CF_WIDTH = 2048
CF_KERNEL = 31
HEAD_DIM = 128
DW_PATTERNS = ((128, 1), (512, 4), (2048, 16))
N_DW_GROUPS = len(DW_PATTERNS)
DW_HEADS = 8
DW_WIDTH = N_DW_GROUPS * DW_HEADS * HEAD_DIM
DW_OUT = DW_HEADS * HEAD_DIM
ROT_DIM = HEAD_DIM // 4
ROPE_THETA = 500000.0
N_BRANCH = 3
IN_SIZES = (SC_WIDTH, SC_WIDTH, SC_WIDTH, CF_WIDTH, CF_WIDTH, DW_WIDTH, DW_WIDTH, DW_WIDTH,
            D_MODEL, D_MODEL, D_MODEL)
IN_COLS = sum(IN_SIZES)
IN_SPLITS = [int(i) for i in np.cumsum(IN_SIZES)[:-1]]
D_FF = 4 * D_MODEL
N_MOD = 6

kernel_name = "hybrid_gated_conv_conformer_dilated_attn_block"


def rmsnorm(x, g):
    xf = x.astype(jnp.float32)
    y = xf * lax.rsqrt(jnp.mean(xf * xf, axis=-1, keepdims=True) + EPS)
    return (y * g.astype(jnp.float32)).astype(x.dtype)


def layernorm(x, g, b):
    xf = x.astype(jnp.float32)
    mu = jnp.mean(xf, axis=-1, keepdims=True)
    var = jnp.mean(jnp.square(xf - mu), axis=-1, keepdims=True)
    y = (xf - mu) * lax.rsqrt(var + EPS)
    return (y * g.astype(jnp.float32) + b.astype(jnp.float32)).astype(x.dtype)


def causal_depthwise_conv(x, w):
    k = w.shape[0]
    return lax.conv_general_dilated(
        x, w[:, None, :].astype(x.dtype), window_strides=(1,), padding=[(k - 1, 0)],
        dimension_numbers=("NWC", "WIO", "NWC"), feature_group_count=x.shape[-1])


def partial_rope(x, positions):
    half = ROT_DIM // 2
    inv_freq = ROPE_THETA ** (-jnp.arange(0, ROT_DIM, 2, dtype=jnp.float32) / ROT_DIM)
    ang = positions.astype(jnp.float32)[..., None] * inv_freq
    cos = jnp.cos(ang)[:, :, None, None, :]
    sin = jnp.sin(ang)[:, :, None, None, :]
    xf = x.astype(jnp.float32)
    x1, x2, rest = xf[..., :half], xf[..., half:ROT_DIM], xf[..., ROT_DIM:]
    out = jnp.concatenate([x1 * cos - x2 * sin, x2 * cos + x1 * sin, rest], axis=-1)
    return out.astype(x.dtype)


def dilated_window_attention(q, k, v, window, dilation):
    b, t, h, dh = q.shape
    n = window // dilation
    span = n * dilation
    tp = -(-t // span) * span
    nb = tp // span
    sub = tp // dilation

    def to_blocks(a):
        a = jnp.pad(a, ((0, 0), (0, tp - t), (0, 0), (0, 0)))
        a = a.reshape(b, sub, dilation, h, dh).transpose(0, 2, 1, 3, 4)
        return a.reshape(b, dilation, nb, n, h, dh)

    def band(a):
        prev = jnp.pad(a, ((0, 0), (0, 0), (1, 0), (0, 0), (0, 0), (0, 0)))[:, :, :-1]
        return jnp.concatenate([prev, a], axis=3)

    qb, kb, vb = to_blocks(q), band(to_blocks(k)), band(to_blocks(v))
    scores = jnp.einsum("brnqhe,brnkhe->brnhqk", qb, kb).astype(jnp.float32) * (dh ** -0.5)
    qi = jnp.arange(n)[:, None]
    kj = jnp.arange(2 * n)[None, :]
    in_window = (kj >= qi) & (kj <= qi + n)
    first = (jnp.arange(nb) == 0)[:, None, None]
    valid = in_window[None] & ~(first & (kj < n)[None])
    scores = jnp.where(valid[None, None, :, None], scores, -jnp.inf)
    m = jnp.max(scores, axis=-1, keepdims=True)
    p = jnp.exp(scores - m)
    s = jnp.sum(p, axis=-1, keepdims=True)
    o = jnp.einsum("brnhqk,brnkhe->brnqhe", (p / s).astype(v.dtype), vb)
    lse = (m + jnp.log(s))[..., 0]
    o = o.reshape(b, dilation, sub, h, dh).transpose(0, 2, 1, 3, 4).reshape(b, tp, h, dh)[:, :t]
    lse = lse.transpose(0, 1, 2, 4, 3).reshape(b, dilation, sub, h)
    lse = lse.transpose(0, 2, 1, 3).reshape(b, tp, h)[:, :t]
    return o, lse


def setup_inputs(seed: int = 0) -> dict:
    key = jax.random.key(seed)
    ks = jax.random.split(key, 24)
    f32 = jnp.float32

    def nrm(k, shape, scale):
        return jax.random.normal(k, shape, f32) * scale

    x = nrm(ks[0], (BATCH, SEQ, D_MODEL), 1.0)
    c = nrm(ks[1], (BATCH, D_MODEL), 1.0)
    offset = jax.random.randint(ks[2], (BATCH, 1), 0, 4096, dtype=jnp.int32)
    positions = offset + jnp.arange(SEQ, dtype=jnp.int32)[None, :]
    return {
        "x": x,
        "c": c,
        "positions": positions,
        "w_ada": nrm(ks[3], (DEPTH, D_MODEL, N_MOD * D_MODEL), D_MODEL ** -0.5),
        "b_ada": nrm(ks[4], (DEPTH, N_MOD * D_MODEL), 0.02),
        "g_mix": 1.0 + nrm(ks[5], (DEPTH, D_MODEL), 0.02),
        "w_in": nrm(ks[6], (DEPTH, D_MODEL, IN_COLS), D_MODEL ** -0.5),
        "conv_a": nrm(ks[7], (DEPTH, SC_KERNEL, SC_WIDTH), SC_KERNEL ** -0.5),
        "conv_b": nrm(ks[8], (DEPTH, CF_KERNEL, CF_WIDTH), CF_KERNEL ** -0.5),
        "conv_b_bias": nrm(ks[9], (DEPTH, CF_WIDTH), 0.02),
        "ln_cf_g": 1.0 + nrm(ks[10], (DEPTH, CF_WIDTH), 0.02),
        "ln_cf_b": nrm(ks[11], (DEPTH, CF_WIDTH), 0.02),
        "w_out_a": nrm(ks[12], (DEPTH, SC_WIDTH, D_MODEL), SC_WIDTH ** -0.5),
        "w_out_b": nrm(ks[13], (DEPTH, CF_WIDTH, D_MODEL), CF_WIDTH ** -0.5),
        "w_out_c": nrm(ks[14], (DEPTH, DW_OUT, D_MODEL), DW_OUT ** -0.5),
        "w_o": nrm(ks[15], (DEPTH, D_MODEL, D_MODEL), D_MODEL ** -0.5),
        "g_mlp": 1.0 + nrm(ks[16], (DEPTH, D_MODEL), 0.02),
        "w_mlp1": nrm(ks[17], (DEPTH, D_MODEL, D_FF), D_MODEL ** -0.5),
        "w_mlp2": nrm(ks[18], (DEPTH, D_FF, D_MODEL), D_FF ** -0.5),
        "g_final": 1.0 + nrm(ks[19], (D_MODEL,), 0.02),
    }


def reference(x, c, positions, w_ada, b_ada, g_mix, w_in, conv_a, conv_b, conv_b_bias, ln_cf_g,
              ln_cf_b, w_out_a, w_out_b, w_out_c, w_o, g_mlp, w_mlp1, w_mlp2, g_final):
    b, t, _ = x.shape
    c_act = jax.nn.silu(c)
    for l in range(DEPTH):
        mod = (c_act @ w_ada[l] + b_ada[l])[:, None, :]
        shift1, scale1, gate1, shift2, scale2, gate2 = jnp.split(mod, N_MOD, axis=-1)

        h = rmsnorm(x, g_mix[l]) * (1.0 + scale1) + shift1
        proj = h @ w_in[l]
        (sc_b, sc_c, sc_x, cf_a, cf_g, dw_q, dw_k, dw_v,
         gate_a, gate_b, gate_c) = jnp.split(proj, IN_SPLITS, axis=-1)

        y_a = (sc_b * causal_depthwise_conv(sc_c * sc_x, conv_a[l])) @ w_out_a[l]

        u = cf_a * jax.nn.sigmoid(cf_g)
        u = causal_depthwise_conv(u, conv_b[l]) + conv_b_bias[l]
        u = jax.nn.silu(layernorm(u, ln_cf_g[l], ln_cf_b[l]))
        y_b = u @ w_out_b[l]

        q = partial_rope(dw_q.reshape(b, t, N_DW_GROUPS, DW_HEADS, HEAD_DIM), positions)
        k = partial_rope(dw_k.reshape(b, t, N_DW_GROUPS, DW_HEADS, HEAD_DIM), positions)
        v = dw_v.reshape(b, t, N_DW_GROUPS, DW_HEADS, HEAD_DIM)
        outs, lses = [], []
        for g, (window, dilation) in enumerate(DW_PATTERNS):
            o_g, lse_g = dilated_window_attention(q[:, :, g], k[:, :, g], v[:, :, g], window, dilation)
            outs.append(o_g)
            lses.append(lse_g)
        wts = jax.nn.softmax(jnp.stack(lses, axis=0), axis=0)
        o = jnp.sum(wts[..., None].astype(v.dtype) * jnp.stack(outs, axis=0), axis=0)
        y_c = o.reshape(b, t, DW_OUT) @ w_out_c[l]

        merged = (jax.nn.sigmoid(gate_a) * y_a + jax.nn.sigmoid(gate_b) * y_b
                  + jax.nn.sigmoid(gate_c) * y_c)
        x = x + gate1 * (merged @ w_o[l])

        h2 = rmsnorm(x, g_mlp[l]) * (1.0 + scale2) + shift2
        x = x + gate2 * (jnp.square(jax.nn.relu(h2 @ w_mlp1[l])) @ w_mlp2[l])

    return rmsnorm(x, g_final)
```

```python
import numpy as np
import ml_dtypes
import concourse.bass as bass
import concourse.mybir as mybir
from concourse.bass_utils import run_bass_kernel_spmd
from contextlib import ExitStack

F32 = mybir.dt.float32
BF16 = mybir.dt.bfloat16
I32 = mybir.dt.int32
U8 = mybir.dt.uint8
ALU = mybir.AluOpType
AF = mybir.ActivationFunctionType
NPBF = ml_dtypes.bfloat16


class Tok:
    __slots__ = ("sem", "key", "val", "stream")

    def __init__(self, sem, key, val, stream):
        self.sem, self.key, self.val, self.stream = sem, key, val, stream


class Buf:
    __slots__ = ("w", "r", "name")

    def __init__(self, name=""):
        self.w = None
        self.r = {}
        self.name = name


class KB:
    ENG = ("pe", "act", "dve", "pool", "sp")

    def __init__(self, nc, es, arena_kb=200, ndma=(("sp", 14), ("pool", 14), ("act", 6))):
        self.nc = nc
        self.eng = {"pe": nc.tensor, "act": nc.scalar, "dve": nc.vector, "pool": nc.gpsimd, "sp": nc.sync}
        self.streams = {k: [] for k in self.ENG}
        self.sem = {}
        self.cnt = {k: 0 for k in self.ENG}
        self.waited = {k: {} for k in self.ENG}
        for k in self.ENG:
            self.sem[k] = es.enter_context(nc.semaphore("s_" + k))
        self.dpool = {}
        self.dcnt = {}
        self.dnext = {}
        for q, n in ndma:
            self.dpool[q] = []
            for i in range(n):
                s = es.enter_context(nc.semaphore(f"d_{q}{i}"))
                self.dpool[q].append((s, f"d_{q}{i}"))
                self.dcnt[f"d_{q}{i}"] = 0
            self.dnext[q] = 0
        self.arena = es.enter_context(nc.sbuf_tensor("arena", [128, arena_kb * 1024], U8))
        self.arena_bytes = arena_kb * 1024
        self.psum = [es.enter_context(nc.psum_tensor(f"ps{i}", [128, 512], F32)) for i in range(8)]
        self.psb = [Buf(f"psb{i}") for i in range(8)]
        self.final_toks = []
        self.n_instr = 0

    def view(self, off, nbytes, dt, pattern=None, **kw):
        assert off % 32 == 0, off
        assert off + nbytes <= self.arena_bytes, (off, nbytes)
        v = self.arena[:, off:off + nbytes].bitcast(dt)
        if pattern:
            v = v.rearrange(pattern, **kw)
        return v

    def _wait(self, stream, t):
        if self.waited[stream].get(t.key, 0) >= t.val:
            return
        self.waited[stream][t.key] = t.val
        self.streams[stream].append(("wait", t.sem, t.val))

    def _deps(self, stream, reads, writes, is_dma):
        for b in reads:
            if b.w is not None:
                self._wait(stream, b.w)
        for b in writes:
            if b.w is not None:
                if is_dma or b.w.stream != stream:
                    self._wait(stream, b.w)
            for t in b.r.values():
                if is_dma or t.stream != stream:
                    self._wait(stream, t)

    def _mark(self, tok, reads, writes):
        for b in writes:
            b.w = tok
            b.r = {}
        for b in reads:
            if b not in writes:
                b.r[tok.key] = tok

    def op(self, stream, fn, reads=(), writes=()):
        self._deps(stream, reads, writes, False)
        self.cnt[stream] += 1
        tok = Tok(self.sem[stream], "s_" + stream, self.cnt[stream], stream)
        self.streams[stream].append(("op", fn))
        self._mark(tok, reads, writes)
        return tok

    def dma(self, q, out, in_, reads=(), writes=(), final=False, **kw):
        self._deps(q, reads, writes, True)
        pool = self.dpool[q]
        sem, key = pool[self.dnext[q] % len(pool)]
        self.dnext[q] += 1
        prev = self.dcnt[key]
        if prev > 0:
            self._wait(q, Tok(sem, key, prev, None))
        self.dcnt[key] = prev + 16
        tok = Tok(sem, key, prev + 16, None)
        self.streams[q].append(("dma", out, in_, sem, kw))
        self._mark(tok, reads, writes)
        if final:
            self.final_toks.append(tok)
        return tok

    def barrier(self, bufs=()):
        toks = []
        for s in self.ENG:
            if self.cnt[s] > 0:
                toks.append(Tok(self.sem[s], "s_" + s, self.cnt[s], s))
        for q, pool in self.dpool.items():
            for sem, key in pool:
                if self.dcnt[key] > 0:
                    toks.append(Tok(sem, key, self.dcnt[key], None))
        for s in self.ENG:
            for t in toks:
                if t.stream != s:
                    self._wait(s, t)

    def finish(self):
        for t in self.final_toks:
            self._wait("sp", t)
        for s in self.ENG:
            if s != "sp" and self.cnt[s] > 0:
                self._wait("sp", Tok(self.sem[s], "s_" + s, self.cnt[s], s))

    def replay(self, block):
        nc = self.nc
        kb = self

        def run(stream, e):
            for item in kb.streams[stream]:
                kind = item[0]
                if kind == "wait":
                    e.wait_ge(item[1], item[2])
                elif kind == "op":
                    ins = item[1](e)
                    ins.then_inc(kb.sem[stream], 1)
                elif kind == "dma":
                    e.dma_start(out=item[1], in_=item[2], **item[4]).then_inc(item[3], 16)
                elif kind == "cc":
                    e.collective_compute("AllGather", ALU.bypass, replica_groups=[list(range(8))],
                                         ins=[item[1]], outs=[item[2]]).then_inc(item[3], 16)
                kb.n_instr += 1

        @block.tensor
        def _(e):
            run("pe", e)

        @block.scalar
        def _(e):
            run("act", e)

        @block.vector
        def _(e):
            run("dve", e)

        @block.gpsimd
        def _(e):
            run("pool", e)

        @block.sync
        def _(e):
            run("sp", e)


D = 4096
TC = 1024
NCORE = 8
ADA_COLS = 6 * D // NCORE


def build_ada():
    nc = bass.Bass("TRN2", target_bir_lowering=False)
    cT = nc.dram_tensor("cT", [128, 32], F32, kind="ExternalInput").ap()
    wada = nc.dram_tensor("wada", [2, D, ADA_COLS], F32, kind="ExternalInput").ap()
    bada = nc.dram_tensor("bada", [1, 2 * ADA_COLS], F32, kind="ExternalInput").ap()
    modp = nc.dram_tensor("modp", [1, 2 * ADA_COLS], F32, kind="ExternalOutput").ap()
    with ExitStack() as es:
        kb = KB(nc, es, arena_kb=120)
        off = 0
        csb = kb.view(off, 128, F32); off += 128
        csil = kb.view(off, 128, F32); off += 128
        bsb = kb.view(off, 2 * ADA_COLS * 4, F32); off += 2 * ADA_COLS * 4
        osb = kb.view(off, 2 * ADA_COLS * 4, F32); off += 2 * ADA_COLS * 4
        NW = 4
        wt = []
        for i in range(NW):
            wt.append(kb.view(off, 8 * 512 * 4, F32, "p (k n) -> p k n", k=8)); off += 8 * 512 * 4
        b_c, b_cs, b_b, b_o = Buf(), Buf(), Buf(), Buf()
        b_w = [Buf() for _ in range(NW)]
        kb.dma("sp", csb, cT, writes=[b_c])
        kb.dma("sp", bsb[0:1, :], bada, writes=[b_b])
        kb.op("act", lambda e: e.activation(out=csil, in_=csb, func=AF.Silu), reads=[b_c], writes=[b_cs])
        wi = 0
        for l in range(2):
            for cb in range(ADA_COLS // 512):
                pb = (l * 6 + cb) % 8
                for kg in range(4):
                    slot = wi % NW
                    wi += 1
                    src = wada[l, kg * 1024:(kg + 1) * 1024, cb * 512:(cb + 1) * 512].rearrange("(k p) n -> p k n", p=128)
                    kb.dma("sp" if wi % 2 else "act", wt[slot], src, writes=[b_w[slot]])

                    def mm(e, slot=slot, kg=kg, pb=pb):
                        ins = None
                        for k in range(8):
                            kc = kg * 8 + k
                            ins = e.matmul(kb.psum[pb][0:1, :], csil[:, kc:kc + 1], wt[slot][:, k, :],
                                           start=(kc == 0), stop=(kc == 31))
                        return ins
                    kb.op("pe", mm, reads=[b_cs, b_w[slot]], writes=[kb.psb[pb]])
                o0 = l * ADA_COLS + cb * 512
                kb.op("dve", lambda e, pb=pb, o0=o0: e.tensor_tensor(out=osb[0:1, o0:o0 + 512], in0=kb.psum[pb][0:1, :],
                                                                    in1=bsb[0:1, o0:o0 + 512], op=ALU.add),
                      reads=[b_b], writes=[kb.psb[pb], b_o])
        kb.dma("sp", modp, osb[0:1, :], reads=[b_o], final=True)
        kb.finish()
        with nc.Block() as block:
            kb.replay(block)
    return nc


def run_ada(c, w_ada, b_ada):
    nc = build_ada()
    cT = np.ascontiguousarray(c.reshape(32, 128).T)
    in_maps = []
    for i in range(NCORE):
        sl = slice(i * ADA_COLS, (i + 1) * ADA_COLS)
        in_maps.append({
            "cT": cT,
            "wada": np.ascontiguousarray(w_ada[:, :, sl]),
            "bada": np.ascontiguousarray(b_ada[:, sl]).reshape(1, 2 * ADA_COLS),
        })
    res = run_bass_kernel_spmd(nc, in_maps, core_ids=list(range(NCORE)))
    mod = np.zeros((2, 6 * D), np.float32)
    for i in range(NCORE):
        r = res.results[i]["modp"].reshape(2, ADA_COLS)
        mod[:, i * ADA_COLS:(i + 1) * ADA_COLS] = r
    return mod


EPS = 1e-6
PI = float(np.pi)
SC_W, CF_W, DW_W = 2048, 2048, 3072
OFF_SCB, OFF_SCC, OFF_SCX = 0, 2048, 4096
OFF_CFA, OFF_CFG = 6144, 8192
OFF_Q, OFF_K, OFF_V = 10240, 13312, 16384
OFF_GATE = 19456
IN_COLS = 31744


def full_slabs_A():
    sl = []
    for j in range(SC_W // 256):
        sl.append(("sc", [(OFF_SCC + j * 256, 256), (OFF_SCX + j * 256, 256)], j * 256))
    for j in range(SC_W // 512):
        sl.append(("scb", [(OFF_SCB + j * 512, 512)], j * 512))
    for j in range(CF_W // 256):
        sl.append(("cf", [(OFF_CFG + j * 256, 256), (OFF_CFA + j * 256, 256)], j * 256))
    for j in range(2 * DW_W // 512):
        sl.append(("qk", [(OFF_Q + j * 512, 512)], j * 512))
    for j in range(DW_W // 512):
        sl.append(("v", [(OFF_V + j * 512, 512)], j * 512))
    for j in range(3 * D // 512):
        sl.append(("gate", [(OFF_GATE + j * 512, 512)], j * 512))
    return sl


def emit_norm_T(kb, xsrc, hT, b_hT, modT, jshift, jscale, gT, xt, xn, ss, sm, ident, b_const, tagbufs):
    a_t, sh_t = sm["a"], sm["sh"]
    b_a = Buf()
    kb.op("dve", lambda e: e.scalar_tensor_tensor(out=a_t, in0=modT[:, jscale * 32:(jscale + 1) * 32], scalar=1.0,
                                                  in1=gT, op0=ALU.add, op1=ALU.mult), reads=[b_const], writes=[b_a])
    b_xt, b_xn, b_ss = tagbufs
    for tt in range(TC // 128):
        s = tt % 2
        kb.dma("sp", xt[s], xsrc[tt * 128:(tt + 1) * 128, :], writes=[b_xt[s]])
        kb.op("act", lambda e, s=s, tt=tt: e.activation(out=xn[s], in_=xt[s], func=AF.Square, accum_out=ss[:, tt:tt + 1]),
              reads=[b_xt[s]], writes=[b_xn[s], b_ss[tt]])
        kb.op("dve", lambda e, tt=tt: e.tensor_scalar(out=ss[:, 8 + tt:9 + tt], in0=ss[:, tt:tt + 1], scalar1=1.0 / D, scalar2=EPS,
                                                     op0=ALU.mult, op1=ALU.add), reads=[b_ss[tt]], writes=[b_ss[tt]])
        kb.op("act", lambda e, tt=tt: e.activation(out=ss[:, 16 + tt:17 + tt], in_=ss[:, 8 + tt:9 + tt], func=AF.Sqrt),
              reads=[b_ss[tt]], writes=[b_ss[tt]])
        kb.op("dve", lambda e, tt=tt: e.reciprocal(out=ss[:, 24 + tt:25 + tt], in_=ss[:, 16 + tt:17 + tt]),
              reads=[b_ss[tt]], writes=[b_ss[tt]])
        kb.op("act", lambda e, s=s, tt=tt: e.activation(out=xn[s], in_=xt[s], func=AF.Copy, scale=ss[:, 24 + tt:25 + tt]),
              reads=[b_xt[s], b_ss[tt]], writes=[b_xn[s]])
        for g in range(4):
            pb = (tt * 4 + g) % 8
            pv = kb.psum[pb][:, :].bitcast(BF16)

            def tr(e, s=s, g=g, pv=pv):
                ins = None
                for j in range(8):
                    c = g * 8 + j
                    ins = e.transpose(pv[:, j * 128:(j + 1) * 128], xn[s][:, c * 128:(c + 1) * 128], ident)
                return ins
            kb.op("pe", tr, reads=[b_xn[s], b_const], writes=[kb.psb[pb]])
            for j in range(8):
                c = g * 8 + j
                dst = hT[:, c, tt * 128:(tt + 1) * 128]
                if g % 2 == 0:
                    kb.op("act", lambda e, pv=pv, j=j, c=c, dst=dst: e.activation(
                        out=dst, in_=pv[:, j * 128:(j + 1) * 128], func=AF.Identity,
                        bias=modT[:, jshift * 32 + c:jshift * 32 + c + 1], scale=a_t[:, c:c + 1]),
                        reads=[b_a, b_const], writes=[kb.psb[pb], b_hT[tt][0]])
                else:
                    kb.op("dve", lambda e, pv=pv, j=j, c=c, dst=dst: e.tensor_scalar(
                        out=dst, in0=pv[:, j * 128:(j + 1) * 128], scalar1=a_t[:, c:c + 1],
                        scalar2=modT[:, jshift * 32 + c:jshift * 32 + c + 1], op0=ALU.mult, op1=ALU.add),
                        reads=[b_a, b_const], writes=[kb.psb[pb], b_hT[tt][1]])


def load_wslab(kb, q, wsl_slot, b_slot, wsrc, segs, nk):
    co = 0
    for (c0, n) in segs:
        for kg in range(0, nk, 8):
            k1 = min(nk, kg + 8)
            kb.dma(q, wsl_slot[:, kg:k1, co:co + n],
                   wsrc[kg * 128:k1 * 128, c0:c0 + n].rearrange("(k p) n -> p k n", p=128), writes=[b_slot])
        co += n


def build_A(slabs, ncols):
    nc = bass.Bass("TRN2", target_bir_lowering=False)
    dt_in = lambda name, shape, dt: nc.dram_tensor(name, shape, dt, kind="ExternalInput").ap()
    dt_out = lambda name, shape, dt: nc.dram_tensor(name, shape, dt, kind="ExternalOutput").ap()
    x = dt_in("x", [TC, D], F32)
    modT_d = dt_in("modT", [128, 192], F32)
    gT_d = dt_in("gT", [128, 32], F32)
    w_in = dt_in("w_in", [D, ncols], F32)
    pos = dt_in("pos", [1, TC], I32)
    invf_d = dt_in("invf", [32, 1], F32)
    jm_d = dt_in("jm", [32, 32], BF16)
    ident_d = dt_in("ident", [128, 128], BF16)
    zT = dt_out("zT", [SC_W, TC], F32)
    scbT = dt_out("scbT", [SC_W, TC], F32)
    uT = dt_out("uT", [CF_W, TC], F32)
    qkT = dt_out("qkT", [2 * DW_W, TC], BF16)
    vtm = dt_out("vtm", [TC, DW_W], BF16)
    sgT = dt_out("sgT", [3 * D, TC], BF16)
    with ExitStack() as es:
        kb = KB(nc, es, arena_kb=196)
        off = 0
        hT = kb.view(off, 65536, BF16, "p (c n) -> p c n", c=32); off += 65536
        wsl = []
        for i in range(2):
            wsl.append(kb.view(off, 32768, BF16, "p (k n) -> p k n", k=32)); off += 32768
        R = off
        xt = [kb.view(R + i * 16384, 16384, F32) for i in range(2)]
        xn = [kb.view(R + 32768 + i * 8192, 8192, BF16) for i in range(2)]
        o2 = R
        st = []
        for i in range(4):
            st.append(kb.view(o2, 4096, F32)); o2 += 4096
        hold = []
        for i in range(2):
            hold.append(kb.view(o2, 4096, F32)); o2 += 4096
        ob = []
        for i in range(3):
            ob.append(kb.view(o2, 2048, BF16)); o2 += 2048
        q32b = kb.view(o2, 2048, BF16); o2 += 2048
        t1 = kb.view(o2, 4096, F32); o2 += 4096
        t2 = kb.view(o2, 4096, F32); o2 += 4096
        cosT = kb.view(o2, 4096, F32); o2 += 4096
        sinT = kb.view(o2, 4096, F32); o2 += 4096
        assert o2 <= R + 49152
        off = R + 49152
        modT = kb.view(off, 768, F32); off += 768
        gT = kb.view(off, 128, F32); off += 128
        a_t = kb.view(off, 128, F32); off += 128
        ss = kb.view(off, 128, F32); off += 128
        ident = kb.view(off, 256, BF16); off += 256
        jm = kb.view(off, 64, BF16); off += 64
        invf = kb.view(off, 32, F32); off += 32
        posi = kb.view(off, 4096, I32); off += 4096
        wk = [kb.view(off + i * 4096, 4096, F32) for i in range(3)]; off += 3 * 4096
        b_const = Buf()
        kb.dma("sp", modT, modT_d, writes=[b_const])
        kb.dma("sp", gT, gT_d, writes=[b_const])
        kb.dma("sp", ident, ident_d, writes=[b_const])
        kb.dma("sp", jm[0:32, :], jm_d, writes=[b_const])
        kb.dma("sp", invf[0:32, 0:1], invf_d, writes=[b_const])
        b_pos = Buf()
        kb.dma("sp", posi[0:32, :], pos.partition_broadcast(32), writes=[b_pos])
        b_hT = [[Buf(), Buf()] for _ in range(8)]
        hT_all = [b for p in b_hT for b in p]
        tagbufs = ([Buf(), Buf()], [Buf(), Buf()], [Buf() for _ in range(8)])
        b_w = [Buf(), Buf()]
        nslab = len(slabs)
        for s in range(min(2, nslab)):
            load_wslab(kb, "pool", wsl[s % 2], b_w[s % 2], w_in, slabs[s][1], 32)
        import os
        if not os.environ.get("A_SKIP_N1"):
            emit_norm_T(kb, x, hT, b_hT, modT, 0, 1, gT, xt, xn, ss, {"a": a_t, "sh": None}, ident, b_const, tagbufs)
        kb.barrier()
        b_tab = Buf()
        P32 = slice(0, 32)
        ROPE_ON = not os.environ.get("A_SKIP_ROPE")
        posf, ang, rr = wk[0], wk[1], wk[2]
        C1 = 6.28125
        C2 = float(2 * np.pi - 6.28125)
        _op = kb.op
        if not ROPE_ON:
            kb.op = lambda *a, **k: None
        kb.op("dve", lambda e: e.tensor_copy(out=posf[P32, :], in_=posi[P32, :]), reads=[b_pos], writes=[b_tab])
        kb.op("dve", lambda e: e.tensor_scalar(out=ang[P32, :], in0=posf[P32, :], scalar1=invf[P32, 0:1], scalar2=None,
                                               op0=ALU.mult), reads=[b_tab, b_const], writes=[b_tab])
        kb.op("dve", lambda e: e.tensor_scalar(out=posf[P32, :], in0=ang[P32, :], scalar1=float(1 / (2 * np.pi)), scalar2=None,
                                               op0=ALU.mult), reads=[b_tab], writes=[b_tab])
        kb.op("dve", lambda e: e.tensor_copy(out=posi[P32, :], in_=posf[P32, :]), reads=[b_tab], writes=[b_tab])
        kb.op("dve", lambda e: e.tensor_copy(out=posf[P32, :], in_=posi[P32, :]), reads=[b_tab], writes=[b_tab])
        kb.op("dve", lambda e: e.scalar_tensor_tensor(out=rr[P32, :], in0=posf[P32, :], scalar=-C1, in1=ang[P32, :],
                                                      op0=ALU.mult, op1=ALU.add), reads=[b_tab], writes=[b_tab])
        kb.op("dve", lambda e: e.scalar_tensor_tensor(out=rr[P32, :], in0=posf[P32, :], scalar=-C2, in1=rr[P32, :],
                                                      op0=ALU.mult, op1=ALU.add), reads=[b_tab], writes=[b_tab])

        def wrap(src, dst):
            kb.op("dve", lambda e: e.tensor_scalar(out=posf[P32, :], in0=src[P32, :], scalar1=PI, scalar2=-2 * PI,
                                                   op0=ALU.is_gt, op1=ALU.mult), reads=[b_tab], writes=[b_tab])
            kb.op("dve", lambda e: e.tensor_tensor(out=dst[P32, :], in0=src[P32, :], in1=posf[P32, :], op=ALU.add),
                  reads=[b_tab], writes=[b_tab])
            kb.op("dve", lambda e: e.tensor_scalar(out=posf[P32, :], in0=dst[P32, :], scalar1=-PI, scalar2=2 * PI,
                                                   op0=ALU.is_lt, op1=ALU.mult), reads=[b_tab], writes=[b_tab])
            kb.op("dve", lambda e: e.tensor_tensor(out=dst[P32, :], in0=dst[P32, :], in1=posf[P32, :], op=ALU.add),
                  reads=[b_tab], writes=[b_tab])
            kb.op("dve", lambda e: e.tensor_scalar(out=dst[P32, :], in0=dst[P32, :], scalar1=-PI, scalar2=PI,
                                                   op0=ALU.max, op1=ALU.min), reads=[b_tab], writes=[b_tab])
        wrap(rr, rr)
        kb.op("act", lambda e: e.activation(out=sinT[P32, :], in_=rr[P32, :], func=AF.Sin), reads=[b_tab], writes=[b_tab])
        kb.op("dve", lambda e: e.tensor_scalar(out=ang[P32, :], in0=rr[P32, :], scalar1=PI / 2, scalar2=None, op0=ALU.add),
              reads=[b_tab], writes=[b_tab])
        wrap(ang, ang)
        kb.op("act", lambda e: e.activation(out=cosT[P32, :], in_=ang[P32, :], func=AF.Sin), reads=[b_tab], writes=[b_tab])
        kb.op = _op

        b_st = [Buf() for _ in range(4)]
        b_hold = [Buf() for _ in range(2)]
        b_ob = [Buf() for _ in range(3)]
        b_q32, b_t1, b_t2 = Buf(), Buf(), Buf()
        cnt = {"blk": 0, "st": 0, "ob": 0, "v": 0}

        def gemm_block(slot, bi):
            pbk = (cnt["blk"] % 3) * 2
            cnt["blk"] += 1

            def mm(e):
                ins = None
                for kc in range(32):
                    for h in range(2):
                        ins = e.matmul(kb.psum[pbk + h][:, :], wsl[slot][:, kc, bi * 128:(bi + 1) * 128],
                                       hT[:, kc, h * 512:(h + 1) * 512], start=(kc == 0), stop=(kc == 31))
                return ins
            kb.op("pe", mm, reads=[b_w[slot]] + hT_all, writes=[kb.psb[pbk], kb.psb[pbk + 1]])
            return pbk

        def act_evac(pbk, dst, func, bdst):
            for h in range(2):
                kb.op("act", lambda e, h=h: e.activation(out=dst[:, h * 512:(h + 1) * 512], in_=kb.psum[pbk + h][:, :], func=func),
                      reads=[], writes=[kb.psb[pbk + h], bdst])

        def mul_evac(pbk, dst, other, bdst, bother):
            for h in range(2):
                kb.op("dve", lambda e, h=h: e.tensor_tensor(out=dst[:, h * 512:(h + 1) * 512], in0=kb.psum[pbk + h][:, :],
                                                            in1=other[:, h * 512:(h + 1) * 512], op=ALU.mult),
                      reads=[bother], writes=[kb.psb[pbk + h], bdst])

        def next_st():
            i = cnt["st"] % 4
            cnt["st"] += 1
            return i

        for s in range(nslab):
            kind, segs, orow = slabs[s]
            slot = s % 2
            if kind in ("sc", "cf"):
                for j in range(2):
                    p1 = gemm_block(slot, j)
                    act_evac(p1, hold[j], AF.Copy if kind == "sc" else AF.Sigmoid, b_hold[j])
                for j in range(2):
                    p2 = gemm_block(slot, 2 + j)
                    i = next_st()
                    mul_evac(p2, st[i], hold[j], b_st[i], b_hold[j])
                    dst = (zT if kind == "sc" else uT)[orow + j * 128:orow + (j + 1) * 128, :]
                    kb.dma("sp", dst, st[i], reads=[b_st[i]], final=True)
            elif kind == "scb":
                for j in range(4):
                    p1 = gemm_block(slot, j)
                    i = next_st()
                    act_evac(p1, st[i], AF.Copy, b_st[i])
                    kb.dma("sp", scbT[orow + j * 128:orow + (j + 1) * 128, :], st[i], reads=[b_st[i]], final=True)
            elif kind == "gate":
                for j in range(4):
                    p1 = gemm_block(slot, j)
                    i = cnt["ob"] % 3
                    cnt["ob"] += 1
                    act_evac(p1, ob[i], AF.Sigmoid, b_ob[i])
                    kb.dma("sp", sgT[orow + j * 128:orow + (j + 1) * 128, :], ob[i], reads=[b_ob[i]], final=True)
            elif kind == "qk":
                for j in range(4):
                    p1 = gemm_block(slot, j)
                    i = next_st()
                    qraw = st[i]
                    act_evac(p1, qraw, AF.Copy, b_st[i])
                    kb.op("dve", lambda e, qraw=qraw: e.tensor_copy(out=q32b[P32, :], in_=qraw[P32, :]), reads=[b_st[i]], writes=[b_q32])

                    def jmm(e):
                        ins = None
                        for h in range(2):
                            ins = e.matmul(kb.psum[6 + h][0:32, :], jm[P32, :], q32b[P32, h * 512:(h + 1) * 512], start=True, stop=True)
                        return ins
                    kb.op("pe", jmm, reads=[b_q32, b_const], writes=[kb.psb[6], kb.psb[7]])
                    kb.op("dve", lambda e, qraw=qraw: e.tensor_tensor(out=t1[P32, :], in0=qraw[P32, :], in1=cosT[P32, :], op=ALU.mult),
                          reads=[b_st[i], b_tab], writes=[b_t1])
                    for h in range(2):
                        kb.op("dve", lambda e, h=h: e.tensor_tensor(out=t2[P32, h * 512:(h + 1) * 512], in0=kb.psum[6 + h][0:32, :],
                                                                    in1=sinT[P32, h * 512:(h + 1) * 512], op=ALU.mult),
                              reads=[b_tab], writes=[kb.psb[6 + h], b_t2])
                    io = cnt["ob"] % 3
                    cnt["ob"] += 1
                    kb.op("pool", lambda e, io=io, qraw=qraw: e.tensor_copy(out=ob[io][:, :], in_=qraw[:, :]),
                          reads=[b_st[i]], writes=[b_ob[io]])
                    kb.op("pool", lambda e, io=io: e.tensor_tensor(out=ob[io][P32, :], in0=t1[P32, :], in1=t2[P32, :], op=ALU.add),
                          reads=[b_t1, b_t2], writes=[b_ob[io]])
                    kb.dma("sp", qkT[orow + j * 128:orow + (j + 1) * 128, :], ob[io], reads=[b_ob[io]], final=True)
            elif kind == "v":
                for tt in range(8):
                    pbk = (cnt["blk"] % 3) * 2 + (cnt["v"] % 2)
                    cnt["v"] += 1
                    if cnt["v"] % 2 == 0:
                        cnt["blk"] += 1

                    def mmv(e, tt=tt, pbk=pbk, slot=slot):
                        ins = None
                        for kc in range(32):
                            ins = e.matmul(kb.psum[pbk][:, :], hT[:, kc, tt * 128:(tt + 1) * 128], wsl[slot][:, kc, :],
                                           start=(kc == 0), stop=(kc == 31))
                        return ins
                    kb.op("pe", mmv, reads=[b_w[slot]] + hT_all, writes=[kb.psb[pbk]])
                    io = cnt["ob"] % 3
                    cnt["ob"] += 1
                    vdst = ob[io][:, 0:512]
                    if tt % 2 == 0:
                        kb.op("act", lambda e, pbk=pbk, vdst=vdst: e.activation(out=vdst, in_=kb.psum[pbk][:, :], func=AF.Copy),
                              reads=[], writes=[kb.psb[pbk], b_ob[io]])
                    else:
                        kb.op("dve", lambda e, pbk=pbk, vdst=vdst: e.tensor_copy(out=vdst, in_=kb.psum[pbk][:, :]),
                              reads=[], writes=[kb.psb[pbk], b_ob[io]])
                    kb.dma("sp", vtm[tt * 128:(tt + 1) * 128, orow:orow + 512], vdst, reads=[b_ob[io]], final=True)
            if s + 2 < nslab:
                load_wslab(kb, "pool", wsl[slot], b_w[slot], w_in, slabs[s + 2][1], 32)
        kb.finish()
        with nc.Block() as block:
            kb.replay(block)
    return nc


def rope_consts():
    invf = (np.float32(500000.0) ** (-(np.arange(0, 32, 2, dtype=np.float32)) / np.float32(32))).astype(np.float32)
    invf32 = np.concatenate([invf, invf]).reshape(32, 1).astype(np.float32)
    jm = np.zeros((32, 32), np.float32)
    for i in range(16):
        jm[16 + i, i] = -1.0
        jm[i, 16 + i] = 1.0
    return invf32, jm.astype(NPBF)


def t_layout(v, nchunk):
    return np.ascontiguousarray(v.reshape(nchunk, 128).T)


GROUPS = [
    (1, 1152, 9, 128, 8),
    (4, 1536, 3, 128, 2),
    (16, 3072, 2, 64, 1),
]
VM_OFF = [0, 9, 9 + 12]
VM_COLS = 9 + 12 + 32
DFF = 4 * D


def build_B():
    nc = bass.Bass("TRN2", target_bir_lowering=False)
    dt_in = lambda name, shape, dt: nc.dram_tensor(name, shape, dt, kind="ExternalInput").ap()
    dt_out = lambda name, shape, dt: nc.dram_tensor(name, shape, dt, kind="ExternalOutput").ap()
    x = dt_in("x", [TC, D], F32)
    modT_d = dt_in("modT", [128, 192], F32)
    modv = dt_in("modv", [6, D], F32)
    gT_d = dt_in("gT", [128, 32], F32)
    gfin = dt_in("gfin", [1, D], F32)
    zTe = dt_in("zTe", [SC_W, TC + 2], F32)
    scbT = dt_in("scbT", [SC_W, TC], F32)
    uTe = dt_in("uTe", [CF_W, TC + 30], F32)
    qT = dt_in("qT", [DW_W, TC], BF16)
    kTe = [dt_in(f"kTe{g}", [1024, GROUPS[g][1]], BF16) for g in range(3)]
    ve = [dt_in(f"ve{g}", [GROUPS[g][1], 1024], BF16) for g in range(3)]
    vm_d = dt_in("vm", [128, VM_COLS], F32)
    sgT = dt_in("sgT", [3 * D, TC], BF16)
    cva_d = dt_in("cva", [128, 48], F32)
    cvb_d = dt_in("cvb", [128, 16 * 31], F32)
    cvec_d = dt_in("cvec", [128, 48], F32)
    woa = dt_in("woa", [SC_W, D], F32)
    wob = dt_in("wob", [CF_W, D], F32)
    woc = dt_in("woc", [1024, D], F32)
    wo = dt_in("wo", [D, D], F32)
    w1 = dt_in("w1", [D, DFF], F32)
    w2 = dt_in("w2", [DFF, D], F32)
    mk_d = dt_in("masks", [128, 256], BF16)
    ones_d = dt_in("onesb", [128, 128], BF16)
    onesf_d = dt_in("onesf", [128, 128], F32)
    ident_d = dt_in("ident", [128, 128], BF16)
    xout = dt_out("xout", [TC, D], F32)
    yout = dt_out("yout", [TC, D], F32)
    mergedT = nc.dram_tensor("mergedT", [D, TC], BF16).ap()
    x1 = nc.dram_tensor("x1s", [TC, D], F32).ap()
    aT = nc.dram_tensor("aTs", [DFF, TC], BF16).ap()
    with ExitStack() as es:
        kb = KB(nc, es, arena_kb=200)
        V = kb.view
        KBY = 1024
        off = 180 * KBY
        modT = V(off, 768, F32); off += 768
        gT = V(off, 128, F32); off += 128
        a_t = V(off, 128, F32); off += 128
        ss = V(off, 128, F32); off += 128
        ident = V(off, 256, BF16); off += 256
        onesb = V(off, 256, BF16); off += 256
        onesf = V(off, 512, F32); off += 512
        masks = V(off, 512, BF16); off += 512
        vm = V(off, 224, F32); off += 224
        cva = V(off, 192, F32); off += 192
        cvb = V(off, 16 * 31 * 4, F32); off += 16 * 31 * 4
        cvec = V(off, 192, F32); off += 192
        assert off <= 200 * KBY
        b_const = Buf()
        for dst, src in ((modT, modT_d), (gT, gT_d), (ident, ident_d), (onesb, ones_d), (onesf, onesf_d), (masks, mk_d),
                         (vm[:, 0:VM_COLS], vm_d), (cva, cva_d), (cvb, cvb_d), (cvec, cvec_d)):
            kb.dma("sp", dst, src, writes=[b_const])
        kb.barrier()

        AinT = V(0, 32 * KBY, BF16, "p (c n) -> p c n", c=16)
        ufT = V(32 * KBY, 32 * KBY, BF16, "p (c n) -> p c n", c=16)
        vb = V(64 * KBY, 64 * KBY, F32, "p (c n) -> p c n", c=16)
        o = 128 * KBY
        ze = [V(o + i * 4224, 4224, F32) for i in range(2)]; o += 2 * 4224
        sb = [V(o + i * 4096, 4096, F32) for i in range(2)]; o += 2 * 4096
        tt_ = [V(o + i * 4096, 4096, F32) for i in range(2)]; o += 2 * 4096
        ue = [V(o + i * 4224, 4224, F32) for i in range(2)]; o += 2 * 4224
        sq = [V(o + i * 4096, 4096, F32) for i in range(2)]; o += 2 * 4096
        mean_bc = V(o, 4096, F32); o += 4096
        rstd_bc = V(o, 4096, F32); o += 4096
        assert o <= 180 * KBY
        b_ze, b_sb, b_t, b_ue, b_sq = ([Buf(), Buf()] for _ in range(5))
        b_Ain, b_uf, b_vb = Buf(), Buf(), [Buf() for _ in range(16)]
        for ch in range(16):
            s = ch % 2
            kb.dma("sp", ze[s][:, 0:TC + 2], zTe[ch * 128:(ch + 1) * 128, :], writes=[b_ze[s]])
            kb.dma("sp", sb[s], scbT[ch * 128:(ch + 1) * 128, :], writes=[b_sb[s]])
            kb.op("dve", lambda e, s=s, ch=ch: e.tensor_scalar(out=tt_[s], in0=ze[s][:, 2:TC + 2], scalar1=cva[:, ch * 3 + 2:ch * 3 + 3],
                                                              scalar2=None, op0=ALU.mult), reads=[b_ze[s], b_const], writes=[b_t[s]])
            for k in (1, 0):
                kb.op("dve", lambda e, s=s, ch=ch, k=k: e.scalar_tensor_tensor(
                    out=tt_[s], in0=ze[s][:, k:k + TC], scalar=cva[:, ch * 3 + k:ch * 3 + k + 1], in1=tt_[s],
                    op0=ALU.mult, op1=ALU.add), reads=[b_ze[s], b_t[s], b_const], writes=[b_t[s]])
            kb.op("pool", lambda e, s=s, ch=ch: e.tensor_tensor(out=AinT[:, ch, :], in0=tt_[s], in1=sb[s], op=ALU.mult),
                  reads=[b_t[s], b_sb[s]], writes=[b_Ain])
        for ch in range(16):
            s = ch % 2
            kb.dma("sp", ue[s][:, 0:TC + 30], uTe[ch * 128:(ch + 1) * 128, :], writes=[b_ue[s]])
            kb.op("dve", lambda e, s=s, ch=ch: e.tensor_scalar(out=vb[:, ch, :], in0=ue[s][:, 30:TC + 30],
                                                              scalar1=cvb[:, ch * 31 + 30:ch * 31 + 31], scalar2=cvec[:, ch:ch + 1],
                                                              op0=ALU.mult, op1=ALU.add), reads=[b_ue[s], b_const], writes=[b_vb[ch]])
            for k in range(30):
                kb.op("dve", lambda e, s=s, ch=ch, k=k: e.scalar_tensor_tensor(
                    out=vb[:, ch, :], in0=ue[s][:, k:k + TC], scalar=cvb[:, ch * 31 + k:ch * 31 + k + 1], in1=vb[:, ch, :],
                    op0=ALU.mult, op1=ALU.add), reads=[b_ue[s], b_vb[ch], b_const], writes=[b_vb[ch]])
            kb.op("act", lambda e, s=s, ch=ch: e.activation(out=sq[s], in_=vb[:, ch, :], func=AF.Square),
                  reads=[b_vb[ch]], writes=[b_sq[s]])

            def stat(e, s=s, ch=ch):
                ins = None
                for h in range(2):
                    ins = e.matmul(kb.psum[h][:, :], onesf, vb[:, ch, h * 512:(h + 1) * 512], start=(ch == 0), stop=(ch == 15))
                    ins = e.matmul(kb.psum[2 + h][:, :], onesf, sq[s][:, h * 512:(h + 1) * 512], start=(ch == 0), stop=(ch == 15))
                return ins
            kb.op("pe", stat, reads=[b_vb[ch], b_sq[s], b_const], writes=[kb.psb[0], kb.psb[1], kb.psb[2], kb.psb[3]])
        b_stat = Buf()
        for h in range(2):
            hs = slice(h * 512, (h + 1) * 512)
            kb.op("dve", lambda e, h=h, hs=hs: e.tensor_scalar(out=mean_bc[:, hs], in0=kb.psum[h][:, :], scalar1=1.0 / CF_W, scalar2=None,
                                                              op0=ALU.mult), writes=[kb.psb[h], b_stat])
            kb.op("dve", lambda e, hs=hs: e.tensor_tensor(out=rstd_bc[:, hs], in0=mean_bc[:, hs], in1=mean_bc[:, hs], op=ALU.mult),
                  reads=[b_stat], writes=[b_stat])
            kb.op("dve", lambda e, h=h, hs=hs: e.scalar_tensor_tensor(out=rstd_bc[:, hs], in0=kb.psum[2 + h][:, :], scalar=1.0 / CF_W,
                                                                     in1=rstd_bc[:, hs], op0=ALU.mult, op1=ALU.subtract),
                  reads=[b_stat], writes=[kb.psb[2 + h], b_stat])
        kb.op("dve", lambda e: e.tensor_scalar(out=rstd_bc, in0=rstd_bc, scalar1=EPS, scalar2=None, op0=ALU.add),
              reads=[b_stat], writes=[b_stat])
        kb.op("act", lambda e: e.activation(out=rstd_bc, in_=rstd_bc, func=AF.Sqrt), reads=[b_stat], writes=[b_stat])
        kb.op("dve", lambda e: e.reciprocal(out=rstd_bc, in_=rstd_bc), reads=[b_stat], writes=[b_stat])
        for ch in range(16):
            s = ch % 2
            kb.op("dve", lambda e, s=s, ch=ch: e.tensor_tensor(out=tt_[s], in0=vb[:, ch, :], in1=mean_bc, op=ALU.subtract),
                  reads=[b_vb[ch], b_stat], writes=[b_t[s]])
            kb.op("pool", lambda e, s=s: e.tensor_tensor(out=tt_[s], in0=tt_[s], in1=rstd_bc, op=ALU.mult),
                  reads=[b_t[s], b_stat], writes=[b_t[s]])
            kb.op("act", lambda e, s=s, ch=ch: e.activation(out=ufT[:, ch, :], in_=tt_[s], func=AF.Silu,
                                                           bias=cvec[:, 32 + ch:33 + ch], scale=cvec[:, 16 + ch:17 + ch]),
                  reads=[b_t[s], b_const], writes=[b_uf])
        kb.barrier()

        oT = V(64 * KBY, 16 * KBY, BF16, "p (c n) -> p c n", c=8)
        o = 80 * KBY
        accn = V(o, 4096, F32); o += 4096
        accd = V(o, 4096, F32); o += 4096
        KTt = [V(o + i * 6144, 6144, BF16) for i in range(2)]; o += 2 * 6144
        QTt = [V(o + i * 2048, 2048, BF16) for i in range(2)]; o += 2 * 2048
        Vtt = [V(o + i * 8192, 8192, BF16) for i in range(2)]; o += 2 * 8192
        Pe = [V(o + i * 512, 512, BF16, "p (a n) -> p a n", a=2) for i in range(2)]; o += 1024
        Pm = [V(o + i * 512, 512, BF16, "p (a n) -> p a n", a=2) for i in range(2)]; o += 1024
        assert o <= 128 * KBY
        b_acc, b_oT = Buf(), Buf()
        b_KT, b_QT, b_Vt, b_Pe, b_Pm = ([Buf(), Buf()] for _ in range(5))
        SCALE = float(128 ** -0.5)
        it = 0
        hg = 0
        for h in range(8):
            hc = slice(h * 128, (h + 1) * 128)
            for g in range(3):
                d, Lg, nkb, QB, nqb = GROUPS[g]
                s = hg % 2
                hg += 1
                KT, QTs = KTt[s], QTt[s]
                Vt = Vtt[s][:, 0:d * nkb * 128].rearrange("p (r k c) -> p r k c", r=d, k=nkb)
                kb.dma("sp", KT[:, 0:Lg], kTe[g][hc, :], writes=[b_KT[s]])
                kb.dma("sp", QTs, qT[(g * 8 + h) * 128:(g * 8 + h + 1) * 128, :], writes=[b_QT[s]])
                if g == 0:
                    kb.dma("sp", Vt[:, 0, :, :], ve[0].rearrange("(k m) c -> m k c", m=128)[:, :, hc], writes=[b_Vt[s]])
                elif g == 1:
                    for k in range(3):
                        kb.dma("sp", Vt[:, :, k, :], ve[1][k * 512:(k + 1) * 512, :].rearrange("(m r) c -> m r c", r=4)[:, :, hc],
                               writes=[b_Vt[s]])
                else:
                    kb.dma("sp", Vt[:, :, 0, :], ve[2][0:2048, :].rearrange("(m r) c -> m r c", r=16)[:, :, hc], writes=[b_Vt[s]])
                    kb.dma("sp", Vt[0:64, :, 1, :], ve[2][2048:3072, :].rearrange("(m r) c -> m r c", r=16)[:, :, hc], writes=[b_Vt[s]])
                for r in range(d):
                    for qb in range(nqb):
                        i0 = qb * QB
                        q0 = r + d * i0
                        qsl = QTs[:, q0:q0 + d * (QB - 1) + 1:d] if d > 1 else QTs[:, q0:q0 + QB]
                        kA, kB_ = qb, qb + 1
                        nB = 128 if g < 2 else 64
                        eA = r + d * 128 * kA
                        eB = r + d * 128 * kB_
                        kslA = KT[:, eA:eA + d * 127 + 1:d] if d > 1 else KT[:, eA:eA + 128]
                        kslB = KT[:, eB:eB + d * (nB - 1) + 1:d] if d > 1 else KT[:, eB:eB + nB]
                        p = it % 2
                        it += 1
                        bS, bN, bD = p, 2 + p, 4 + p
                        psS = kb.psum[bS][:, 0:256].rearrange("p (a n) -> p a n", a=2)

                        def qk(e, kslA=kslA, kslB=kslB, qsl=qsl, psS=psS, nB=nB, QB=QB):
                            e.matmul(psS[:, 0, 0:QB], kslA, qsl, start=True, stop=True)
                            return e.matmul(psS[0:nB, 1, 0:QB], kslB, qsl, start=True, stop=True)
                        kb.op("pe", qk, reads=[b_KT[s], b_QT[s]], writes=[kb.psb[bS]])
                        pe_, pm_ = Pe[p], Pm[p]

                        def ex(e, psS=psS, pe_=pe_, nB=nB, QB=QB):
                            e.activation(out=pe_[:, 0, 0:QB], in_=psS[:, 0, 0:QB], func=AF.Exp, scale=SCALE)
                            return e.activation(out=pe_[0:nB, 1, 0:QB], in_=psS[0:nB, 1, 0:QB], func=AF.Exp, scale=SCALE)
                        kb.op("act", ex, writes=[kb.psb[bS], b_Pe[p]])
                        vc = VM_OFF[g] + r * nkb

                        def mk(e, pe_=pe_, pm_=pm_, nB=nB, QB=QB, vc=vc, kA=kA, kB_=kB_):
                            e.scalar_tensor_tensor(out=pm_[:, 0, 0:QB], in0=pe_[:, 0, 0:QB], scalar=vm[:, vc + kA:vc + kA + 1],
                                                   in1=masks[:, 0:QB], op0=ALU.mult, op1=ALU.mult)
                            return e.scalar_tensor_tensor(out=pm_[0:nB, 1, 0:QB], in0=pe_[0:nB, 1, 0:QB],
                                                          scalar=vm[0:nB, vc + kB_:vc + kB_ + 1], in1=masks[0:nB, 128:128 + QB],
                                                          op0=ALU.mult, op1=ALU.mult)
                        kb.op("dve", mk, reads=[b_Pe[p], b_const], writes=[b_Pm[p]])

                        def pv(e, pm_=pm_, Vt=Vt, r=r, kA=kA, kB_=kB_, nB=nB, QB=QB, bN=bN, bD=bD):
                            e.matmul(kb.psum[bN][:, 0:QB], Vt[:, r, kA, :], pm_[:, 0, 0:QB], start=True, stop=False)
                            e.matmul(kb.psum[bN][:, 0:QB], Vt[0:nB, r, kB_, :], pm_[0:nB, 1, 0:QB], start=False, stop=True)
                            e.matmul(kb.psum[bD][:, 0:QB], onesb[:, :], pm_[:, 0, 0:QB], start=True, stop=False)
                            return e.matmul(kb.psum[bD][:, 0:QB], onesb[0:nB, :], pm_[0:nB, 1, 0:QB], start=False, stop=True)
                        kb.op("pe", pv, reads=[b_Pm[p], b_Vt[s], b_const], writes=[kb.psb[bN], kb.psb[bD]])
                        tsl = slice(q0, q0 + d * (QB - 1) + 1, d) if d > 1 else slice(q0, q0 + QB)
                        if g == 0:
                            def ac(e, tsl=tsl, bN=bN, bD=bD, QB=QB):
                                e.tensor_copy(out=accn[:, tsl], in_=kb.psum[bN][:, 0:QB])
                                return e.tensor_copy(out=accd[:, tsl], in_=kb.psum[bD][:, 0:QB])
                        else:
                            def ac(e, tsl=tsl, bN=bN, bD=bD, QB=QB):
                                e.tensor_tensor(out=accn[:, tsl], in0=kb.psum[bN][:, 0:QB], in1=accn[:, tsl], op=ALU.add)
                                return e.tensor_tensor(out=accd[:, tsl], in0=kb.psum[bD][:, 0:QB], in1=accd[:, tsl], op=ALU.add)
                        kb.op("dve", ac, reads=[b_acc], writes=[kb.psb[bN], kb.psb[bD], b_acc])
            kb.op("dve", lambda e: e.reciprocal(out=accd, in_=accd), reads=[b_acc], writes=[b_acc])
            kb.op("pool", lambda e, h=h: e.tensor_tensor(out=oT[:, h, :], in0=accn, in1=accd, op=ALU.mult),
                  reads=[b_acc], writes=[b_oT])
        kb.barrier()

        o = 80 * KBY
        sgt = [[V(o + (i * 3 + f) * 1024, 1024, BF16) for f in range(3)] for i in range(2)]; o += 6 * 1024
        mt = [[V(o + (i * 3 + f) * 2048, 2048, F32) for f in range(3)] for i in range(2)]; o += 6 * 2048
        mo = [V(o + i * 1024, 1024, BF16) for i in range(2)]; o += 2 * 1024
        assert o <= 128 * KBY
        o = 128 * KBY
        wsA = [V(o + i * 20480, 8192, BF16, "p (k n) -> p k n", k=16) for i in range(2)]
        wsB = [V(o + i * 20480 + 8192, 8192, BF16, "p (k n) -> p k n", k=16) for i in range(2)]
        wsC = [V(o + i * 20480 + 16384, 4096, BF16, "p (k n) -> p k n", k=8) for i in range(2)]
        b_ws = [Buf(), Buf()]
        b_sg, b_mt, b_mo = ([Buf(), Buf()] for _ in range(3))

        def load3(cs):
            sl = cs % 2
            c0 = cs * 256
            load_wslab(kb, "pool", wsA[sl], b_ws[sl], woa, [(c0, 256)], 16)
            load_wslab(kb, "pool", wsB[sl], b_ws[sl], wob, [(c0, 256)], 16)
            load_wslab(kb, "pool", wsC[sl], b_ws[sl], woc, [(c0, 256)], 8)
        load3(0)
        load3(1)
        it = 0
        for cs in range(16):
            sl = cs % 2
            for j in range(2):
                crow = cs * 256 + j * 128
                for h in range(2):
                    hs = slice(h * 512, (h + 1) * 512)
                    p = it % 2
                    it += 1
                    bk = p * 3

                    def mm3(e, sl=sl, j=j, hs=hs, bk=bk):
                        ins = None
                        for kc in range(16):
                            ins = e.matmul(kb.psum[bk][:, :], wsA[sl][:, kc, j * 128:(j + 1) * 128], AinT[:, kc, hs],
                                           start=(kc == 0), stop=(kc == 15))
                        for kc in range(16):
                            ins = e.matmul(kb.psum[bk + 1][:, :], wsB[sl][:, kc, j * 128:(j + 1) * 128], ufT[:, kc, hs],
                                           start=(kc == 0), stop=(kc == 15))
                        for kc in range(8):
                            ins = e.matmul(kb.psum[bk + 2][:, :], wsC[sl][:, kc, j * 128:(j + 1) * 128], oT[:, kc, hs],
                                           start=(kc == 0), stop=(kc == 7))
                        return ins
                    kb.op("pe", mm3, reads=[b_ws[sl], b_Ain, b_uf, b_oT], writes=[kb.psb[bk], kb.psb[bk + 1], kb.psb[bk + 2]])
                    for f in range(3):
                        kb.dma("sp", sgt[p][f], sgT[f * D + crow:f * D + crow + 128, hs], writes=[b_sg[p]])
                    for f in range(3):
                        kb.op("dve", lambda e, p=p, f=f, bk=bk: e.tensor_tensor(out=mt[p][f], in0=kb.psum[bk + f][:, :], in1=sgt[p][f],
                                                                                op=ALU.mult),
                              reads=[b_sg[p]], writes=[kb.psb[bk + f], b_mt[p]])
                    kb.op("pool", lambda e, p=p: e.tensor_tensor(out=mt[p][0], in0=mt[p][0], in1=mt[p][1], op=ALU.add),
                          reads=[b_mt[p]], writes=[b_mt[p]])
                    kb.op("pool", lambda e, p=p: e.tensor_tensor(out=mo[p], in0=mt[p][0], in1=mt[p][2], op=ALU.add),
                          reads=[b_mt[p]], writes=[b_mo[p]])
                    kb.dma("sp", mergedT[crow:crow + 128, hs], mo[p], reads=[b_mo[p]])
            if cs + 2 < 16:
                load3(cs + 2)
        kb.barrier()

        def tokmajor_residual(mTres, b_mT, wsrc, jgate, xin_d, xo_d, final):
            o = 64 * KBY
            wsl = [V(o + i * 32 * KBY, 32 * KBY, BF16, "p (k n) -> p k n", k=32) for i in range(2)]
            o = 128 * KBY
            gbc = V(o, 16384, F32); o += 16384
            xi = [V(o + i * 2048, 2048, F32) for i in range(3)]; o += 3 * 2048
            tm = [V(o + i * 2048, 2048, F32) for i in range(3)]; o += 3 * 2048
            b_g, b_w = Buf(), [Buf(), Buf()]
            b_xi, b_tm = [Buf() for _ in range(3)], [Buf() for _ in range(3)]
            kb.dma("sp", gbc, modv[jgate:jgate + 1, :].partition_broadcast(128), writes=[b_g])
            for s in range(2):
                load_wslab(kb, "pool", wsl[s], b_w[s], wsrc, [(s * 512, 512)], 32)
            it = 0
            for cs in range(8):
                sl = cs % 2
                cols = slice(cs * 512, (cs + 1) * 512)
                for tt in range(8):
                    rows = slice(tt * 128, (tt + 1) * 128)
                    pb = it % 8
                    q = it % 3
                    it += 1

                    def mm(e, sl=sl, rows=rows, pb=pb):
                        ins = None
                        for kc in range(32):
                            ins = e.matmul(kb.psum[pb][:, :], mTres[:, kc, rows], wsl[sl][:, kc, :], start=(kc == 0), stop=(kc == 31))
                        return ins
                    kb.op("pe", mm, reads=[b_w[sl], b_mT], writes=[kb.psb[pb]])
                    kb.dma("sp", xi[q], xin_d[rows, cols], writes=[b_xi[q]])
                    kb.op("dve", lambda e, pb=pb, q=q, cols=cols: e.tensor_tensor(out=tm[q], in0=kb.psum[pb][:, :], in1=gbc[:, cols], op=ALU.mult),
                          reads=[b_g], writes=[kb.psb[pb], b_tm[q]])
                    kb.op("pool", lambda e, q=q: e.tensor_tensor(out=tm[q], in0=tm[q], in1=xi[q], op=ALU.add),
                          reads=[b_xi[q], b_tm[q]], writes=[b_tm[q]])
                    kb.dma("sp", xo_d[rows, cols], tm[q], reads=[b_tm[q]], final=final)
                if cs + 2 < 8:
                    load_wslab(kb, "pool", wsl[sl], b_w[sl], wsrc, [((cs + 2) * 512, 512)], 32)

        mT = V(0, 64 * KBY, BF16, "p (c n) -> p c n", c=32)
        b_mT = Buf()
        for kg in range(4):
            kb.dma("sp", mT[:, kg * 8:(kg + 1) * 8, :], mergedT[kg * 1024:(kg + 1) * 1024, :].rearrange("(k p) n -> p k n", p=128),
                   writes=[b_mT])
        tokmajor_residual(mT, b_mT, wo, 2, x, x1, False)
        kb.barrier()

        hT = V(0, 64 * KBY, BF16, "p (c n) -> p c n", c=32)
        o = 128 * KBY
        xt = [V(o + i * 16384, 16384, F32) for i in range(2)]
        xn = [V(o + 32768 + i * 8192, 8192, BF16) for i in range(2)]
        b_hT = [[Buf(), Buf()] for _ in range(8)]
        hT_all = [b for pp in b_hT for b in pp]
        tagbufs = ([Buf(), Buf()], [Buf(), Buf()], [Buf() for _ in range(8)])
        emit_norm_T(kb, x1, hT, b_hT, modT, 3, 4, gT, xt, xn, ss, {"a": a_t, "sh": None}, ident, b_const, tagbufs)
        kb.barrier()

        o = 64 * KBY
        wsl = [V(o + i * 32 * KBY, 32 * KBY, BF16, "p (k n) -> p k n", k=32) for i in range(2)]
        o = 128 * KBY
        rl = [V(o + i * 2048, 2048, F32) for i in range(4)]; o += 4 * 2048
        ao = [V(o + i * 1024, 1024, BF16) for i in range(4)]; o += 4 * 1024
        b_w, b_rl, b_ao = [Buf(), Buf()], [Buf() for _ in range(4)], [Buf() for _ in range(4)]
        for s in range(2):
            load_wslab(kb, "pool", wsl[s], b_w[s], w1, [(s * 512, 512)], 32)
        it = 0
        for cs in range(DFF // 512):
            sl = cs % 2
            for bi in range(4):
                pbk = (it % 4) * 2
                it += 1

                def mm(e, sl=sl, bi=bi, pbk=pbk):
                    ins = None
                    for kc in range(32):
                        for h in range(2):
                            ins = e.matmul(kb.psum[pbk + h][:, :], wsl[sl][:, kc, bi * 128:(bi + 1) * 128],
                                           hT[:, kc, h * 512:(h + 1) * 512], start=(kc == 0), stop=(kc == 31))
                    return ins
                kb.op("pe", mm, reads=[b_w[sl]] + hT_all, writes=[kb.psb[pbk], kb.psb[pbk + 1]])
                frow = cs * 512 + bi * 128
                for h in range(2):
                    q = (it * 2 + h) % 4
                    kb.op("act", lambda e, q=q, pbk=pbk, h=h: e.activation(out=rl[q], in_=kb.psum[pbk + h][:, :], func=AF.Relu),
                          writes=[kb.psb[pbk + h], b_rl[q]])
                    kb.op("pool", lambda e, q=q: e.tensor_tensor(out=ao[q], in0=rl[q], in1=rl[q], op=ALU.mult),
                          reads=[b_rl[q]], writes=[b_ao[q]])
                    kb.dma("sp", aT[frow:frow + 128, h * 512:(h + 1) * 512], ao[q], reads=[b_ao[q]])
            if cs + 2 < DFF // 512:
                load_wslab(kb, "pool", wsl[sl], b_w[sl], w1, [((cs + 2) * 512, 512)], 32)
        kb.barrier()

        o = 0
        w2s = [V(o + i * 8192, 8192, BF16, "p (k n) -> p k n", k=8) for i in range(3)]; o += 3 * 8192
        ag = [V(o + i * 16384, 16384, BF16, "p (k n) -> p k n", k=8) for i in range(3)]; o += 3 * 16384
        gbc = V(o, 16384, F32); o += 16384
        xi = [V(o + i * 2048, 2048, F32) for i in range(3)]; o += 3 * 2048
        tm = [V(o + i * 2048, 2048, F32) for i in range(3)]; o += 3 * 2048
        b_g, b_w2, b_ag = Buf(), [Buf() for _ in range(3)], [Buf() for _ in range(3)]
        b_xi, b_tm = [Buf() for _ in range(3)], [Buf() for _ in range(3)]
        kb.dma("sp", gbc, modv[5:6, :].partition_broadcast(128), writes=[b_g])
        it = 0
        it2 = 0
        NKG = DFF // 1024
        for cs in range(8):
            cols = slice(cs * 512, (cs + 1) * 512)
            for kg in range(NKG):
                s3 = it % 3
                it += 1
                kb.dma("pool", w2s[s3], w2[kg * 1024:(kg + 1) * 1024, cols].rearrange("(k p) n -> p k n", p=128), writes=[b_w2[s3]])
                kb.dma("sp", ag[s3], aT[kg * 1024:(kg + 1) * 1024, :].rearrange("(k p) n -> p k n", p=128), writes=[b_ag[s3]])

                def mm(e, s3=s3, kg=kg):
                    ins = None
                    for tt in range(8):
                        for k in range(8):
                            ins = e.matmul(kb.psum[tt][:, :], ag[s3][:, k, tt * 128:(tt + 1) * 128], w2s[s3][:, k, :],
                                           start=(kg == 0 and k == 0), stop=(kg == NKG - 1 and k == 7))
                    return ins
                kb.op("pe", mm, reads=[b_w2[s3], b_ag[s3]], writes=list(kb.psb))
            for tt in range(8):
                rows = slice(tt * 128, (tt + 1) * 128)
                q = it2 % 3
                it2 += 1
                kb.dma("sp", xi[q], x1[rows, cols], writes=[b_xi[q]])
                kb.op("dve", lambda e, tt=tt, q=q, cols=cols: e.tensor_tensor(out=tm[q], in0=kb.psum[tt][:, :], in1=gbc[:, cols], op=ALU.mult),
                      reads=[b_g], writes=[kb.psb[tt], b_tm[q]])
                kb.op("pool", lambda e, q=q: e.tensor_tensor(out=tm[q], in0=tm[q], in1=xi[q], op=ALU.add),
                      reads=[b_xi[q], b_tm[q]], writes=[b_tm[q]])
                kb.dma("sp", xout[rows, cols], tm[q], reads=[b_tm[q]], final=True)
        kb.barrier()

        o = 0
        gfb = V(o, 16384, F32); o += 16384
        xt = [V(o + i * 16384, 16384, F32) for i in range(2)]; o += 2 * 16384
        yt = [V(o + i * 16384, 16384, F32) for i in range(2)]; o += 2 * 16384
        b_gf, b_xt, b_yt, b_s2 = Buf(), [Buf(), Buf()], [Buf(), Buf()], [Buf() for _ in range(8)]
        kb.dma("sp", gfb, gfin.partition_broadcast(128), writes=[b_gf])
        for tt in range(8):
            s = tt % 2
            rows = slice(tt * 128, (tt + 1) * 128)
            kb.dma("sp", xt[s], xout[rows, :], writes=[b_xt[s]])
            kb.op("act", lambda e, s=s, tt=tt: e.activation(out=yt[s], in_=xt[s], func=AF.Square, accum_out=ss[:, tt:tt + 1]),
                  reads=[b_xt[s]], writes=[b_yt[s], b_s2[tt]])
            kb.op("dve", lambda e, tt=tt: e.tensor_scalar(out=ss[:, 8 + tt:9 + tt], in0=ss[:, tt:tt + 1], scalar1=1.0 / D, scalar2=EPS,
                                                         op0=ALU.mult, op1=ALU.add), reads=[b_s2[tt]], writes=[b_s2[tt]])
            kb.op("act", lambda e, tt=tt: e.activation(out=ss[:, 16 + tt:17 + tt], in_=ss[:, 8 + tt:9 + tt], func=AF.Sqrt),
                  reads=[b_s2[tt]], writes=[b_s2[tt]])
            kb.op("dve", lambda e, tt=tt: e.reciprocal(out=ss[:, 24 + tt:25 + tt], in_=ss[:, 16 + tt:17 + tt]),
                  reads=[b_s2[tt]], writes=[b_s2[tt]])
            kb.op("dve", lambda e, s=s, tt=tt: e.scalar_tensor_tensor(out=yt[s], in0=xt[s], scalar=ss[:, 24 + tt:25 + tt], in1=gfb,
                                                                     op0=ALU.mult, op1=ALU.mult),
                  reads=[b_xt[s], b_s2[tt], b_gf], writes=[b_yt[s]])
            kb.dma("sp", yout[rows, :], yt[s], reads=[b_yt[s]], final=True)
        kb.finish()
        with nc.Block() as block:
            kb.replay(block)
    return nc


_PROGS = {}


def _prog(name):
    if name not in _PROGS:
        if name == "A":
            _PROGS[name] = build_A(full_slabs_A(), IN_COLS)
        else:
            _PROGS[name] = build_B()
    return _PROGS[name]


def run_A(xcur, modl, g_mix_l, w_in_l, pos):
    nc = _prog("A")
    invf32, jm = rope_consts()
    modT = np.concatenate([t_layout(modl[j * D:(j + 1) * D], 32) for j in range(6)], axis=1)
    gT = t_layout(g_mix_l, 32)
    ident = np.eye(128, dtype=np.float32).astype(NPBF)
    w = np.ascontiguousarray(w_in_l)
    in_maps = []
    for c in range(NCORE):
        in_maps.append({"x": np.ascontiguousarray(xcur[c * TC:(c + 1) * TC]), "modT": modT, "gT": gT, "w_in": w,
                        "pos": np.ascontiguousarray(pos[c * TC:(c + 1) * TC]).reshape(1, TC).astype(np.int32),
                        "invf": invf32, "jm": jm, "ident": ident})
    res = run_bass_kernel_spmd(nc, in_maps, core_ids=list(range(NCORE)))
    return res.results


def glue_B(A_res):
    zT = np.concatenate([r["zT"] for r in A_res], axis=1)
    uT = np.concatenate([r["uT"] for r in A_res], axis=1)
    kT = np.concatenate([r["qkT"][DW_W:] for r in A_res], axis=1)
    v = np.concatenate([r["vtm"] for r in A_res], axis=0)
    PAD = 2048
    zpad = np.concatenate([np.zeros((SC_W, PAD), zT.dtype), zT], axis=1)
    upad = np.concatenate([np.zeros((CF_W, PAD), uT.dtype), uT], axis=1)
    kpad = np.concatenate([np.zeros((DW_W, PAD), kT.dtype), kT], axis=1)
    vpad = np.concatenate([np.zeros((PAD, DW_W), v.dtype), v], axis=0)
    out = []
    for c in range(NCORE):
        t0 = PAD + c * TC
        m = {"zTe": np.ascontiguousarray(zpad[:, t0 - 2:t0 + TC]),
             "uTe": np.ascontiguousarray(upad[:, t0 - 30:t0 + TC]),
             "scbT": A_res[c]["scbT"], "qT": np.ascontiguousarray(A_res[c]["qkT"][:DW_W]), "sgT": A_res[c]["sgT"]}
        vm = np.zeros((128, VM_COLS), np.float32)
        for g, (d, Lg, nkb, QB, nqb) in enumerate(GROUPS):
            W = 128 * d
            m[f"kTe{g}"] = np.ascontiguousarray(kpad[g * 1024:(g + 1) * 1024, t0 - W:t0 + TC])
            m[f"ve{g}"] = np.ascontiguousarray(vpad[t0 - W:t0 + TC, g * 1024:(g + 1) * 1024])
            for r in range(d):
                for kbi in range(nkb):
                    for mm in range(128):
                        e = r + d * (128 * kbi + mm)
                        if e < Lg and (c * TC - W + e) >= 0:
                            vm[mm, VM_OFF[g] + r * nkb + kbi] = 1.0
        m["vm"] = vm
        out.append(m)
    return out


def run_B(xcur, modl, glue, P, l):
    nc = _prog("B")
    modT = np.concatenate([t_layout(modl[j * D:(j + 1) * D], 32) for j in range(6)], axis=1)
    modv = np.ascontiguousarray(modl.reshape(6, D))
    gT = t_layout(P["g_mlp"][l], 32)
    gfin = np.ascontiguousarray(P["g_final"]).reshape(1, D)
    ca, cb = P["conv_a"][l], P["conv_b"][l]
    cva = np.ascontiguousarray(ca.reshape(3, 16, 128).transpose(2, 1, 0)).reshape(128, 48)
    cvb = np.ascontiguousarray(cb.reshape(31, 16, 128).transpose(2, 1, 0)).reshape(128, 16 * 31)
    cvec = np.concatenate([t_layout(P["conv_b_bias"][l], 16), t_layout(P["ln_cf_g"][l], 16), t_layout(P["ln_cf_b"][l], 16)], axis=1)
    mm_, ii_ = np.meshgrid(np.arange(128), np.arange(128), indexing="ij")
    masks = np.concatenate([(mm_ >= ii_), (mm_ <= ii_)], axis=1).astype(np.float32).astype(NPBF)
    shared = {"modT": modT, "modv": modv, "gT": gT, "gfin": gfin, "cva": cva, "cvb": cvb, "cvec": np.ascontiguousarray(cvec),
              "woa": np.ascontiguousarray(P["w_out_a"][l]), "wob": np.ascontiguousarray(P["w_out_b"][l]),
              "woc": np.ascontiguousarray(P["w_out_c"][l]), "wo": np.ascontiguousarray(P["w_o"][l]),
              "w1": np.ascontiguousarray(P["w_mlp1"][l]), "w2": np.ascontiguousarray(P["w_mlp2"][l]),
              "masks": masks, "onesb": np.ones((128, 128), np.float32).astype(NPBF), "onesf": np.ones((128, 128), np.float32),
              "ident": np.eye(128, dtype=np.float32).astype(NPBF)}
    in_maps = []
    for c in range(NCORE):
        m = dict(shared)
        m.update(glue[c])
        m["x"] = np.ascontiguousarray(xcur[c * TC:(c + 1) * TC])
        in_maps.append(m)
    res = run_bass_kernel_spmd(nc, in_maps, core_ids=list(range(NCORE)))
    xo = np.concatenate([r["xout"] for r in res.results], axis=0)
    yo = np.concatenate([r["yout"] for r in res.results], axis=0)
    return xo, yo


def kernel(**inp):
    P = {k: np.asarray(v) for k, v in inp.items()}
    x = P["x"][0].astype(np.float32)
    pos = P["positions"][0]
    mod = run_ada(P["c"].astype(np.float32), P["w_ada"], P["b_ada"])
    xcur = x
    yo = None
    for l in range(2):
        A_res = run_A(xcur, mod[l], P["g_mix"][l], P["w_in"][l], pos)
        glue = glue_B(A_res)
        del A_res
        xcur, yo = run_B(xcur, mod[l], glue, P, l)
    return yo.reshape(1, NCORE * TC, D).astype(np.float32)
```

```python
import numpy as np
import ml_dtypes
import concourse.bass as bass
import concourse.mybir as mybir
from concourse.bass_utils import run_bass_kernel_spmd
from contextlib import ExitStack

F32 = mybir.dt.float32
BF16 = mybir.dt.bfloat16
I32 = mybir.dt.int32
U8 = mybir.dt.uint8
ALU = mybir.AluOpType
AF = mybir.ActivationFunctionType
NPBF = ml_dtypes.bfloat16


class Tok:
    __slots__ = ("sem", "key", "val", "stream")

    def __init__(self, sem, key, val, stream):
        self.sem, self.key, self.val, self.stream = sem, key, val, stream


class Buf:
    __slots__ = ("w", "r", "name")

    def __init__(self, name=""):
        self.w = None
        self.r = {}
        self.name = name


class KB:
    ENG = ("pe", "act", "dve", "pool", "sp")

    def __init__(self, nc, es, arena_kb=200, ndma=(("sp", 14), ("pool", 14), ("act", 6))):
        self.nc = nc
        self.eng = {"pe": nc.tensor, "act": nc.scalar, "dve": nc.vector, "pool": nc.gpsimd, "sp": nc.sync}
        self.streams = {k: [] for k in self.ENG}
        self.sem = {}
        self.cnt = {k: 0 for k in self.ENG}
        self.waited = {k: {} for k in self.ENG}
        for k in self.ENG:
            self.sem[k] = es.enter_context(nc.semaphore("s_" + k))
        self.dpool = {}
        self.dcnt = {}
        self.dnext = {}
        for q, n in ndma:
            self.dpool[q] = []
            for i in range(n):
                s = es.enter_context(nc.semaphore(f"d_{q}{i}"))
                self.dpool[q].append((s, f"d_{q}{i}"))
                self.dcnt[f"d_{q}{i}"] = 0
            self.dnext[q] = 0
        self.arena = es.enter_context(nc.sbuf_tensor("arena", [128, arena_kb * 1024], U8))
        self.arena_bytes = arena_kb * 1024
        self.psum = [es.enter_context(nc.psum_tensor(f"ps{i}", [128, 512], F32)) for i in range(8)]
        self.psb = [Buf(f"psb{i}") for i in range(8)]
        self.final_toks = []
        self.n_instr = 0

    def view(self, off, nbytes, dt, pattern=None, **kw):
        assert off % 32 == 0, off
        assert off + nbytes <= self.arena_bytes, (off, nbytes)
        v = self.arena[:, off:off + nbytes].bitcast(dt)
        if pattern:
            v = v.rearrange(pattern, **kw)
        return v

    def _wait(self, stream, t):
        if self.waited[stream].get(t.key, 0) >= t.val:
            return
        self.waited[stream][t.key] = t.val
        self.streams[stream].append(("wait", t.sem, t.val))

    def _deps(self, stream, reads, writes, is_dma):
        for b in reads:
            if b.w is not None:
                self._wait(stream, b.w)
        for b in writes:
            if b.w is not None:
                if is_dma or b.w.stream != stream:
                    self._wait(stream, b.w)
            for t in b.r.values():
                if is_dma or t.stream != stream:
                    self._wait(stream, t)

    def _mark(self, tok, reads, writes):
        for b in writes:
            b.w = tok
            b.r = {}
        for b in reads:
            if b not in writes:
                b.r[tok.key] = tok

    def op(self, stream, fn, reads=(), writes=()):
        self._deps(stream, reads, writes, False)
        self.cnt[stream] += 1
        tok = Tok(self.sem[stream], "s_" + stream, self.cnt[stream], stream)
        self.streams[stream].append(("op", fn))
        self._mark(tok, reads, writes)
        return tok

    def dma(self, q, out, in_, reads=(), writes=(), final=False, **kw):
        self._deps(q, reads, writes, True)
        pool = self.dpool[q]
        sem, key = pool[self.dnext[q] % len(pool)]
        self.dnext[q] += 1
        prev = self.dcnt[key]
        if prev > 0:
            self._wait(q, Tok(sem, key, prev, None))
        self.dcnt[key] = prev + 16
        tok = Tok(sem, key, prev + 16, None)
        self.streams[q].append(("dma", out, in_, sem, kw))
        self._mark(tok, reads, writes)
        if final:
            self.final_toks.append(tok)
        return tok

    def barrier(self, bufs=()):
        toks = []
        for s in self.ENG:
            if self.cnt[s] > 0:
                toks.append(Tok(self.sem[s], "s_" + s, self.cnt[s], s))
        for q, pool in self.dpool.items():
            for sem, key in pool:
                if self.dcnt[key] > 0:
                    toks.append(Tok(sem, key, self.dcnt[key], None))
        for s in self.ENG:
            for t in toks:
                if t.stream != s:
                    self._wait(s, t)

    def finish(self):
        for t in self.final_toks:
            self._wait("sp", t)
        for s in self.ENG:
            if s != "sp" and self.cnt[s] > 0:
                self._wait("sp", Tok(self.sem[s], "s_" + s, self.cnt[s], s))

    def replay(self, block):
        nc = self.nc
        kb = self

        def run(stream, e):
            for item in kb.streams[stream]:
                kind = item[0]
                if kind == "wait":
                    e.wait_ge(item[1], item[2])
                elif kind == "op":
                    ins = item[1](e)
                    ins.then_inc(kb.sem[stream], 1)
                elif kind == "dma":
                    e.dma_start(out=item[1], in_=item[2], **item[4]).then_inc(item[3], 16)
                elif kind == "cc":
                    e.collective_compute("AllGather", ALU.bypass, replica_groups=[list(range(8))],
                                         ins=[item[1]], outs=[item[2]]).then_inc(item[3], 16)
                kb.n_instr += 1

        @block.tensor
        def _(e):
            run("pe", e)

        @block.scalar
        def _(e):
            run("act", e)

        @block.vector
        def _(e):
            run("dve", e)

        @block.gpsimd
        def _(e):
            run("pool", e)

        @block.sync
        def _(e):
            run("sp", e)


D = 4096
TC = 1024
NCORE = 8
ADA_COLS = 6 * D // NCORE


def build_ada():
    nc = bass.Bass("TRN2", target_bir_lowering=False)
    cT = nc.dram_tensor("cT", [128, 32], F32, kind="ExternalInput").ap()
    wada = nc.dram_tensor("wada", [2, D, ADA_COLS], F32, kind="ExternalInput").ap()
    bada = nc.dram_tensor("bada", [1, 2 * ADA_COLS], F32, kind="ExternalInput").ap()
    modp = nc.dram_tensor("modp", [1, 2 * ADA_COLS], F32, kind="ExternalOutput").ap()
    with ExitStack() as es:
        kb = KB(nc, es, arena_kb=120)
        off = 0
        csb = kb.view(off, 128, F32); off += 128
        csil = kb.view(off, 128, F32); off += 128
        bsb = kb.view(off, 2 * ADA_COLS * 4, F32); off += 2 * ADA_COLS * 4
        osb = kb.view(off, 2 * ADA_COLS * 4, F32); off += 2 * ADA_COLS * 4
        NW = 4
        wt = []
        for i in range(NW):
            wt.append(kb.view(off, 8 * 512 * 4, F32, "p (k n) -> p k n", k=8)); off += 8 * 512 * 4
        b_c, b_cs, b_b, b_o = Buf(), Buf(), Buf(), Buf()
        b_w = [Buf() for _ in range(NW)]
        kb.dma("sp", csb, cT, writes=[b_c])
        kb.dma("sp", bsb[0:1, :], bada, writes=[b_b])
        kb.op("act", lambda e: e.activation(out=csil, in_=csb, func=AF.Silu), reads=[b_c], writes=[b_cs])
        wi = 0
        for l in range(2):
            for cb in range(ADA_COLS // 512):
                pb = (l * 6 + cb) % 8
                for kg in range(4):
                    slot = wi % NW
                    wi += 1
                    src = wada[l, kg * 1024:(kg + 1) * 1024, cb * 512:(cb + 1) * 512].rearrange("(k p) n -> p k n", p=128)
                    kb.dma("sp" if wi % 2 else "act", wt[slot], src, writes=[b_w[slot]])

                    def mm(e, slot=slot, kg=kg, pb=pb):
                        ins = None
                        for k in range(8):
                            kc = kg * 8 + k
                            ins = e.matmul(kb.psum[pb][0:1, :], csil[:, kc:kc + 1], wt[slot][:, k, :],
                                           start=(kc == 0), stop=(kc == 31))
                        return ins
                    kb.op("pe", mm, reads=[b_cs, b_w[slot]], writes=[kb.psb[pb]])
                o0 = l * ADA_COLS + cb * 512
                kb.op("dve", lambda e, pb=pb, o0=o0: e.tensor_tensor(out=osb[0:1, o0:o0 + 512], in0=kb.psum[pb][0:1, :],
                                                                    in1=bsb[0:1, o0:o0 + 512], op=ALU.add),
                      reads=[b_b], writes=[kb.psb[pb], b_o])
        kb.dma("sp", modp, osb[0:1, :], reads=[b_o], final=True)
        kb.finish()
        with nc.Block() as block:
            kb.replay(block)
    return nc


def run_ada(c, w_ada, b_ada):
    nc = build_ada()
    cT = np.ascontiguousarray(c.reshape(32, 128).T)
    in_maps = []
    for i in range(NCORE):
        sl = slice(i * ADA_COLS, (i + 1) * ADA_COLS)
        in_maps.append({
            "cT": cT,
            "wada": np.ascontiguousarray(w_ada[:, :, sl]),
            "bada": np.ascontiguousarray(b_ada[:, sl]).reshape(1, 2 * ADA_COLS),
        })
    res = run_bass_kernel_spmd(nc, in_maps, core_ids=list(range(NCORE)))
    mod = np.zeros((2, 6 * D), np.float32)
    for i in range(NCORE):
        r = res.results[i]["modp"].reshape(2, ADA_COLS)
        mod[:, i * ADA_COLS:(i + 1) * ADA_COLS] = r
    return mod


EPS = 1e-6
PI = float(np.pi)
SC_W, CF_W, DW_W = 2048, 2048, 3072
OFF_SCB, OFF_SCC, OFF_SCX = 0, 2048, 4096
OFF_CFA, OFF_CFG = 6144, 8192
OFF_Q, OFF_K, OFF_V = 10240, 13312, 16384
OFF_GATE = 19456
IN_COLS = 31744


def full_slabs_A():
    sl = []
    for j in range(SC_W // 256):
        sl.append(("sc", [(OFF_SCC + j * 256, 256), (OFF_SCX + j * 256, 256)], j * 256))
    for j in range(SC_W // 512):
        sl.append(("scb", [(OFF_SCB + j * 512, 512)], j * 512))
    for j in range(CF_W // 256):
        sl.append(("cf", [(OFF_CFG + j * 256, 256), (OFF_CFA + j * 256, 256)], j * 256))
    for j in range(2 * DW_W // 512):
        sl.append(("qk", [(OFF_Q + j * 512, 512)], j * 512))
    for j in range(DW_W // 512):
        sl.append(("v", [(OFF_V + j * 512, 512)], j * 512))
    for j in range(3 * D // 512):
        sl.append(("gate", [(OFF_GATE + j * 512, 512)], j * 512))
    return sl


def emit_norm_T(kb, xsrc, hT, b_hT, modT, jshift, jscale, gT, xt, xn, ss, sm, ident, b_const, tagbufs):
    a_t, sh_t = sm["a"], sm["sh"]
    b_a = Buf()
    kb.op("dve", lambda e: e.scalar_tensor_tensor(out=a_t, in0=modT[:, jscale * 32:(jscale + 1) * 32], scalar=1.0,
                                                  in1=gT, op0=ALU.add, op1=ALU.mult), reads=[b_const], writes=[b_a])
    b_xt, b_xn, b_ss = tagbufs
    for tt in range(TC // 128):
        s = tt % 2
        kb.dma("sp", xt[s], xsrc[tt * 128:(tt + 1) * 128, :], writes=[b_xt[s]])
        kb.op("act", lambda e, s=s, tt=tt: e.activation(out=xn[s], in_=xt[s], func=AF.Square, accum_out=ss[:, tt:tt + 1]),
              reads=[b_xt[s]], writes=[b_xn[s], b_ss[tt]])
        kb.op("dve", lambda e, tt=tt: e.tensor_scalar(out=ss[:, 8 + tt:9 + tt], in0=ss[:, tt:tt + 1], scalar1=1.0 / D, scalar2=EPS,
                                                     op0=ALU.mult, op1=ALU.add), reads=[b_ss[tt]], writes=[b_ss[tt]])
        kb.op("act", lambda e, tt=tt: e.activation(out=ss[:, 16 + tt:17 + tt], in_=ss[:, 8 + tt:9 + tt], func=AF.Sqrt),
              reads=[b_ss[tt]], writes=[b_ss[tt]])
        kb.op("dve", lambda e, tt=tt: e.reciprocal(out=ss[:, 24 + tt:25 + tt], in_=ss[:, 16 + tt:17 + tt]),
              reads=[b_ss[tt]], writes=[b_ss[tt]])
        kb.op("act", lambda e, s=s, tt=tt: e.activation(out=xn[s], in_=xt[s], func=AF.Copy, scale=ss[:, 24 + tt:25 + tt]),
              reads=[b_xt[s], b_ss[tt]], writes=[b_xn[s]])
        for g in range(4):
            pb = (tt * 4 + g) % 8
            pv = kb.psum[pb][:, :].bitcast(BF16)

            def tr(e, s=s, g=g, pv=pv):
                ins = None
                for j in range(8):
                    c = g * 8 + j
                    ins = e.transpose(pv[:, j * 128:(j + 1) * 128], xn[s][:, c * 128:(c + 1) * 128], ident)
                return ins
            kb.op("pe", tr, reads=[b_xn[s], b_const], writes=[kb.psb[pb]])
            for j in range(8):
                c = g * 8 + j
                dst = hT[:, c, tt * 128:(tt + 1) * 128]
                if g % 2 == 0:
                    kb.op("act", lambda e, pv=pv, j=j, c=c, dst=dst: e.activation(
                        out=dst, in_=pv[:, j * 128:(j + 1) * 128], func=AF.Identity,
                        bias=modT[:, jshift * 32 + c:jshift * 32 + c + 1], scale=a_t[:, c:c + 1]),
                        reads=[b_a, b_const], writes=[kb.psb[pb], b_hT[tt][0]])
                else:
                    kb.op("dve", lambda e, pv=pv, j=j, c=c, dst=dst: e.tensor_scalar(
                        out=dst, in0=pv[:, j * 128:(j + 1) * 128], scalar1=a_t[:, c:c + 1],
                        scalar2=modT[:, jshift * 32 + c:jshift * 32 + c + 1], op0=ALU.mult, op1=ALU.add),
                        reads=[b_a, b_const], writes=[kb.psb[pb], b_hT[tt][1]])


def load_wslab(kb, q, wsl_slot, b_slot, wsrc, segs, nk):
    co = 0
    for (c0, n) in segs:
        for kg in range(0, nk, 8):
            k1 = min(nk, kg + 8)
            kb.dma(q, wsl_slot[:, kg:k1, co:co + n],
                   wsrc[kg * 128:k1 * 128, c0:c0 + n].rearrange("(k p) n -> p k n", p=128), writes=[b_slot])
        co += n


def build_A(slabs, ncols):
    nc = bass.Bass("TRN2", target_bir_lowering=False)
    dt_in = lambda name, shape, dt: nc.dram_tensor(name, shape, dt, kind="ExternalInput").ap()
    dt_out = lambda name, shape, dt: nc.dram_tensor(name, shape, dt, kind="ExternalOutput").ap()
    x = dt_in("x", [TC, D], F32)
    modT_d = dt_in("modT", [128, 192], F32)
    gT_d = dt_in("gT", [128, 32], F32)
    w_in = dt_in("w_in", [D, ncols], F32)
    pos = dt_in("pos", [1, TC], I32)
    invf_d = dt_in("invf", [32, 1], F32)
    jm_d = dt_in("jm", [32, 32], BF16)
    ident_d = dt_in("ident", [128, 128], BF16)
    zT = dt_out("zT", [SC_W, TC], F32)
    scbT = dt_out("scbT", [SC_W, TC], F32)
    uT = dt_out("uT", [CF_W, TC], F32)
    qkT = dt_out("qkT", [2 * DW_W, TC], BF16)
    vtm = dt_out("vtm", [TC, DW_W], BF16)
    sgT = dt_out("sgT", [3 * D, TC], BF16)
    with ExitStack() as es:
        kb = KB(nc, es, arena_kb=196)
        off = 0
        hT = kb.view(off, 65536, BF16, "p (c n) -> p c n", c=32); off += 65536
        wsl = []
        for i in range(2):
            wsl.append(kb.view(off, 32768, BF16, "p (k n) -> p k n", k=32)); off += 32768
        R = off
        xt = [kb.view(R + i * 16384, 16384, F32) for i in range(2)]
        xn = [kb.view(R + 32768 + i * 8192, 8192, BF16) for i in range(2)]
        o2 = R
        st = []
        for i in range(4):
            st.append(kb.view(o2, 4096, F32)); o2 += 4096
        hold = []
        for i in range(2):
            hold.append(kb.view(o2, 4096, F32)); o2 += 4096
        ob = []
        for i in range(3):
            ob.append(kb.view(o2, 2048, BF16)); o2 += 2048
        q32b = kb.view(o2, 2048, BF16); o2 += 2048
        t1 = kb.view(o2, 4096, F32); o2 += 4096
        t2 = kb.view(o2, 4096, F32); o2 += 4096
        cosT = kb.view(o2, 4096, F32); o2 += 4096
        sinT = kb.view(o2, 4096, F32); o2 += 4096
        assert o2 <= R + 49152
        off = R + 49152
        modT = kb.view(off, 768, F32); off += 768
        gT = kb.view(off, 128, F32); off += 128
        a_t = kb.view(off, 128, F32); off += 128
        ss = kb.view(off, 128, F32); off += 128
        ident = kb.view(off, 256, BF16); off += 256
        jm = kb.view(off, 64, BF16); off += 64
        invf = kb.view(off, 32, F32); off += 32
        posi = kb.view(off, 4096, I32); off += 4096
        wk = [kb.view(off + i * 4096, 4096, F32) for i in range(3)]; off += 3 * 4096
        b_const = Buf()
        kb.dma("sp", modT, modT_d, writes=[b_const])
        kb.dma("sp", gT, gT_d, writes=[b_const])
        kb.dma("sp", ident, ident_d, writes=[b_const])
        kb.dma("sp", jm[0:32, :], jm_d, writes=[b_const])
        kb.dma("sp", invf[0:32, 0:1], invf_d, writes=[b_const])
        b_pos = Buf()
        kb.dma("sp", posi[0:32, :], pos.partition_broadcast(32), writes=[b_pos])
        b_hT = [[Buf(), Buf()] for _ in range(8)]
        hT_all = [b for p in b_hT for b in p]
        tagbufs = ([Buf(), Buf()], [Buf(), Buf()], [Buf() for _ in range(8)])
        b_w = [Buf(), Buf()]
        nslab = len(slabs)
        for s in range(min(2, nslab)):
            load_wslab(kb, "pool", wsl[s % 2], b_w[s % 2], w_in, slabs[s][1], 32)
        import os
        if not os.environ.get("A_SKIP_N1"):
            emit_norm_T(kb, x, hT, b_hT, modT, 0, 1, gT, xt, xn, ss, {"a": a_t, "sh": None}, ident, b_const, tagbufs)
        kb.barrier()
        b_tab = Buf()
        P32 = slice(0, 32)
        ROPE_ON = not os.environ.get("A_SKIP_ROPE")
        posf, ang, rr = wk[0], wk[1], wk[2]
        C1 = 6.28125
        C2 = float(2 * np.pi - 6.28125)
        _op = kb.op
        if not ROPE_ON:
            kb.op = lambda *a, **k: None
        kb.op("dve", lambda e: e.tensor_copy(out=posf[P32, :], in_=posi[P32, :]), reads=[b_pos], writes=[b_tab])
        kb.op("dve", lambda e: e.tensor_scalar(out=ang[P32, :], in0=posf[P32, :], scalar1=invf[P32, 0:1], scalar2=None,
                                               op0=ALU.mult), reads=[b_tab, b_const], writes=[b_tab])
        kb.op("dve", lambda e: e.tensor_scalar(out=posf[P32, :], in0=ang[P32, :], scalar1=float(1 / (2 * np.pi)), scalar2=None,
                                               op0=ALU.mult), reads=[b_tab], writes=[b_tab])
        kb.op("dve", lambda e: e.tensor_copy(out=posi[P32, :], in_=posf[P32, :]), reads=[b_tab], writes=[b_tab])
        kb.op("dve", lambda e: e.tensor_copy(out=posf[P32, :], in_=posi[P32, :]), reads=[b_tab], writes=[b_tab])
        kb.op("dve", lambda e: e.scalar_tensor_tensor(out=rr[P32, :], in0=posf[P32, :], scalar=-C1, in1=ang[P32, :],
                                                      op0=ALU.mult, op1=ALU.add), reads=[b_tab], writes=[b_tab])
        kb.op("dve", lambda e: e.scalar_tensor_tensor(out=rr[P32, :], in0=posf[P32, :], scalar=-C2, in1=rr[P32, :],
                                                      op0=ALU.mult, op1=ALU.add), reads=[b_tab], writes=[b_tab])

        def wrap(src, dst):
            kb.op("dve", lambda e: e.tensor_scalar(out=posf[P32, :], in0=src[P32, :], scalar1=PI, scalar2=-2 * PI,
                                                   op0=ALU.is_gt, op1=ALU.mult), reads=[b_tab], writes=[b_tab])
            kb.op("dve", lambda e: e.tensor_tensor(out=dst[P32, :], in0=src[P32, :], in1=posf[P32, :], op=ALU.add),
                  reads=[b_tab], writes=[b_tab])
            kb.op("dve", lambda e: e.tensor_scalar(out=posf[P32, :], in0=dst[P32, :], scalar1=-PI, scalar2=2 * PI,
                                                   op0=ALU.is_lt, op1=ALU.mult), reads=[b_tab], writes=[b_tab])
            kb.op("dve", lambda e: e.tensor_tensor(out=dst[P32, :], in0=dst[P32, :], in1=posf[P32, :], op=ALU.add),
                  reads=[b_tab], writes=[b_tab])
            kb.op("dve", lambda e: e.tensor_scalar(out=dst[P32, :], in0=dst[P32, :], scalar1=-PI, scalar2=PI,
                                                   op0=ALU.max, op1=ALU.min), reads=[b_tab], writes=[b_tab])
        wrap(rr, rr)
        kb.op("act", lambda e: e.activation(out=sinT[P32, :], in_=rr[P32, :], func=AF.Sin), reads=[b_tab], writes=[b_tab])
        kb.op("dve", lambda e: e.tensor_scalar(out=ang[P32, :], in0=rr[P32, :], scalar1=PI / 2, scalar2=None, op0=ALU.add),
              reads=[b_tab], writes=[b_tab])
        wrap(ang, ang)
        kb.op("act", lambda e: e.activation(out=cosT[P32, :], in_=ang[P32, :], func=AF.Sin), reads=[b_tab], writes=[b_tab])
        kb.op = _op

        b_st = [Buf() for _ in range(4)]
        b_hold = [Buf() for _ in range(2)]
        b_ob = [Buf() for _ in range(3)]
        b_q32, b_t1, b_t2 = Buf(), Buf(), Buf()
        cnt = {"blk": 0, "st": 0, "ob": 0, "v": 0}

        def gemm_block(slot, bi):
            pbk = (cnt["blk"] % 3) * 2
            cnt["blk"] += 1

            def mm(e):
                ins = None
                for kc in range(32):
                    for h in range(2):
                        ins = e.matmul(kb.psum[pbk + h][:, :], wsl[slot][:, kc, bi * 128:(bi + 1) * 128],
                                       hT[:, kc, h * 512:(h + 1) * 512], start=(kc == 0), stop=(kc == 31))
                return ins
            kb.op("pe", mm, reads=[b_w[slot]] + hT_all, writes=[kb.psb[pbk], kb.psb[pbk + 1]])
            return pbk

        def act_evac(pbk, dst, func, bdst):
            for h in range(2):
                kb.op("act", lambda e, h=h: e.activation(out=dst[:, h * 512:(h + 1) * 512], in_=kb.psum[pbk + h][:, :], func=func),
                      reads=[], writes=[kb.psb[pbk + h], bdst])

        def mul_evac(pbk, dst, other, bdst, bother):
            for h in range(2):
                kb.op("dve", lambda e, h=h: e.tensor_tensor(out=dst[:, h * 512:(h + 1) * 512], in0=kb.psum[pbk + h][:, :],
                                                            in1=other[:, h * 512:(h + 1) * 512], op=ALU.mult),
                      reads=[bother], writes=[kb.psb[pbk + h], bdst])

        def next_st():
            i = cnt["st"] % 4
            cnt["st"] += 1
            return i

        for s in range(nslab):
            kind, segs, orow = slabs[s]
            slot = s % 2
            if kind in ("sc", "cf"):
                for j in range(2):
                    p1 = gemm_block(slot, j)
                    act_evac(p1, hold[j], AF.Copy if kind == "sc" else AF.Sigmoid, b_hold[j])
                for j in range(2):
                    p2 = gemm_block(slot, 2 + j)
                    i = next_st()
                    mul_evac(p2, st[i], hold[j], b_st[i], b_hold[j])
                    dst = (zT if kind == "sc" else uT)[orow + j * 128:orow + (j + 1) * 128, :]
                    kb.dma("sp", dst, st[i], reads=[b_st[i]], final=True)
            elif kind == "scb":
                for j in range(4):
                    p1 = gemm_block(slot, j)
                    i = next_st()
                    act_evac(p1, st[i], AF.Copy, b_st[i])
                    kb.dma("sp", scbT[orow + j * 128:orow + (j + 1) * 128, :], st[i], reads=[b_st[i]], final=True)
            elif kind == "gate":
                for j in range(4):
                    p1 = gemm_block(slot, j)
                    i = cnt["ob"] % 3
                    cnt["ob"] += 1
                    act_evac(p1, ob[i], AF.Sigmoid, b_ob[i])
                    kb.dma("sp", sgT[orow + j * 128:orow + (j + 1) * 128, :], ob[i], reads=[b_ob[i]], final=True)
            elif kind == "qk":
                for j in range(4):
                    p1 = gemm_block(slot, j)
                    i = next_st()
                    qraw = st[i]
                    act_evac(p1, qraw, AF.Copy, b_st[i])
                    kb.op("dve", lambda e, qraw=qraw: e.tensor_copy(out=q32b[P32, :], in_=qraw[P32, :]), reads=[b_st[i]], writes=[b_q32])

                    def jmm(e):
                        ins = None
                        for h in range(2):
                            ins = e.matmul(kb.psum[6 + h][0:32, :], jm[P32, :], q32b[P32, h * 512:(h + 1) * 512], start=True, stop=True)
                        return ins
                    kb.op("pe", jmm, reads=[b_q32, b_const], writes=[kb.psb[6], kb.psb[7]])
                    kb.op("dve", lambda e, qraw=qraw: e.tensor_tensor(out=t1[P32, :], in0=qraw[P32, :], in1=cosT[P32, :], op=ALU.mult),
                          reads=[b_st[i], b_tab], writes=[b_t1])
                    for h in range(2):
                        kb.op("dve", lambda e, h=h: e.tensor_tensor(out=t2[P32, h * 512:(h + 1) * 512], in0=kb.psum[6 + h][0:32, :],
                                                                    in1=sinT[P32, h * 512:(h + 1) * 512], op=ALU.mult),
                              reads=[b_tab], writes=[kb.psb[6 + h], b_t2])
                    io = cnt["ob"] % 3
                    cnt["ob"] += 1
                    kb.op("pool", lambda e, io=io, qraw=qraw: e.tensor_copy(out=ob[io][:, :], in_=qraw[:, :]),
                          reads=[b_st[i]], writes=[b_ob[io]])
                    kb.op("pool", lambda e, io=io: e.tensor_tensor(out=ob[io][P32, :], in0=t1[P32, :], in1=t2[P32, :], op=ALU.add),
                          reads=[b_t1, b_t2], writes=[b_ob[io]])
                    kb.dma("sp", qkT[orow + j * 128:orow + (j + 1) * 128, :], ob[io], reads=[b_ob[io]], final=True)
            elif kind == "v":
                for tt in range(8):
                    pbk = (cnt["blk"] % 3) * 2 + (cnt["v"] % 2)
                    cnt["v"] += 1
                    if cnt["v"] % 2 == 0:
                        cnt["blk"] += 1

                    def mmv(e, tt=tt, pbk=pbk, slot=slot):
                        ins = None
                        for kc in range(32):
                            ins = e.matmul(kb.psum[pbk][:, :], hT[:, kc, tt * 128:(tt + 1) * 128], wsl[slot][:, kc, :],
                                           start=(kc == 0), stop=(kc == 31))
                        return ins
                    kb.op("pe", mmv, reads=[b_w[slot]] + hT_all, writes=[kb.psb[pbk]])
                    io = cnt["ob"] % 3
                    cnt["ob"] += 1
                    vdst = ob[io][:, 0:512]
                    if tt % 2 == 0:
                        kb.op("act", lambda e, pbk=pbk, vdst=vdst: e.activation(out=vdst, in_=kb.psum[pbk][:, :], func=AF.Copy),
                              reads=[], writes=[kb.psb[pbk], b_ob[io]])
                    else:
                        kb.op("dve", lambda e, pbk=pbk, vdst=vdst: e.tensor_copy(out=vdst, in_=kb.psum[pbk][:, :]),
                              reads=[], writes=[kb.psb[pbk], b_ob[io]])
                    kb.dma("sp", vtm[tt * 128:(tt + 1) * 128, orow:orow + 512], vdst, reads=[b_ob[io]], final=True)
            if s + 2 < nslab:
                load_wslab(kb, "pool", wsl[slot], b_w[slot], w_in, slabs[s + 2][1], 32)
        kb.finish()
        with nc.Block() as block:
            kb.replay(block)
    return nc


def rope_consts():
    invf = (np.float32(500000.0) ** (-(np.arange(0, 32, 2, dtype=np.float32)) / np.float32(32))).astype(np.float32)
    invf32 = np.concatenate([invf, invf]).reshape(32, 1).astype(np.float32)
    jm = np.zeros((32, 32), np.float32)
    for i in range(16):
        jm[16 + i, i] = -1.0
        jm[i, 16 + i] = 1.0
    return invf32, jm.astype(NPBF)


def t_layout(v, nchunk):
    return np.ascontiguousarray(v.reshape(nchunk, 128).T)


GROUPS = [
    (1, 1152, 9, 128, 8),
    (4, 1536, 3, 128, 2),
    (16, 3072, 2, 64, 1),
]
VM_OFF = [0, 9, 9 + 12]
VM_COLS = 9 + 12 + 32
DFF = 4 * D


def build_B():
    nc = bass.Bass("TRN2", target_bir_lowering=False)
    dt_in = lambda name, shape, dt: nc.dram_tensor(name, shape, dt, kind="ExternalInput").ap()
    dt_out = lambda name, shape, dt: nc.dram_tensor(name, shape, dt, kind="ExternalOutput").ap()
    x = dt_in("x", [TC, D], F32)
    modT_d = dt_in("modT", [128, 192], F32)
    modv = dt_in("modv", [6, D], F32)
    gT_d = dt_in("gT", [128, 32], F32)
    gfin = dt_in("gfin", [1, D], F32)
    zTe = dt_in("zTe", [SC_W, TC + 2], F32)
    scbT = dt_in("scbT", [SC_W, TC], F32)
    uTe = dt_in("uTe", [CF_W, TC + 30], F32)
    qT = dt_in("qT", [DW_W, TC], BF16)
    kTe = [dt_in(f"kTe{g}", [1024, GROUPS[g][1]], BF16) for g in range(3)]
    ve = [dt_in(f"ve{g}", [GROUPS[g][1], 1024], BF16) for g in range(3)]
    vm_d = dt_in("vm", [128, VM_COLS], F32)
    sgT = dt_in("sgT", [3 * D, TC], BF16)
    cva_d = dt_in("cva", [128, 48], F32)
    cvb_d = dt_in("cvb", [128, 16 * 31], F32)
    cvec_d = dt_in("cvec", [128, 48], F32)
    woa = dt_in("woa", [SC_W, D], F32)
    wob = dt_in("wob", [CF_W, D], F32)
    woc = dt_in("woc", [1024, D], F32)
    wo = dt_in("wo", [D, D], F32)
    w1 = dt_in("w1", [D, DFF], F32)
    w2 = dt_in("w2", [DFF, D], F32)
    mk_d = dt_in("masks", [128, 256], BF16)
    ones_d = dt_in("onesb", [128, 128], BF16)
    onesf_d = dt_in("onesf", [128, 128], F32)
    ident_d = dt_in("ident", [128, 128], BF16)
    xout = dt_out("xout", [TC, D], F32)
    yout = dt_out("yout", [TC, D], F32)
    mergedT = nc.dram_tensor("mergedT", [D, TC], BF16).ap()
    x1 = nc.dram_tensor("x1s", [TC, D], F32).ap()
    aT = nc.dram_tensor("aTs", [DFF, TC], BF16).ap()
    with ExitStack() as es:
        kb = KB(nc, es, arena_kb=200)
        V = kb.view
        KBY = 1024
        off = 180 * KBY
        modT = V(off, 768, F32); off += 768
        gT = V(off, 128, F32); off += 128
        a_t = V(off, 128, F32); off += 128
        ss = V(off, 128, F32); off += 128
        ident = V(off, 256, BF16); off += 256
        onesb = V(off, 256, BF16); off += 256
        onesf = V(off, 512, F32); off += 512
        masks = V(off, 512, BF16); off += 512
        vm = V(off, 224, F32); off += 224
        cva = V(off, 192, F32); off += 192
        cvb = V(off, 16 * 31 * 4, F32); off += 16 * 31 * 4
        cvec = V(off, 192, F32); off += 192
        assert off <= 200 * KBY
        b_const = Buf()
        for dst, src in ((modT, modT_d), (gT, gT_d), (ident, ident_d), (onesb, ones_d), (onesf, onesf_d), (masks, mk_d),
                         (vm[:, 0:VM_COLS], vm_d), (cva, cva_d), (cvb, cvb_d), (cvec, cvec_d)):
            kb.dma("sp", dst, src, writes=[b_const])
        kb.barrier()

        AinT = V(0, 32 * KBY, BF16, "p (c n) -> p c n", c=16)
        ufT = V(32 * KBY, 32 * KBY, BF16, "p (c n) -> p c n", c=16)
        vb = V(64 * KBY, 64 * KBY, F32, "p (c n) -> p c n", c=16)
        o = 128 * KBY
        ze = [V(o + i * 4224, 4224, F32) for i in range(2)]; o += 2 * 4224
        sb = [V(o + i * 4096, 4096, F32) for i in range(2)]; o += 2 * 4096
        tt_ = [V(o + i * 4096, 4096, F32) for i in range(2)]; o += 2 * 4096
        ue = [V(o + i * 4224, 4224, F32) for i in range(2)]; o += 2 * 4224
        sq = [V(o + i * 4096, 4096, F32) for i in range(2)]; o += 2 * 4096
        mean_bc = V(o, 4096, F32); o += 4096
        rstd_bc = V(o, 4096, F32); o += 4096
        assert o <= 180 * KBY
        b_ze, b_sb, b_t, b_ue, b_sq = ([Buf(), Buf()] for _ in range(5))
        b_Ain, b_uf, b_vb = Buf(), Buf(), [Buf() for _ in range(16)]
        for ch in range(16):
            s = ch % 2
            kb.dma("sp", ze[s][:, 0:TC + 2], zTe[ch * 128:(ch + 1) * 128, :], writes=[b_ze[s]])
            kb.dma("sp", sb[s], scbT[ch * 128:(ch + 1) * 128, :], writes=[b_sb[s]])
            kb.op("dve", lambda e, s=s, ch=ch: e.tensor_scalar(out=tt_[s], in0=ze[s][:, 2:TC + 2], scalar1=cva[:, ch * 3 + 2:ch * 3 + 3],
                                                              scalar2=None, op0=ALU.mult), reads=[b_ze[s], b_const], writes=[b_t[s]])
            for k in (1, 0):
                kb.op("dve", lambda e, s=s, ch=ch, k=k: e.scalar_tensor_tensor(
                    out=tt_[s], in0=ze[s][:, k:k + TC], scalar=cva[:, ch * 3 + k:ch * 3 + k + 1], in1=tt_[s],
                    op0=ALU.mult, op1=ALU.add), reads=[b_ze[s], b_t[s], b_const], writes=[b_t[s]])
            kb.op("pool", lambda e, s=s, ch=ch: e.tensor_tensor(out=AinT[:, ch, :], in0=tt_[s], in1=sb[s], op=ALU.mult),
                  reads=[b_t[s], b_sb[s]], writes=[b_Ain])
        for ch0 in range(0, 16, 2):
            pair = (ch0, ch0 + 1)
            for ch in pair:
                s = ch % 2
                kb.dma("sp", ue[s][:, 0:TC + 30], uTe[ch * 128:(ch + 1) * 128, :], writes=[b_ue[s]])
                kb.op("dve", lambda e, s=s, ch=ch: e.tensor_scalar(out=vb[:, ch, :], in0=ue[s][:, 30:TC + 30],
                                                                  scalar1=cvb[:, ch * 31 + 30:ch * 31 + 31], scalar2=cvec[:, ch:ch + 1],
                                                                  op0=ALU.mult, op1=ALU.add), reads=[b_ue[s], b_const], writes=[b_vb[ch]])
            for k in range(30):
                for ch in pair:
                    s = ch % 2
                    kb.op("dve", lambda e, s=s, ch=ch, k=k: e.scalar_tensor_tensor(
                        out=vb[:, ch, :], in0=ue[s][:, k:k + TC], scalar=cvb[:, ch * 31 + k:ch * 31 + k + 1], in1=vb[:, ch, :],
                        op0=ALU.mult, op1=ALU.add), reads=[b_ue[s], b_vb[ch], b_const], writes=[b_vb[ch]])
            for ch in pair:
                s = ch % 2
                kb.op("act", lambda e, s=s, ch=ch: e.activation(out=sq[s], in_=vb[:, ch, :], func=AF.Square),
                      reads=[b_vb[ch]], writes=[b_sq[s]])

                def stat(e, s=s, ch=ch):
                    ins = None
                    for h in range(2):
                        ins = e.matmul(kb.psum[h][:, :], onesf, vb[:, ch, h * 512:(h + 1) * 512], start=(ch == 0), stop=(ch == 15))
                        ins = e.matmul(kb.psum[2 + h][:, :], onesf, sq[s][:, h * 512:(h + 1) * 512], start=(ch == 0), stop=(ch == 15))
                    return ins
                kb.op("pe", stat, reads=[b_vb[ch], b_sq[s], b_const], writes=[kb.psb[0], kb.psb[1], kb.psb[2], kb.psb[3]])
        b_stat = Buf()
        for h in range(2):
            hs = slice(h * 512, (h + 1) * 512)
            kb.op("dve", lambda e, h=h, hs=hs: e.tensor_scalar(out=mean_bc[:, hs], in0=kb.psum[h][:, :], scalar1=1.0 / CF_W, scalar2=None,
                                                              op0=ALU.mult), writes=[kb.psb[h], b_stat])
            kb.op("dve", lambda e, hs=hs: e.tensor_tensor(out=rstd_bc[:, hs], in0=mean_bc[:, hs], in1=mean_bc[:, hs], op=ALU.mult),
                  reads=[b_stat], writes=[b_stat])
            kb.op("dve", lambda e, h=h, hs=hs: e.scalar_tensor_tensor(out=rstd_bc[:, hs], in0=kb.psum[2 + h][:, :], scalar=1.0 / CF_W,
                                                                     in1=rstd_bc[:, hs], op0=ALU.mult, op1=ALU.subtract),
                  reads=[b_stat], writes=[kb.psb[2 + h], b_stat])
        kb.op("dve", lambda e: e.tensor_scalar(out=rstd_bc, in0=rstd_bc, scalar1=EPS, scalar2=None, op0=ALU.add),
              reads=[b_stat], writes=[b_stat])
        kb.op("act", lambda e: e.activation(out=rstd_bc, in_=rstd_bc, func=AF.Sqrt), reads=[b_stat], writes=[b_stat])
        kb.op("dve", lambda e: e.reciprocal(out=rstd_bc, in_=rstd_bc), reads=[b_stat], writes=[b_stat])
        for ch in range(16):
            s = ch % 2
            kb.op("dve", lambda e, s=s, ch=ch: e.tensor_tensor(out=tt_[s], in0=vb[:, ch, :], in1=mean_bc, op=ALU.subtract),
                  reads=[b_vb[ch], b_stat], writes=[b_t[s]])
            kb.op("pool", lambda e, s=s: e.tensor_tensor(out=tt_[s], in0=tt_[s], in1=rstd_bc, op=ALU.mult),
                  reads=[b_t[s], b_stat], writes=[b_t[s]])
            kb.op("act", lambda e, s=s, ch=ch: e.activation(out=ufT[:, ch, :], in_=tt_[s], func=AF.Silu,
                                                           bias=cvec[:, 32 + ch:33 + ch], scale=cvec[:, 16 + ch:17 + ch]),
                  reads=[b_t[s], b_const], writes=[b_uf])
        kb.barrier()

        oT = V(64 * KBY, 16 * KBY, BF16, "p (c n) -> p c n", c=8)
        o = 80 * KBY
        accn = V(o, 4096, F32); o += 4096
        accd = V(o, 4096, F32); o += 4096
        KTt = [V(o + i * 6144, 6144, BF16) for i in range(2)]; o += 2 * 6144
        QTt = [V(o + i * 2048, 2048, BF16) for i in range(2)]; o += 2 * 2048
        Vtt = [V(o + i * 8192, 8192, BF16) for i in range(2)]; o += 2 * 8192
        Pe = [V(o + i * 512, 512, BF16, "p (a n) -> p a n", a=2) for i in range(2)]; o += 1024
        Pm = [V(o + i * 512, 512, BF16, "p (a n) -> p a n", a=2) for i in range(2)]; o += 1024
        assert o <= 128 * KBY
        b_acc, b_oT = Buf(), Buf()
        b_KT, b_QT, b_Vt, b_Pe, b_Pm = ([Buf(), Buf()] for _ in range(5))
        SCALE = float(128 ** -0.5)
        it = 0
        hg = 0
        for h in range(8):
            hc = slice(h * 128, (h + 1) * 128)
            for g in range(3):
                d, Lg, nkb, QB, nqb = GROUPS[g]
                s = hg % 2
                hg += 1
                KT, QTs = KTt[s], QTt[s]
                Vt = Vtt[s][:, 0:d * nkb * 128].rearrange("p (r k c) -> p r k c", r=d, k=nkb)
                kb.dma("sp", KT[:, 0:Lg], kTe[g][hc, :], writes=[b_KT[s]])
                kb.dma("sp", QTs, qT[(g * 8 + h) * 128:(g * 8 + h + 1) * 128, :], writes=[b_QT[s]])
                if g == 0:
                    kb.dma("sp", Vt[:, 0, :, :], ve[0].rearrange("(k m) c -> m k c", m=128)[:, :, hc], writes=[b_Vt[s]])
                elif g == 1:
                    for k in range(3):
                        kb.dma("sp", Vt[:, :, k, :], ve[1][k * 512:(k + 1) * 512, :].rearrange("(m r) c -> m r c", r=4)[:, :, hc],
                               writes=[b_Vt[s]])
                else:
                    kb.dma("sp", Vt[:, :, 0, :], ve[2][0:2048, :].rearrange("(m r) c -> m r c", r=16)[:, :, hc], writes=[b_Vt[s]])
                    kb.dma("sp", Vt[0:64, :, 1, :], ve[2][2048:3072, :].rearrange("(m r) c -> m r c", r=16)[:, :, hc], writes=[b_Vt[s]])
                for r in range(d):
                    for qb in range(nqb):
                        i0 = qb * QB
                        q0 = r + d * i0
                        qsl = QTs[:, q0:q0 + d * (QB - 1) + 1:d] if d > 1 else QTs[:, q0:q0 + QB]
                        kA, kB_ = qb, qb + 1
                        nB = 128 if g < 2 else 64
                        eA = r + d * 128 * kA
                        eB = r + d * 128 * kB_
                        kslA = KT[:, eA:eA + d * 127 + 1:d] if d > 1 else KT[:, eA:eA + 128]
                        kslB = KT[:, eB:eB + d * (nB - 1) + 1:d] if d > 1 else KT[:, eB:eB + nB]
                        p = it % 2
                        it += 1
                        bS, bN, bD = p, 2 + p, 4 + p
                        psS = kb.psum[bS][:, 0:256].rearrange("p (a n) -> p a n", a=2)

                        def qk(e, kslA=kslA, kslB=kslB, qsl=qsl, psS=psS, nB=nB, QB=QB):
                            e.matmul(psS[:, 0, 0:QB], kslA, qsl, start=True, stop=True)
                            return e.matmul(psS[0:nB, 1, 0:QB], kslB, qsl, start=True, stop=True)
                        kb.op("pe", qk, reads=[b_KT[s], b_QT[s]], writes=[kb.psb[bS]])
                        pe_, pm_ = Pe[p], Pm[p]

                        def ex(e, psS=psS, pe_=pe_, nB=nB, QB=QB):
                            e.activation(out=pe_[:, 0, 0:QB], in_=psS[:, 0, 0:QB], func=AF.Exp, scale=SCALE)
                            return e.activation(out=pe_[0:nB, 1, 0:QB], in_=psS[0:nB, 1, 0:QB], func=AF.Exp, scale=SCALE)
                        kb.op("act", ex, writes=[kb.psb[bS], b_Pe[p]])
                        vc = VM_OFF[g] + r * nkb

                        def mk(e, pe_=pe_, pm_=pm_, nB=nB, QB=QB, vc=vc, kA=kA, kB_=kB_):
                            e.scalar_tensor_tensor(out=pm_[:, 0, 0:QB], in0=pe_[:, 0, 0:QB], scalar=vm[:, vc + kA:vc + kA + 1],
                                                   in1=masks[:, 0:QB], op0=ALU.mult, op1=ALU.mult)
                            return e.scalar_tensor_tensor(out=pm_[0:nB, 1, 0:QB], in0=pe_[0:nB, 1, 0:QB],
                                                          scalar=vm[0:nB, vc + kB_:vc + kB_ + 1], in1=masks[0:nB, 128:128 + QB],
                                                          op0=ALU.mult, op1=ALU.mult)
                        kb.op("dve", mk, reads=[b_Pe[p], b_const], writes=[b_Pm[p]])

                        def pv(e, pm_=pm_, Vt=Vt, r=r, kA=kA, kB_=kB_, nB=nB, QB=QB, bN=bN, bD=bD):
                            e.matmul(kb.psum[bN][:, 0:QB], Vt[:, r, kA, :], pm_[:, 0, 0:QB], start=True, stop=False)
                            e.matmul(kb.psum[bN][:, 0:QB], Vt[0:nB, r, kB_, :], pm_[0:nB, 1, 0:QB], start=False, stop=True)
                            e.matmul(kb.psum[bD][:, 0:QB], onesb[:, :], pm_[:, 0, 0:QB], start=True, stop=False)
                            return e.matmul(kb.psum[bD][:, 0:QB], onesb[0:nB, :], pm_[0:nB, 1, 0:QB], start=False, stop=True)
                        kb.op("pe", pv, reads=[b_Pm[p], b_Vt[s], b_const], writes=[kb.psb[bN], kb.psb[bD]])
                        tsl = slice(q0, q0 + d * (QB - 1) + 1, d) if d > 1 else slice(q0, q0 + QB)
                        if g == 0:
                            def ac(e, tsl=tsl, bN=bN, bD=bD, QB=QB):
                                e.tensor_copy(out=accn[:, tsl], in_=kb.psum[bN][:, 0:QB])
                                return e.tensor_copy(out=accd[:, tsl], in_=kb.psum[bD][:, 0:QB])
                        else:
                            def ac(e, tsl=tsl, bN=bN, bD=bD, QB=QB):
                                e.tensor_tensor(out=accn[:, tsl], in0=kb.psum[bN][:, 0:QB], in1=accn[:, tsl], op=ALU.add)
                                return e.tensor_tensor(out=accd[:, tsl], in0=kb.psum[bD][:, 0:QB], in1=accd[:, tsl], op=ALU.add)
                        kb.op("dve", ac, reads=[b_acc], writes=[kb.psb[bN], kb.psb[bD], b_acc])
            kb.op("dve", lambda e: e.reciprocal(out=accd, in_=accd), reads=[b_acc], writes=[b_acc])
            kb.op("pool", lambda e, h=h: e.tensor_tensor(out=oT[:, h, :], in0=accn, in1=accd, op=ALU.mult),
                  reads=[b_acc], writes=[b_oT])
        kb.barrier()

        o = 80 * KBY
        sgt = [[V(o + (i * 3 + f) * 1024, 1024, BF16) for f in range(3)] for i in range(2)]; o += 6 * 1024
        mt = [[V(o + (i * 3 + f) * 2048, 2048, F32) for f in range(3)] for i in range(2)]; o += 6 * 2048
        mo = [V(o + i * 1024, 1024, BF16) for i in range(2)]; o += 2 * 1024
        assert o <= 100 * KBY
        wbase = [100 * KBY, 140 * KBY]
        wsA = [V(wbase[i], 16384, BF16, "p (k n) -> p k n", k=16) for i in range(2)]
        wsB = [V(wbase[i] + 16384, 16384, BF16, "p (k n) -> p k n", k=16) for i in range(2)]
        wsC = [V(wbase[i] + 32768, 8192, BF16, "p (k n) -> p k n", k=8) for i in range(2)]
        b_ws = [Buf(), Buf()]
        b_sg, b_mt, b_mo = ([Buf(), Buf()] for _ in range(3))

        def load3(cs):
            sl = cs % 2
            c0 = cs * 512
            load_wslab(kb, "pool", wsA[sl], b_ws[sl], woa, [(c0, 512)], 16)
            load_wslab(kb, "pool", wsB[sl], b_ws[sl], wob, [(c0, 512)], 16)
            load_wslab(kb, "pool", wsC[sl], b_ws[sl], woc, [(c0, 512)], 8)
        load3(0)
        load3(1)
        it = 0
        for cs in range(8):
            sl = cs % 2
            for j in range(4):
                crow = cs * 512 + j * 128
                for h in range(2):
                    hs = slice(h * 512, (h + 1) * 512)
                    p = it % 2
                    it += 1
                    bk = p * 3

                    def mm3(e, sl=sl, j=j, hs=hs, bk=bk):
                        ins = None
                        for kc in range(16):
                            ins = e.matmul(kb.psum[bk][:, :], wsA[sl][:, kc, j * 128:(j + 1) * 128], AinT[:, kc, hs],
                                           start=(kc == 0), stop=(kc == 15))
                        for kc in range(16):
                            ins = e.matmul(kb.psum[bk + 1][:, :], wsB[sl][:, kc, j * 128:(j + 1) * 128], ufT[:, kc, hs],
                                           start=(kc == 0), stop=(kc == 15))
                        for kc in range(8):
                            ins = e.matmul(kb.psum[bk + 2][:, :], wsC[sl][:, kc, j * 128:(j + 1) * 128], oT[:, kc, hs],
                                           start=(kc == 0), stop=(kc == 7))
                        return ins
                    kb.op("pe", mm3, reads=[b_ws[sl], b_Ain, b_uf, b_oT], writes=[kb.psb[bk], kb.psb[bk + 1], kb.psb[bk + 2]])
                    for f in range(3):
                        kb.dma("sp", sgt[p][f], sgT[f * D + crow:f * D + crow + 128, hs], writes=[b_sg[p]])
                    for f in range(3):
                        kb.op("dve", lambda e, p=p, f=f, bk=bk: e.tensor_tensor(out=mt[p][f], in0=kb.psum[bk + f][:, :], in1=sgt[p][f],
                                                                                op=ALU.mult),
                              reads=[b_sg[p]], writes=[kb.psb[bk + f], b_mt[p]])
                    kb.op("pool", lambda e, p=p: e.tensor_tensor(out=mt[p][0], in0=mt[p][0], in1=mt[p][1], op=ALU.add),
                          reads=[b_mt[p]], writes=[b_mt[p]])
                    kb.op("pool", lambda e, p=p: e.tensor_tensor(out=mo[p], in0=mt[p][0], in1=mt[p][2], op=ALU.add),
                          reads=[b_mt[p]], writes=[b_mo[p]])
                    kb.dma("act", mergedT[crow:crow + 128, hs], mo[p], reads=[b_mo[p]])
            if cs + 2 < 8:
                load3(cs + 2)
        kb.barrier()

        def tokmajor_residual(mTres, b_mT, wsrc, jgate, xin_d, xo_d, final):
            o = 64 * KBY
            wsl = [V(o + i * 32 * KBY, 32 * KBY, BF16, "p (k n) -> p k n", k=32) for i in range(2)]
            o = 128 * KBY
            gbc = V(o, 16384, F32); o += 16384
            xi = [V(o + i * 2048, 2048, F32) for i in range(3)]; o += 3 * 2048
            tm = [V(o + i * 2048, 2048, F32) for i in range(3)]; o += 3 * 2048
            b_g, b_w = Buf(), [Buf(), Buf()]
            b_xi, b_tm = [Buf() for _ in range(3)], [Buf() for _ in range(3)]
            kb.dma("sp", gbc, modv[jgate:jgate + 1, :].partition_broadcast(128), writes=[b_g])
            for s in range(2):
                load_wslab(kb, "pool", wsl[s], b_w[s], wsrc, [(s * 512, 512)], 32)
            it = 0
            for cs in range(8):
                sl = cs % 2
                cols = slice(cs * 512, (cs + 1) * 512)
                for tt in range(8):
                    rows = slice(tt * 128, (tt + 1) * 128)
                    pb = it % 8
                    q = it % 3
                    it += 1

                    def mm(e, sl=sl, rows=rows, pb=pb):
                        ins = None
                        for kc in range(32):
                            ins = e.matmul(kb.psum[pb][:, :], mTres[:, kc, rows], wsl[sl][:, kc, :], start=(kc == 0), stop=(kc == 31))
                        return ins
                    kb.op("pe", mm, reads=[b_w[sl], b_mT], writes=[kb.psb[pb]])
                    kb.dma("sp", xi[q], xin_d[rows, cols], writes=[b_xi[q]])
                    kb.op("dve", lambda e, pb=pb, q=q, cols=cols: e.tensor_tensor(out=tm[q], in0=kb.psum[pb][:, :], in1=gbc[:, cols], op=ALU.mult),
                          reads=[b_g], writes=[kb.psb[pb], b_tm[q]])
                    kb.op("pool", lambda e, q=q: e.tensor_tensor(out=tm[q], in0=tm[q], in1=xi[q], op=ALU.add),
                          reads=[b_xi[q], b_tm[q]], writes=[b_tm[q]])
                    kb.dma("act", xo_d[rows, cols], tm[q], reads=[b_tm[q]], final=final)
                if cs + 2 < 8:
                    load_wslab(kb, "pool", wsl[sl], b_w[sl], wsrc, [((cs + 2) * 512, 512)], 32)

        mT = V(0, 64 * KBY, BF16, "p (c n) -> p c n", c=32)
        b_mT = Buf()
        for kg in range(4):
            kb.dma("sp", mT[:, kg * 8:(kg + 1) * 8, :], mergedT[kg * 1024:(kg + 1) * 1024, :].rearrange("(k p) n -> p k n", p=128),
                   writes=[b_mT])
        tokmajor_residual(mT, b_mT, wo, 2, x, x1, False)
        kb.barrier()

        hT = V(0, 64 * KBY, BF16, "p (c n) -> p c n", c=32)
        o = 128 * KBY
        xt = [V(o + i * 16384, 16384, F32) for i in range(2)]
        xn = [V(o + 32768 + i * 8192, 8192, BF16) for i in range(2)]
        b_hT = [[Buf(), Buf()] for _ in range(8)]
        hT_all = [b for pp in b_hT for b in pp]
        tagbufs = ([Buf(), Buf()], [Buf(), Buf()], [Buf() for _ in range(8)])
        emit_norm_T(kb, x1, hT, b_hT, modT, 3, 4, gT, xt, xn, ss, {"a": a_t, "sh": None}, ident, b_const, tagbufs)
        kb.barrier()

        o = 64 * KBY
        wsl = [V(o + i * 32 * KBY, 32 * KBY, BF16, "p (k n) -> p k n", k=32) for i in range(2)]
        o = 128 * KBY
        rl = [V(o + i * 2048, 2048, F32) for i in range(4)]; o += 4 * 2048
        ao = [V(o + i * 1024, 1024, BF16) for i in range(4)]; o += 4 * 1024
        b_w, b_rl, b_ao = [Buf(), Buf()], [Buf() for _ in range(4)], [Buf() for _ in range(4)]
        for s in range(2):
            load_wslab(kb, "pool", wsl[s], b_w[s], w1, [(s * 512, 512)], 32)
        it = 0
        for cs in range(DFF // 512):
            sl = cs % 2
            for bi in range(4):
                pbk = (it % 4) * 2
                it += 1

                def mm(e, sl=sl, bi=bi, pbk=pbk):
                    ins = None
                    for kc in range(32):
                        for h in range(2):
                            ins = e.matmul(kb.psum[pbk + h][:, :], wsl[sl][:, kc, bi * 128:(bi + 1) * 128],
                                           hT[:, kc, h * 512:(h + 1) * 512], start=(kc == 0), stop=(kc == 31))
                    return ins
                kb.op("pe", mm, reads=[b_w[sl]] + hT_all, writes=[kb.psb[pbk], kb.psb[pbk + 1]])
                frow = cs * 512 + bi * 128
                for h in range(2):
                    q = (it * 2 + h) % 4
                    kb.op("act", lambda e, q=q, pbk=pbk, h=h: e.activation(out=rl[q], in_=kb.psum[pbk + h][:, :], func=AF.Relu),
                          writes=[kb.psb[pbk + h], b_rl[q]])
                    kb.op("pool", lambda e, q=q: e.tensor_tensor(out=ao[q], in0=rl[q], in1=rl[q], op=ALU.mult),
                          reads=[b_rl[q]], writes=[b_ao[q]])
                    kb.dma("sp", aT[frow:frow + 128, h * 512:(h + 1) * 512], ao[q], reads=[b_ao[q]])
            if cs + 2 < DFF // 512:
                load_wslab(kb, "pool", wsl[sl], b_w[sl], w1, [((cs + 2) * 512, 512)], 32)
        kb.barrier()

        o = 0
        w2s = [V(o + i * 8192, 8192, BF16, "p (k n) -> p k n", k=8) for i in range(3)]; o += 3 * 8192
        ag = [V(o + i * 16384, 16384, BF16, "p (k n) -> p k n", k=8) for i in range(3)]; o += 3 * 16384
        gbc = V(o, 16384, F32); o += 16384
        xi = [V(o + i * 2048, 2048, F32) for i in range(8)]; o += 8 * 2048
        tm = [V(o + i * 2048, 2048, F32) for i in range(4)]; o += 4 * 2048
        b_g, b_w2, b_ag = Buf(), [Buf() for _ in range(3)], [Buf() for _ in range(3)]
        b_xi, b_tm = [Buf() for _ in range(8)], [Buf() for _ in range(4)]
        kb.dma("sp", gbc, modv[5:6, :].partition_broadcast(128), writes=[b_g])
        NKG = DFF // 1024
        steps = [(cs, kg) for cs in range(8) for kg in range(NKG)]

        def issue_loads(i):
            cs, kg = steps[i]
            s3 = i % 3
            kb.dma("pool", w2s[s3], w2[kg * 1024:(kg + 1) * 1024, cs * 512:(cs + 1) * 512].rearrange("(k p) n -> p k n", p=128),
                   writes=[b_w2[s3]])
            kb.dma("sp", ag[s3], aT[kg * 1024:(kg + 1) * 1024, :].rearrange("(k p) n -> p k n", p=128), writes=[b_ag[s3]])
        for i in range(3):
            issue_loads(i)
        it2 = 0
        for i, (cs, kg) in enumerate(steps):
            cols = slice(cs * 512, (cs + 1) * 512)
            s3 = i % 3
            if kg == 0:
                for tt in range(8):
                    kb.dma("sp", xi[tt], x1[tt * 128:(tt + 1) * 128, cols], writes=[b_xi[tt]])
            for tt in range(8):
                def mm(e, s3=s3, kg=kg, tt=tt):
                    ins = None
                    for k in range(8):
                        ins = e.matmul(kb.psum[tt][:, :], ag[s3][:, k, tt * 128:(tt + 1) * 128], w2s[s3][:, k, :],
                                       start=(kg == 0 and k == 0), stop=(kg == NKG - 1 and k == 7))
                    return ins
                kb.op("pe", mm, reads=[b_w2[s3], b_ag[s3]], writes=[kb.psb[tt]])
            if i + 3 < len(steps):
                issue_loads(i + 3)
            if kg == NKG - 1:
                for tt in range(8):
                    rows = slice(tt * 128, (tt + 1) * 128)
                    q = it2 % 4
                    it2 += 1
                    kb.op("dve", lambda e, tt=tt, q=q, cols=cols: e.tensor_tensor(out=tm[q], in0=kb.psum[tt][:, :], in1=gbc[:, cols], op=ALU.mult),
                          reads=[b_g], writes=[kb.psb[tt], b_tm[q]])
                    kb.op("pool", lambda e, q=q, tt=tt: e.tensor_tensor(out=tm[q], in0=tm[q], in1=xi[tt], op=ALU.add),
                          reads=[b_xi[tt], b_tm[q]], writes=[b_tm[q]])
                    kb.dma("act", xout[rows, cols], tm[q], reads=[b_tm[q]], final=True)
        kb.barrier()

        o = 0
        gfb = V(o, 16384, F32); o += 16384
        xt = [V(o + i * 16384, 16384, F32) for i in range(2)]; o += 2 * 16384
        yt = [V(o + i * 16384, 16384, F32) for i in range(2)]; o += 2 * 16384
        b_gf, b_xt, b_yt, b_s2 = Buf(), [Buf(), Buf()], [Buf(), Buf()], [Buf() for _ in range(8)]
        kb.dma("sp", gfb, gfin.partition_broadcast(128), writes=[b_gf])
        for tt in range(8):
            s = tt % 2
            rows = slice(tt * 128, (tt + 1) * 128)
            kb.dma("sp", xt[s], xout[rows, :], writes=[b_xt[s]])
            kb.op("act", lambda e, s=s, tt=tt: e.activation(out=yt[s], in_=xt[s], func=AF.Square, accum_out=ss[:, tt:tt + 1]),
                  reads=[b_xt[s]], writes=[b_yt[s], b_s2[tt]])
            kb.op("dve", lambda e, tt=tt: e.tensor_scalar(out=ss[:, 8 + tt:9 + tt], in0=ss[:, tt:tt + 1], scalar1=1.0 / D, scalar2=EPS,
                                                         op0=ALU.mult, op1=ALU.add), reads=[b_s2[tt]], writes=[b_s2[tt]])
            kb.op("act", lambda e, tt=tt: e.activation(out=ss[:, 16 + tt:17 + tt], in_=ss[:, 8 + tt:9 + tt], func=AF.Sqrt),
                  reads=[b_s2[tt]], writes=[b_s2[tt]])
            kb.op("dve", lambda e, tt=tt: e.reciprocal(out=ss[:, 24 + tt:25 + tt], in_=ss[:, 16 + tt:17 + tt]),
                  reads=[b_s2[tt]], writes=[b_s2[tt]])
            kb.op("dve", lambda e, s=s, tt=tt: e.scalar_tensor_tensor(out=yt[s], in0=xt[s], scalar=ss[:, 24 + tt:25 + tt], in1=gfb,
                                                                     op0=ALU.mult, op1=ALU.mult),
                  reads=[b_xt[s], b_s2[tt], b_gf], writes=[b_yt[s]])
            kb.dma("sp", yout[rows, :], yt[s], reads=[b_yt[s]], final=True)
        kb.finish()
        with nc.Block() as block:
            kb.replay(block)
    return nc


_PROGS = {}


def _prog(name):
    if name not in _PROGS:
        if name == "A":
            _PROGS[name] = build_A(full_slabs_A(), IN_COLS)
        else:
            _PROGS[name] = build_B()
    return _PROGS[name]


def run_A(xcur, modl, g_mix_l, w_in_l, pos):
    nc = _prog("A")
    invf32, jm = rope_consts()
    modT = np.concatenate([t_layout(modl[j * D:(j + 1) * D], 32) for j in range(6)], axis=1)
    gT = t_layout(g_mix_l, 32)
    ident = np.eye(128, dtype=np.float32).astype(NPBF)
    w = np.ascontiguousarray(w_in_l)
    in_maps = []
    for c in range(NCORE):
        in_maps.append({"x": np.ascontiguousarray(xcur[c * TC:(c + 1) * TC]), "modT": modT, "gT": gT, "w_in": w,
                        "pos": np.ascontiguousarray(pos[c * TC:(c + 1) * TC]).reshape(1, TC).astype(np.int32),
                        "invf": invf32, "jm": jm, "ident": ident})
    res = run_bass_kernel_spmd(nc, in_maps, core_ids=list(range(NCORE)))
    return res.results


def glue_B(A_res):
    zT = np.concatenate([r["zT"] for r in A_res], axis=1)
    uT = np.concatenate([r["uT"] for r in A_res], axis=1)
    kT = np.concatenate([r["qkT"][DW_W:] for r in A_res], axis=1)
    v = np.concatenate([r["vtm"] for r in A_res], axis=0)
    PAD = 2048
    zpad = np.concatenate([np.zeros((SC_W, PAD), zT.dtype), zT], axis=1)
    upad = np.concatenate([np.zeros((CF_W, PAD), uT.dtype), uT], axis=1)
    kpad = np.concatenate([np.zeros((DW_W, PAD), kT.dtype), kT], axis=1)
    vpad = np.concatenate([np.zeros((PAD, DW_W), v.dtype), v], axis=0)
    out = []
    for c in range(NCORE):
        t0 = PAD + c * TC
        m = {"zTe": np.ascontiguousarray(zpad[:, t0 - 2:t0 + TC]),
             "uTe": np.ascontiguousarray(upad[:, t0 - 30:t0 + TC]),
             "scbT": A_res[c]["scbT"], "qT": np.ascontiguousarray(A_res[c]["qkT"][:DW_W]), "sgT": A_res[c]["sgT"]}
        vm = np.zeros((128, VM_COLS), np.float32)
        for g, (d, Lg, nkb, QB, nqb) in enumerate(GROUPS):
            W = 128 * d
            m[f"kTe{g}"] = np.ascontiguousarray(kpad[g * 1024:(g + 1) * 1024, t0 - W:t0 + TC])
            m[f"ve{g}"] = np.ascontiguousarray(vpad[t0 - W:t0 + TC, g * 1024:(g + 1) * 1024])
            for r in range(d):
                for kbi in range(nkb):
                    for mm in range(128):
                        e = r + d * (128 * kbi + mm)
                        if e < Lg and (c * TC - W + e) >= 0:
                            vm[mm, VM_OFF[g] + r * nkb + kbi] = 1.0
        m["vm"] = vm
        out.append(m)
    return out


def run_B(xcur, modl, glue, P, l):
    nc = _prog("B")
    modT = np.concatenate([t_layout(modl[j * D:(j + 1) * D], 32) for j in range(6)], axis=1)
    modv = np.ascontiguousarray(modl.reshape(6, D))
    gT = t_layout(P["g_mlp"][l], 32)
    gfin = np.ascontiguousarray(P["g_final"]).reshape(1, D)
    ca, cb = P["conv_a"][l], P["conv_b"][l]
    cva = np.ascontiguousarray(ca.reshape(3, 16, 128).transpose(2, 1, 0)).reshape(128, 48)
    cvb = np.ascontiguousarray(cb.reshape(31, 16, 128).transpose(2, 1, 0)).reshape(128, 16 * 31)
    cvec = np.concatenate([t_layout(P["conv_b_bias"][l], 16), t_layout(P["ln_cf_g"][l], 16), t_layout(P["ln_cf_b"][l], 16)], axis=1)
    mm_, ii_ = np.meshgrid(np.arange(128), np.arange(128), indexing="ij")
    masks = np.concatenate([(mm_ >= ii_), (mm_ <= ii_)], axis=1).astype(np.float32).astype(NPBF)
    shared = {"modT": modT, "modv": modv, "gT": gT, "gfin": gfin, "cva": cva, "cvb": cvb, "cvec": np.ascontiguousarray(cvec),
              "woa": np.ascontiguousarray(P["w_out_a"][l]), "wob": np.ascontiguousarray(P["w_out_b"][l]),
              "woc": np.ascontiguousarray(P["w_out_c"][l]), "wo": np.ascontiguousarray(P["w_o"][l]),
              "w1": np.ascontiguousarray(P["w_mlp1"][l]), "w2": np.ascontiguousarray(P["w_mlp2"][l]),
              "masks": masks, "onesb": np.ones((128, 128), np.float32).astype(NPBF), "onesf": np.ones((128, 128), np.float32),
              "ident": np.eye(128, dtype=np.float32).astype(NPBF)}
    in_maps = []
    for c in range(NCORE):
        m = dict(shared)
        m.update(glue[c])
        m["x"] = np.ascontiguousarray(xcur[c * TC:(c + 1) * TC])
        in_maps.append(m)
    res = run_bass_kernel_spmd(nc, in_maps, core_ids=list(range(NCORE)))
    xo = np.concatenate([r["xout"] for r in res.results], axis=0)
    yo = np.concatenate([r["yout"] for r in res.results], axis=0)
    return xo, yo


def kernel(**inp):
    P = {k: np.asarray(v) for k, v in inp.items()}
    x = P["x"][0].astype(np.float32)
    pos = P["positions"][0]
    mod = run_ada(P["c"].astype(np.float32), P["w_ada"], P["b_ada"])
    xcur = x
    yo = None
    for l in range(2):
        A_res = run_A(xcur, mod[l], P["g_mix"][l], P["w_in"][l], pos)
        glue = glue_B(A_res)
        del A_res
        xcur, yo = run_B(xcur, mod[l], glue, P, l)
    return yo.reshape(1, NCORE * TC, D).astype(np.float32)
```

```python
import numpy as np
import ml_dtypes
import concourse.bass as bass
import concourse.mybir as mybir
from concourse.bass_utils import run_bass_kernel_spmd
from contextlib import ExitStack

F32 = mybir.dt.float32
BF16 = mybir.dt.bfloat16
I32 = mybir.dt.int32
U8 = mybir.dt.uint8
ALU = mybir.AluOpType
AF = mybir.ActivationFunctionType
NPBF = ml_dtypes.bfloat16


class Tok:
    __slots__ = ("sem", "key", "val", "stream")

    def __init__(self, sem, key, val, stream):
        self.sem, self.key, self.val, self.stream = sem, key, val, stream


class Buf:
    __slots__ = ("w", "r", "name")

    def __init__(self, name=""):
        self.w = None
        self.r = {}
        self.name = name


class KB:
    ENG = ("pe", "act", "dve", "pool", "sp")

    def __init__(self, nc, es, arena_kb=200, ndma=(("sp", 14), ("pool", 14), ("act", 6))):
        self.nc = nc
        self.eng = {"pe": nc.tensor, "act": nc.scalar, "dve": nc.vector, "pool": nc.gpsimd, "sp": nc.sync}
        self.streams = {k: [] for k in self.ENG}
        self.sem = {}
        self.cnt = {k: 0 for k in self.ENG}
        self.waited = {k: {} for k in self.ENG}
        for k in self.ENG:
            self.sem[k] = es.enter_context(nc.semaphore("s_" + k))
        self.dpool = {}
        self.dcnt = {}
        self.dnext = {}
        for q, n in ndma:
            self.dpool[q] = []
            for i in range(n):
                s = es.enter_context(nc.semaphore(f"d_{q}{i}"))
                self.dpool[q].append((s, f"d_{q}{i}"))
                self.dcnt[f"d_{q}{i}"] = 0
            self.dnext[q] = 0
        self.arena = es.enter_context(nc.sbuf_tensor("arena", [128, arena_kb * 1024], U8))
        self.arena_bytes = arena_kb * 1024
        self.psum = [es.enter_context(nc.psum_tensor(f"ps{i}", [128, 512], F32)) for i in range(8)]
        self.psb = [Buf(f"psb{i}") for i in range(8)]
        self.final_toks = []
        self.n_instr = 0

    def view(self, off, nbytes, dt, pattern=None, **kw):
        assert off % 32 == 0, off
        assert off + nbytes <= self.arena_bytes, (off, nbytes)
        v = self.arena[:, off:off + nbytes].bitcast(dt)
        if pattern:
            v = v.rearrange(pattern, **kw)
        return v

    def _wait(self, stream, t):
        if self.waited[stream].get(t.key, 0) >= t.val:
            return
        self.waited[stream][t.key] = t.val
        self.streams[stream].append(("wait", t.sem, t.val))

    def _deps(self, stream, reads, writes, is_dma):
        for b in reads:
            if b.w is not None:
                self._wait(stream, b.w)
        for b in writes:
            if b.w is not None:
                if is_dma or b.w.stream != stream:
                    self._wait(stream, b.w)
            for t in b.r.values():
                if is_dma or t.stream != stream:
                    self._wait(stream, t)

    def _mark(self, tok, reads, writes):
        for b in writes:
            b.w = tok
            b.r = {}
        for b in reads:
            if b not in writes:
                b.r[tok.key] = tok

    def op(self, stream, fn, reads=(), writes=()):
        self._deps(stream, reads, writes, False)
        self.cnt[stream] += 1
        tok = Tok(self.sem[stream], "s_" + stream, self.cnt[stream], stream)
        self.streams[stream].append(("op", fn))
        self._mark(tok, reads, writes)
        return tok

    def dma(self, q, out, in_, reads=(), writes=(), final=False, **kw):
        self._deps(q, reads, writes, True)
        pool = self.dpool[q]
        sem, key = pool[self.dnext[q] % len(pool)]
        self.dnext[q] += 1
        prev = self.dcnt[key]
        if prev > 0:
            self._wait(q, Tok(sem, key, prev, None))
        self.dcnt[key] = prev + 16
        tok = Tok(sem, key, prev + 16, None)
        self.streams[q].append(("dma", out, in_, sem, kw))
        self._mark(tok, reads, writes)
        if final:
            self.final_toks.append(tok)
        return tok

    def barrier(self, bufs=()):
        toks = []
        for s in self.ENG:
            if self.cnt[s] > 0:
                toks.append(Tok(self.sem[s], "s_" + s, self.cnt[s], s))
        for q, pool in self.dpool.items():
            for sem, key in pool:
                if self.dcnt[key] > 0:
                    toks.append(Tok(sem, key, self.dcnt[key], None))
        for s in self.ENG:
            for t in toks:
                if t.stream != s:
                    self._wait(s, t)

    def finish(self):
        for t in self.final_toks:
            self._wait("sp", t)
        for s in self.ENG:
            if s != "sp" and self.cnt[s] > 0:
                self._wait("sp", Tok(self.sem[s], "s_" + s, self.cnt[s], s))

    def replay(self, block):
        nc = self.nc
        kb = self

        def run(stream, e):
            for item in kb.streams[stream]:
                kind = item[0]
                if kind == "wait":
                    e.wait_ge(item[1], item[2])
                elif kind == "op":
                    ins = item[1](e)
                    ins.then_inc(kb.sem[stream], 1)
                elif kind == "dma":
                    e.dma_start(out=item[1], in_=item[2], **item[4]).then_inc(item[3], 16)
                elif kind == "cc":
                    e.collective_compute("AllGather", ALU.bypass, replica_groups=[list(range(8))],
                                         ins=[item[1]], outs=[item[2]]).then_inc(item[3], 16)
                kb.n_instr += 1

        @block.tensor
        def _(e):
            run("pe", e)

        @block.scalar
        def _(e):
            run("act", e)

        @block.vector
        def _(e):
            run("dve", e)

        @block.gpsimd
        def _(e):
            run("pool", e)

        @block.sync
        def _(e):
            run("sp", e)


D = 4096
TC = 1024
NCORE = 8
ADA_COLS = 6 * D // NCORE


def build_ada():
    nc = bass.Bass("TRN2", target_bir_lowering=False)
    cT = nc.dram_tensor("cT", [128, 32], F32, kind="ExternalInput").ap()
    wada = nc.dram_tensor("wada", [2, D, ADA_COLS], F32, kind="ExternalInput").ap()
    bada = nc.dram_tensor("bada", [1, 2 * ADA_COLS], F32, kind="ExternalInput").ap()
    modp = nc.dram_tensor("modp", [1, 2 * ADA_COLS], F32, kind="ExternalOutput").ap()
    with ExitStack() as es:
        kb = KB(nc, es, arena_kb=120)
        off = 0
        csb = kb.view(off, 128, F32); off += 128
        csil = kb.view(off, 128, F32); off += 128
        bsb = kb.view(off, 2 * ADA_COLS * 4, F32); off += 2 * ADA_COLS * 4
        osb = kb.view(off, 2 * ADA_COLS * 4, F32); off += 2 * ADA_COLS * 4
        NW = 4
        wt = []
        for i in range(NW):
            wt.append(kb.view(off, 8 * 512 * 4, F32, "p (k n) -> p k n", k=8)); off += 8 * 512 * 4
        b_c, b_cs, b_b, b_o = Buf(), Buf(), Buf(), Buf()
        b_w = [Buf() for _ in range(NW)]
        kb.dma("sp", csb, cT, writes=[b_c])
        kb.dma("sp", bsb[0:1, :], bada, writes=[b_b])
        kb.op("act", lambda e: e.activation(out=csil, in_=csb, func=AF.Silu), reads=[b_c], writes=[b_cs])
        wi = 0
        for l in range(2):
            for cb in range(ADA_COLS // 512):
                pb = (l * 6 + cb) % 8
                for kg in range(4):
                    slot = wi % NW
                    wi += 1
                    src = wada[l, kg * 1024:(kg + 1) * 1024, cb * 512:(cb + 1) * 512].rearrange("(k p) n -> p k n", p=128)
                    kb.dma("sp" if wi % 2 else "act", wt[slot], src, writes=[b_w[slot]])

                    def mm(e, slot=slot, kg=kg, pb=pb):
                        ins = None
                        for k in range(8):
                            kc = kg * 8 + k
                            ins = e.matmul(kb.psum[pb][0:1, :], csil[:, kc:kc + 1], wt[slot][:, k, :],
                                           start=(kc == 0), stop=(kc == 31))
                        return ins
                    kb.op("pe", mm, reads=[b_cs, b_w[slot]], writes=[kb.psb[pb]])
                o0 = l * ADA_COLS + cb * 512
                kb.op("dve", lambda e, pb=pb, o0=o0: e.tensor_tensor(out=osb[0:1, o0:o0 + 512], in0=kb.psum[pb][0:1, :],
                                                                    in1=bsb[0:1, o0:o0 + 512], op=ALU.add),
                      reads=[b_b], writes=[kb.psb[pb], b_o])
        kb.dma("sp", modp, osb[0:1, :], reads=[b_o], final=True)
        kb.finish()
        with nc.Block() as block:
            kb.replay(block)
    return nc


def run_ada(c, w_ada, b_ada):
    nc = build_ada()
    cT = np.ascontiguousarray(c.reshape(32, 128).T)
    in_maps = []
    for i in range(NCORE):
        sl = slice(i * ADA_COLS, (i + 1) * ADA_COLS)
        in_maps.append({
            "cT": cT,
            "wada": np.ascontiguousarray(w_ada[:, :, sl]),
            "bada": np.ascontiguousarray(b_ada[:, sl]).reshape(1, 2 * ADA_COLS),
        })
    res = run_bass_kernel_spmd(nc, in_maps, core_ids=list(range(NCORE)))
    mod = np.zeros((2, 6 * D), np.float32)
    for i in range(NCORE):
        r = res.results[i]["modp"].reshape(2, ADA_COLS)
        mod[:, i * ADA_COLS:(i + 1) * ADA_COLS] = r
    return mod


EPS = 1e-6
PI = float(np.pi)
SC_W, CF_W, DW_W = 2048, 2048, 3072
OFF_SCB, OFF_SCC, OFF_SCX = 0, 2048, 4096
OFF_CFA, OFF_CFG = 6144, 8192
OFF_Q, OFF_K, OFF_V = 10240, 13312, 16384
OFF_GATE = 19456
IN_COLS = 31744


def full_slabs_A():
    sl = []
    for j in range(SC_W // 256):
        sl.append(("sc", [(OFF_SCC + j * 256, 256), (OFF_SCX + j * 256, 256)], j * 256))
    for j in range(SC_W // 512):
        sl.append(("scb", [(OFF_SCB + j * 512, 512)], j * 512))
    for j in range(CF_W // 256):
        sl.append(("cf", [(OFF_CFG + j * 256, 256), (OFF_CFA + j * 256, 256)], j * 256))
    for j in range(2 * DW_W // 512):
        sl.append(("qk", [(OFF_Q + j * 512, 512)], j * 512))
    for j in range(DW_W // 512):
        sl.append(("v", [(OFF_V + j * 512, 512)], j * 512))
    for j in range(3 * D // 512):
        sl.append(("gate", [(OFF_GATE + j * 512, 512)], j * 512))
    return sl


def emit_norm_T(kb, xsrc, hT, b_hT, modT, jshift, jscale, gT, xt, xn, ss, sm, ident, b_const, tagbufs):
    a_t, sh_t = sm["a"], sm["sh"]
    b_a = Buf()
    kb.op("dve", lambda e: e.scalar_tensor_tensor(out=a_t, in0=modT[:, jscale * 32:(jscale + 1) * 32], scalar=1.0,
                                                  in1=gT, op0=ALU.add, op1=ALU.mult), reads=[b_const], writes=[b_a])
    b_xt, b_xn, b_ss = tagbufs
    for tt in range(TC // 128):
        s = tt % 2
        kb.dma("sp", xt[s], xsrc[tt * 128:(tt + 1) * 128, :], writes=[b_xt[s]])
        kb.op("act", lambda e, s=s, tt=tt: e.activation(out=xn[s], in_=xt[s], func=AF.Square, accum_out=ss[:, tt:tt + 1]),
              reads=[b_xt[s]], writes=[b_xn[s], b_ss[tt]])
        kb.op("dve", lambda e, tt=tt: e.tensor_scalar(out=ss[:, 8 + tt:9 + tt], in0=ss[:, tt:tt + 1], scalar1=1.0 / D, scalar2=EPS,
                                                     op0=ALU.mult, op1=ALU.add), reads=[b_ss[tt]], writes=[b_ss[tt]])
        kb.op("act", lambda e, tt=tt: e.activation(out=ss[:, 16 + tt:17 + tt], in_=ss[:, 8 + tt:9 + tt], func=AF.Sqrt),
              reads=[b_ss[tt]], writes=[b_ss[tt]])
        kb.op("dve", lambda e, tt=tt: e.reciprocal(out=ss[:, 24 + tt:25 + tt], in_=ss[:, 16 + tt:17 + tt]),
              reads=[b_ss[tt]], writes=[b_ss[tt]])
        kb.op("act", lambda e, s=s, tt=tt: e.activation(out=xn[s], in_=xt[s], func=AF.Copy, scale=ss[:, 24 + tt:25 + tt]),
              reads=[b_xt[s], b_ss[tt]], writes=[b_xn[s]])
        for g in range(4):
            pb = (tt * 4 + g) % 8
            pv = kb.psum[pb][:, :].bitcast(BF16)

            def tr(e, s=s, g=g, pv=pv):
                ins = None
                for j in range(8):
                    c = g * 8 + j
                    ins = e.transpose(pv[:, j * 128:(j + 1) * 128], xn[s][:, c * 128:(c + 1) * 128], ident)
                return ins
            kb.op("pe", tr, reads=[b_xn[s], b_const], writes=[kb.psb[pb]])
            for j in range(8):
                c = g * 8 + j
                dst = hT[:, c, tt * 128:(tt + 1) * 128]
                if g % 2 == 0:
                    kb.op("act", lambda e, pv=pv, j=j, c=c, dst=dst: e.activation(
                        out=dst, in_=pv[:, j * 128:(j + 1) * 128], func=AF.Identity,
                        bias=modT[:, jshift * 32 + c:jshift * 32 + c + 1], scale=a_t[:, c:c + 1]),
                        reads=[b_a, b_const], writes=[kb.psb[pb], b_hT[tt][0]])
                else:
                    kb.op("dve", lambda e, pv=pv, j=j, c=c, dst=dst: e.tensor_scalar(
                        out=dst, in0=pv[:, j * 128:(j + 1) * 128], scalar1=a_t[:, c:c + 1],
                        scalar2=modT[:, jshift * 32 + c:jshift * 32 + c + 1], op0=ALU.mult, op1=ALU.add),
                        reads=[b_a, b_const], writes=[kb.psb[pb], b_hT[tt][1]])


def load_wslab(kb, q, wsl_slot, b_slot, wsrc, segs, nk):
    co = 0
    for (c0, n) in segs:
        for kg in range(0, nk, 8):
            k1 = min(nk, kg + 8)
            kb.dma(q, wsl_slot[:, kg:k1, co:co + n],
                   wsrc[kg * 128:k1 * 128, c0:c0 + n].rearrange("(k p) n -> p k n", p=128), writes=[b_slot])
        co += n


def build_A(slabs, ncols):
    nc = bass.Bass("TRN2", target_bir_lowering=False)
    dt_in = lambda name, shape, dt: nc.dram_tensor(name, shape, dt, kind="ExternalInput").ap()
    dt_out = lambda name, shape, dt: nc.dram_tensor(name, shape, dt, kind="ExternalOutput").ap()
    x = dt_in("x", [TC, D], F32)
    modT_d = dt_in("modT", [128, 192], F32)
    gT_d = dt_in("gT", [128, 32], F32)
    w_in = dt_in("w_in", [D, ncols], F32)
    pos = dt_in("pos", [1, TC], I32)
    invf_d = dt_in("invf", [32, 1], F32)
    jm_d = dt_in("jm", [32, 32], BF16)
    ident_d = dt_in("ident", [128, 128], BF16)
    zT = dt_out("zT", [SC_W, TC], F32)
    scbT = dt_out("scbT", [SC_W, TC], F32)
    uT = dt_out("uT", [CF_W, TC], F32)
    qkT = dt_out("qkT", [2 * DW_W, TC], BF16)
    vtm = dt_out("vtm", [TC, DW_W], BF16)
    sgT = dt_out("sgT", [3 * D, TC], BF16)
    with ExitStack() as es:
        kb = KB(nc, es, arena_kb=196)
        off = 0
        hT = kb.view(off, 65536, BF16, "p (c n) -> p c n", c=32); off += 65536
        wsl = []
        for i in range(2):
            wsl.append(kb.view(off, 32768, BF16, "p (k n) -> p k n", k=32)); off += 32768
        R = off
        xt = [kb.view(R + i * 16384, 16384, F32) for i in range(2)]
        xn = [kb.view(R + 32768 + i * 8192, 8192, BF16) for i in range(2)]
        o2 = R
        st = []
        for i in range(4):
            st.append(kb.view(o2, 4096, F32)); o2 += 4096
        hold = []
        for i in range(2):
            hold.append(kb.view(o2, 4096, F32)); o2 += 4096
        ob = []
        for i in range(3):
            ob.append(kb.view(o2, 2048, BF16)); o2 += 2048
        q32b = kb.view(o2, 2048, BF16); o2 += 2048
        t1 = kb.view(o2, 4096, F32); o2 += 4096
        t2 = kb.view(o2, 4096, F32); o2 += 4096
        cosT = kb.view(o2, 4096, F32); o2 += 4096
        sinT = kb.view(o2, 4096, F32); o2 += 4096
        assert o2 <= R + 49152
        off = R + 49152
        modT = kb.view(off, 768, F32); off += 768
        gT = kb.view(off, 128, F32); off += 128
        a_t = kb.view(off, 128, F32); off += 128
        ss = kb.view(off, 128, F32); off += 128
        ident = kb.view(off, 256, BF16); off += 256
        jm = kb.view(off, 64, BF16); off += 64
        invf = kb.view(off, 32, F32); off += 32
        posi = kb.view(off, 4096, I32); off += 4096
        wk = [kb.view(off + i * 4096, 4096, F32) for i in range(3)]; off += 3 * 4096
        b_const = Buf()
        kb.dma("sp", modT, modT_d, writes=[b_const])
        kb.dma("sp", gT, gT_d, writes=[b_const])
        kb.dma("sp", ident, ident_d, writes=[b_const])
        kb.dma("sp", jm[0:32, :], jm_d, writes=[b_const])
        kb.dma("sp", invf[0:32, 0:1], invf_d, writes=[b_const])
        b_pos = Buf()
        kb.dma("sp", posi[0:32, :], pos.partition_broadcast(32), writes=[b_pos])
        b_hT = [[Buf(), Buf()] for _ in range(8)]
        hT_all = [b for p in b_hT for b in p]
        tagbufs = ([Buf(), Buf()], [Buf(), Buf()], [Buf() for _ in range(8)])
        b_w = [Buf(), Buf()]
        nslab = len(slabs)
        for s in range(min(2, nslab)):
            load_wslab(kb, "pool", wsl[s % 2], b_w[s % 2], w_in, slabs[s][1], 32)
        import os
        if not os.environ.get("A_SKIP_N1"):
            emit_norm_T(kb, x, hT, b_hT, modT, 0, 1, gT, xt, xn, ss, {"a": a_t, "sh": None}, ident, b_const, tagbufs)
        kb.barrier()
        b_tab = Buf()
        P32 = slice(0, 32)
        ROPE_ON = not os.environ.get("A_SKIP_ROPE")
        posf, ang, rr = wk[0], wk[1], wk[2]
        C1 = 6.28125
        C2 = float(2 * np.pi - 6.28125)
        _op = kb.op
        if not ROPE_ON:
            kb.op = lambda *a, **k: None
        kb.op("dve", lambda e: e.tensor_copy(out=posf[P32, :], in_=posi[P32, :]), reads=[b_pos], writes=[b_tab])
        kb.op("dve", lambda e: e.tensor_scalar(out=ang[P32, :], in0=posf[P32, :], scalar1=invf[P32, 0:1], scalar2=None,
                                               op0=ALU.mult), reads=[b_tab, b_const], writes=[b_tab])
        kb.op("dve", lambda e: e.tensor_scalar(out=posf[P32, :], in0=ang[P32, :], scalar1=float(1 / (2 * np.pi)), scalar2=None,
                                               op0=ALU.mult), reads=[b_tab], writes=[b_tab])
        kb.op("dve", lambda e: e.tensor_copy(out=posi[P32, :], in_=posf[P32, :]), reads=[b_tab], writes=[b_tab])
        kb.op("dve", lambda e: e.tensor_copy(out=posf[P32, :], in_=posi[P32, :]), reads=[b_tab], writes=[b_tab])
        kb.op("dve", lambda e: e.scalar_tensor_tensor(out=rr[P32, :], in0=posf[P32, :], scalar=-C1, in1=ang[P32, :],
                                                      op0=ALU.mult, op1=ALU.add), reads=[b_tab], writes=[b_tab])
        kb.op("dve", lambda e: e.scalar_tensor_tensor(out=rr[P32, :], in0=posf[P32, :], scalar=-C2, in1=rr[P32, :],
                                                      op0=ALU.mult, op1=ALU.add), reads=[b_tab], writes=[b_tab])

        def wrap(src, dst):
            kb.op("dve", lambda e: e.tensor_scalar(out=posf[P32, :], in0=src[P32, :], scalar1=PI, scalar2=-2 * PI,
                                                   op0=ALU.is_gt, op1=ALU.mult), reads=[b_tab], writes=[b_tab])
            kb.op("dve", lambda e: e.tensor_tensor(out=dst[P32, :], in0=src[P32, :], in1=posf[P32, :], op=ALU.add),
                  reads=[b_tab], writes=[b_tab])
            kb.op("dve", lambda e: e.tensor_scalar(out=posf[P32, :], in0=dst[P32, :], scalar1=-PI, scalar2=2 * PI,
                                                   op0=ALU.is_lt, op1=ALU.mult), reads=[b_tab], writes=[b_tab])
            kb.op("dve", lambda e: e.tensor_tensor(out=dst[P32, :], in0=dst[P32, :], in1=posf[P32, :], op=ALU.add),
                  reads=[b_tab], writes=[b_tab])
            kb.op("dve", lambda e: e.tensor_scalar(out=dst[P32, :], in0=dst[P32, :], scalar1=-PI, scalar2=PI,
                                                   op0=ALU.max, op1=ALU.min), reads=[b_tab], writes=[b_tab])
        wrap(rr, rr)
        kb.op("act", lambda e: e.activation(out=sinT[P32, :], in_=rr[P32, :], func=AF.Sin), reads=[b_tab], writes=[b_tab])
        kb.op("dve", lambda e: e.tensor_scalar(out=ang[P32, :], in0=rr[P32, :], scalar1=PI / 2, scalar2=None, op0=ALU.add),
              reads=[b_tab], writes=[b_tab])
        wrap(ang, ang)
        kb.op("act", lambda e: e.activation(out=cosT[P32, :], in_=ang[P32, :], func=AF.Sin), reads=[b_tab], writes=[b_tab])
        kb.op = _op

        b_st = [Buf() for _ in range(4)]
        b_hold = [Buf() for _ in range(2)]
        b_ob = [Buf() for _ in range(3)]
        b_q32, b_t1, b_t2 = Buf(), Buf(), Buf()
        cnt = {"blk": 0, "st": 0, "ob": 0, "v": 0}

        def gemm_block(slot, bi):
            pbk = (cnt["blk"] % 3) * 2
            cnt["blk"] += 1

            def mm(e):
                ins = None
                for kc in range(32):
                    for h in range(2):
                        ins = e.matmul(kb.psum[pbk + h][:, :], wsl[slot][:, kc, bi * 128:(bi + 1) * 128],
                                       hT[:, kc, h * 512:(h + 1) * 512], start=(kc == 0), stop=(kc == 31))
                return ins
            kb.op("pe", mm, reads=[b_w[slot]] + hT_all, writes=[kb.psb[pbk], kb.psb[pbk + 1]])
            return pbk

        def act_evac(pbk, dst, func, bdst):
            for h in range(2):
                kb.op("act", lambda e, h=h: e.activation(out=dst[:, h * 512:(h + 1) * 512], in_=kb.psum[pbk + h][:, :], func=func),
                      reads=[], writes=[kb.psb[pbk + h], bdst])

        def mul_evac(pbk, dst, other, bdst, bother):
            for h in range(2):
                kb.op("dve", lambda e, h=h: e.tensor_tensor(out=dst[:, h * 512:(h + 1) * 512], in0=kb.psum[pbk + h][:, :],
                                                            in1=other[:, h * 512:(h + 1) * 512], op=ALU.mult),
                      reads=[bother], writes=[kb.psb[pbk + h], bdst])

        def next_st():
            i = cnt["st"] % 4
            cnt["st"] += 1
            return i

        for s in range(nslab):
            kind, segs, orow = slabs[s]
            slot = s % 2
            if kind in ("sc", "cf"):
                for j in range(2):
                    p1 = gemm_block(slot, j)
                    act_evac(p1, hold[j], AF.Copy if kind == "sc" else AF.Sigmoid, b_hold[j])
                for j in range(2):
                    p2 = gemm_block(slot, 2 + j)
                    i = next_st()
                    mul_evac(p2, st[i], hold[j], b_st[i], b_hold[j])
                    dst = (zT if kind == "sc" else uT)[orow + j * 128:orow + (j + 1) * 128, :]
                    kb.dma("sp", dst, st[i], reads=[b_st[i]], final=True)
            elif kind == "scb":
                for j in range(4):
                    p1 = gemm_block(slot, j)
                    i = next_st()
                    act_evac(p1, st[i], AF.Copy, b_st[i])
                    kb.dma("sp", scbT[orow + j * 128:orow + (j + 1) * 128, :], st[i], reads=[b_st[i]], final=True)
            elif kind == "gate":
                for j in range(4):
                    p1 = gemm_block(slot, j)
                    i = cnt["ob"] % 3
                    cnt["ob"] += 1
                    act_evac(p1, ob[i], AF.Sigmoid, b_ob[i])
                    kb.dma("sp", sgT[orow + j * 128:orow + (j + 1) * 128, :], ob[i], reads=[b_ob[i]], final=True)
            elif kind == "qk":
                for j in range(4):
                    p1 = gemm_block(slot, j)
                    i = next_st()
                    qraw = st[i]
                    act_evac(p1, qraw, AF.Copy, b_st[i])
                    kb.op("dve", lambda e, qraw=qraw: e.tensor_copy(out=q32b[P32, :], in_=qraw[P32, :]), reads=[b_st[i]], writes=[b_q32])

                    def jmm(e):
                        ins = None
                        for h in range(2):
                            ins = e.matmul(kb.psum[6 + h][0:32, :], jm[P32, :], q32b[P32, h * 512:(h + 1) * 512], start=True, stop=True)
                        return ins
                    kb.op("pe", jmm, reads=[b_q32, b_const], writes=[kb.psb[6], kb.psb[7]])
                    kb.op("dve", lambda e, qraw=qraw: e.tensor_tensor(out=t1[P32, :], in0=qraw[P32, :], in1=cosT[P32, :], op=ALU.mult),
                          reads=[b_st[i], b_tab], writes=[b_t1])
                    for h in range(2):
                        kb.op("dve", lambda e, h=h: e.tensor_tensor(out=t2[P32, h * 512:(h + 1) * 512], in0=kb.psum[6 + h][0:32, :],
                                                                    in1=sinT[P32, h * 512:(h + 1) * 512], op=ALU.mult),
                              reads=[b_tab], writes=[kb.psb[6 + h], b_t2])
                    io = cnt["ob"] % 3
                    cnt["ob"] += 1
                    kb.op("pool", lambda e, io=io, qraw=qraw: e.tensor_copy(out=ob[io][:, :], in_=qraw[:, :]),
                          reads=[b_st[i]], writes=[b_ob[io]])
                    kb.op("pool", lambda e, io=io: e.tensor_tensor(out=ob[io][P32, :], in0=t1[P32, :], in1=t2[P32, :], op=ALU.add),
                          reads=[b_t1, b_t2], writes=[b_ob[io]])
                    kb.dma("sp", qkT[orow + j * 128:orow + (j + 1) * 128, :], ob[io], reads=[b_ob[io]], final=True)
            elif kind == "v":
                for tt in range(8):
                    pbk = (cnt["blk"] % 3) * 2 + (cnt["v"] % 2)
                    cnt["v"] += 1
                    if cnt["v"] % 2 == 0:
                        cnt["blk"] += 1

                    def mmv(e, tt=tt, pbk=pbk, slot=slot):
                        ins = None
                        for kc in range(32):
                            ins = e.matmul(kb.psum[pbk][:, :], hT[:, kc, tt * 128:(tt + 1) * 128], wsl[slot][:, kc, :],
                                           start=(kc == 0), stop=(kc == 31))
                        return ins
                    kb.op("pe", mmv, reads=[b_w[slot]] + hT_all, writes=[kb.psb[pbk]])
                    io = cnt["ob"] % 3
                    cnt["ob"] += 1
                    vdst = ob[io][:, 0:512]
                    if tt % 2 == 0:
                        kb.op("act", lambda e, pbk=pbk, vdst=vdst: e.activation(out=vdst, in_=kb.psum[pbk][:, :], func=AF.Copy),
                              reads=[], writes=[kb.psb[pbk], b_ob[io]])
                    else:
                        kb.op("dve", lambda e, pbk=pbk, vdst=vdst: e.tensor_copy(out=vdst, in_=kb.psum[pbk][:, :]),
                              reads=[], writes=[kb.psb[pbk], b_ob[io]])
                    kb.dma("sp", vtm[tt * 128:(tt + 1) * 128, orow:orow + 512], vdst, reads=[b_ob[io]], final=True)
            if s + 2 < nslab:
                load_wslab(kb, "pool", wsl[slot], b_w[slot], w_in, slabs[s + 2][1], 32)
        kb.finish()
        with nc.Block() as block:
            kb.replay(block)
    return nc


def rope_consts():
    invf = (np.float32(500000.0) ** (-(np.arange(0, 32, 2, dtype=np.float32)) / np.float32(32))).astype(np.float32)
    invf32 = np.concatenate([invf, invf]).reshape(32, 1).astype(np.float32)
    jm = np.zeros((32, 32), np.float32)
    for i in range(16):
        jm[16 + i, i] = -1.0
        jm[i, 16 + i] = 1.0
    return invf32, jm.astype(NPBF)


def t_layout(v, nchunk):
    return np.ascontiguousarray(v.reshape(nchunk, 128).T)


GROUPS = [
    (1, 1152, 9, 128, 8),
    (4, 1536, 3, 128, 2),
    (16, 3072, 2, 64, 1),
]
VM_OFF = [0, 9, 9 + 12]
VM_COLS = 9 + 12 + 32
DFF = 4 * D


def build_B():
    nc = bass.Bass("TRN2", target_bir_lowering=False)
    dt_in = lambda name, shape, dt: nc.dram_tensor(name, shape, dt, kind="ExternalInput").ap()
    dt_out = lambda name, shape, dt: nc.dram_tensor(name, shape, dt, kind="ExternalOutput").ap()
    x = dt_in("x", [TC, D], F32)
    modT_d = dt_in("modT", [128, 192], F32)
    modv = dt_in("modv", [6, D], F32)
    gT_d = dt_in("gT", [128, 32], F32)
    gfin = dt_in("gfin", [1, D], F32)
    zTe = dt_in("zTe", [SC_W, TC + 2], F32)
    scbT = dt_in("scbT", [SC_W, TC], F32)
    uTe = dt_in("uTe", [CF_W, TC + 30], F32)
    qT = dt_in("qT", [DW_W, TC], BF16)
    kTe = [dt_in(f"kTe{g}", [1024, GROUPS[g][1]], BF16) for g in range(3)]
    ve = [dt_in(f"ve{g}", [GROUPS[g][1], 1024], BF16) for g in range(3)]
    vm_d = dt_in("vm", [128, VM_COLS], F32)
    sgT = dt_in("sgT", [3 * D, TC], BF16)
    cva_d = dt_in("cva", [128, 48], F32)
    cvb_d = dt_in("cvb", [128, 16 * 31], F32)
    cvec_d = dt_in("cvec", [128, 48], F32)
    woa = dt_in("woa", [SC_W, D], F32)
    wob = dt_in("wob", [CF_W, D], F32)
    woc = dt_in("woc", [1024, D], F32)
    wo = dt_in("wo", [D, D], F32)
    w1 = dt_in("w1", [D, DFF], F32)
    w2 = dt_in("w2", [DFF, D], F32)
    mk_d = dt_in("masks", [128, 256], BF16)
    ones_d = dt_in("onesb", [128, 128], BF16)
    onesf_d = dt_in("onesf", [128, 128], F32)
    ident_d = dt_in("ident", [128, 128], BF16)
    xout = dt_out("xout", [TC, D], F32)
    yout = dt_out("yout", [TC, D], F32)
    mergedT = nc.dram_tensor("mergedT", [D, TC], BF16).ap()
    x1 = nc.dram_tensor("x1s", [TC, D], F32).ap()
    aT = nc.dram_tensor("aTs", [DFF, TC], BF16).ap()
    with ExitStack() as es:
        kb = KB(nc, es, arena_kb=200)
        V = kb.view
        KBY = 1024
        off = 180 * KBY
        modT = V(off, 768, F32); off += 768
        gT = V(off, 128, F32); off += 128
        a_t = V(off, 128, F32); off += 128
        ss = V(off, 128, F32); off += 128
        ident = V(off, 256, BF16); off += 256
        onesb = V(off, 256, BF16); off += 256
        onesf = V(off, 512, F32); off += 512
        masks = V(off, 512, BF16); off += 512
        vm = V(off, 224, F32); off += 224
        cva = V(off, 192, F32); off += 192
        cvb = V(off, 16 * 31 * 4, F32); off += 16 * 31 * 4
        cvec = V(off, 192, F32); off += 192
        assert off <= 200 * KBY
        b_const = Buf()
        for dst, src in ((modT, modT_d), (gT, gT_d), (ident, ident_d), (onesb, ones_d), (onesf, onesf_d), (masks, mk_d),
                         (vm[:, 0:VM_COLS], vm_d), (cva, cva_d), (cvb, cvb_d), (cvec, cvec_d)):
            kb.dma("sp", dst, src, writes=[b_const])
        kb.barrier()

        AinT = V(0, 32 * KBY, BF16, "p (c n) -> p c n", c=16)
        ufT = V(32 * KBY, 32 * KBY, BF16, "p (c n) -> p c n", c=16)
        vb = V(64 * KBY, 64 * KBY, F32, "p (c n) -> p c n", c=16)
        o = 128 * KBY
        ze = [V(o + i * 4224, 4224, F32) for i in range(2)]; o += 2 * 4224
        sb = [V(o + i * 4096, 4096, F32) for i in range(2)]; o += 2 * 4096
        tt_ = [V(o + i * 4096, 4096, F32) for i in range(2)]; o += 2 * 4096
        ue = [V(o + i * 4224, 4224, F32) for i in range(2)]; o += 2 * 4224
        sq = [V(o + i * 4096, 4096, F32) for i in range(2)]; o += 2 * 4096
        mean_bc = V(o, 4096, F32); o += 4096
        rstd_bc = V(o, 4096, F32); o += 4096
        assert o <= 180 * KBY
        b_ze, b_sb, b_t, b_ue, b_sq = ([Buf(), Buf()] for _ in range(5))
        b_Ain, b_uf, b_vb = Buf(), Buf(), [Buf() for _ in range(16)]
        for ch in range(16):
            s = ch % 2
            kb.dma("sp", ze[s][:, 0:TC + 2], zTe[ch * 128:(ch + 1) * 128, :], writes=[b_ze[s]])
            kb.dma("sp", sb[s], scbT[ch * 128:(ch + 1) * 128, :], writes=[b_sb[s]])
            kb.op("dve", lambda e, s=s, ch=ch: e.tensor_scalar(out=tt_[s], in0=ze[s][:, 2:TC + 2], scalar1=cva[:, ch * 3 + 2:ch * 3 + 3],
                                                              scalar2=None, op0=ALU.mult), reads=[b_ze[s], b_const], writes=[b_t[s]])
            for k in (1, 0):
                kb.op("dve", lambda e, s=s, ch=ch, k=k: e.scalar_tensor_tensor(
                    out=tt_[s], in0=ze[s][:, k:k + TC], scalar=cva[:, ch * 3 + k:ch * 3 + k + 1], in1=tt_[s],
                    op0=ALU.mult, op1=ALU.add), reads=[b_ze[s], b_t[s], b_const], writes=[b_t[s]])
            kb.op("pool", lambda e, s=s, ch=ch: e.tensor_tensor(out=AinT[:, ch, :], in0=tt_[s], in1=sb[s], op=ALU.mult),
                  reads=[b_t[s], b_sb[s]], writes=[b_Ain])
        for ch0 in range(0, 16, 2):
            pair = (ch0, ch0 + 1)
            for ch in pair:
                s = ch % 2
                kb.dma("sp", ue[s][:, 0:TC + 30], uTe[ch * 128:(ch + 1) * 128, :], writes=[b_ue[s]])
                kb.op("dve", lambda e, s=s, ch=ch: e.tensor_scalar(out=vb[:, ch, :], in0=ue[s][:, 30:TC + 30],
                                                                  scalar1=cvb[:, ch * 31 + 30:ch * 31 + 31], scalar2=cvec[:, ch:ch + 1],
                                                                  op0=ALU.mult, op1=ALU.add), reads=[b_ue[s], b_const], writes=[b_vb[ch]])
            for k in range(30):
                for ch in pair:
                    s = ch % 2
                    kb.op("dve", lambda e, s=s, ch=ch, k=k: e.scalar_tensor_tensor(
                        out=vb[:, ch, :], in0=ue[s][:, k:k + TC], scalar=cvb[:, ch * 31 + k:ch * 31 + k + 1], in1=vb[:, ch, :],
                        op0=ALU.mult, op1=ALU.add), reads=[b_ue[s], b_vb[ch], b_const], writes=[b_vb[ch]])
            for ch in pair:
                s = ch % 2
                kb.op("act", lambda e, s=s, ch=ch: e.activation(out=sq[s], in_=vb[:, ch, :], func=AF.Square),
                      reads=[b_vb[ch]], writes=[b_sq[s]])

                def stat(e, s=s, ch=ch):
                    ins = None
                    for h in range(2):
                        ins = e.matmul(kb.psum[h][:, :], onesf, vb[:, ch, h * 512:(h + 1) * 512], start=(ch == 0), stop=(ch == 15))
                        ins = e.matmul(kb.psum[2 + h][:, :], onesf, sq[s][:, h * 512:(h + 1) * 512], start=(ch == 0), stop=(ch == 15))
                    return ins
                kb.op("pe", stat, reads=[b_vb[ch], b_sq[s], b_const], writes=[kb.psb[0], kb.psb[1], kb.psb[2], kb.psb[3]])
        b_stat = Buf()
        for h in range(2):
            hs = slice(h * 512, (h + 1) * 512)
            kb.op("dve", lambda e, h=h, hs=hs: e.tensor_scalar(out=mean_bc[:, hs], in0=kb.psum[h][:, :], scalar1=1.0 / CF_W, scalar2=None,
                                                              op0=ALU.mult), writes=[kb.psb[h], b_stat])
            kb.op("dve", lambda e, hs=hs: e.tensor_tensor(out=rstd_bc[:, hs], in0=mean_bc[:, hs], in1=mean_bc[:, hs], op=ALU.mult),
                  reads=[b_stat], writes=[b_stat])
            kb.op("dve", lambda e, h=h, hs=hs: e.scalar_tensor_tensor(out=rstd_bc[:, hs], in0=kb.psum[2 + h][:, :], scalar=1.0 / CF_W,
                                                                     in1=rstd_bc[:, hs], op0=ALU.mult, op1=ALU.subtract),
                  reads=[b_stat], writes=[kb.psb[2 + h], b_stat])
        kb.op("dve", lambda e: e.tensor_scalar(out=rstd_bc, in0=rstd_bc, scalar1=EPS, scalar2=None, op0=ALU.add),
              reads=[b_stat], writes=[b_stat])
        kb.op("act", lambda e: e.activation(out=rstd_bc, in_=rstd_bc, func=AF.Sqrt), reads=[b_stat], writes=[b_stat])
        kb.op("dve", lambda e: e.reciprocal(out=rstd_bc, in_=rstd_bc), reads=[b_stat], writes=[b_stat])
        for ch in range(16):
            s = ch % 2
            kb.op("dve", lambda e, s=s, ch=ch: e.tensor_tensor(out=tt_[s], in0=vb[:, ch, :], in1=mean_bc, op=ALU.subtract),
                  reads=[b_vb[ch], b_stat], writes=[b_t[s]])
            kb.op("pool", lambda e, s=s: e.tensor_tensor(out=tt_[s], in0=tt_[s], in1=rstd_bc, op=ALU.mult),
                  reads=[b_t[s], b_stat], writes=[b_t[s]])
            kb.op("act", lambda e, s=s, ch=ch: e.activation(out=ufT[:, ch, :], in_=tt_[s], func=AF.Silu,
                                                           bias=cvec[:, 32 + ch:33 + ch], scale=cvec[:, 16 + ch:17 + ch]),
                  reads=[b_t[s], b_const], writes=[b_uf])
        kb.barrier()

        oT = V(64 * KBY, 16 * KBY, BF16, "p (c n) -> p c n", c=8)
        o = 80 * KBY
        accn = V(o, 4096, F32); o += 4096
        accd = V(o, 4096, F32); o += 4096
        KTt = [V(o + i * 6144, 6144, BF16) for i in range(2)]; o += 2 * 6144
        QTt = [V(o + i * 2048, 2048, BF16) for i in range(2)]; o += 2 * 2048
        Vtt = [V(o + i * 8192, 8192, BF16) for i in range(2)]; o += 2 * 8192
        Pe = [V(o + i * 512, 512, BF16, "p (a n) -> p a n", a=2) for i in range(2)]; o += 1024
        Pm = [V(o + i * 512, 512, BF16, "p (a n) -> p a n", a=2) for i in range(2)]; o += 1024
        assert o <= 128 * KBY
        b_acc, b_oT = Buf(), Buf()
        b_KT, b_QT, b_Vt, b_Pe, b_Pm = ([Buf(), Buf()] for _ in range(5))
        SCALE = float(128 ** -0.5)
        it = 0
        hg = 0
        pend = []

        def flush():
            while pend:
                pend.pop(0)()
        for h in range(8):
            hc = slice(h * 128, (h + 1) * 128)
            for g in range(3):
                d, Lg, nkb, QB, nqb = GROUPS[g]
                s = hg % 2
                hg += 1
                KT, QTs = KTt[s], QTt[s]
                Vt = Vtt[s][:, 0:d * nkb * 128].rearrange("p (r k c) -> p r k c", r=d, k=nkb)
                kb.dma("sp", KT[:, 0:Lg], kTe[g][hc, :], writes=[b_KT[s]])
                kb.dma("sp", QTs, qT[(g * 8 + h) * 128:(g * 8 + h + 1) * 128, :], writes=[b_QT[s]])
                if g == 0:
                    kb.dma("sp", Vt[:, 0, :, :], ve[0].rearrange("(k m) c -> m k c", m=128)[:, :, hc], writes=[b_Vt[s]])
                elif g == 1:
                    for k in range(3):
                        kb.dma("sp", Vt[:, :, k, :], ve[1][k * 512:(k + 1) * 512, :].rearrange("(m r) c -> m r c", r=4)[:, :, hc],
                               writes=[b_Vt[s]])
                else:
                    kb.dma("sp", Vt[:, :, 0, :], ve[2][0:2048, :].rearrange("(m r) c -> m r c", r=16)[:, :, hc], writes=[b_Vt[s]])
                    kb.dma("sp", Vt[0:64, :, 1, :], ve[2][2048:3072, :].rearrange("(m r) c -> m r c", r=16)[:, :, hc], writes=[b_Vt[s]])
                for r in range(d):
                    for qb in range(nqb):
                        i0 = qb * QB
                        q0 = r + d * i0
                        qsl = QTs[:, q0:q0 + d * (QB - 1) + 1:d] if d > 1 else QTs[:, q0:q0 + QB]
                        kA, kB_ = qb, qb + 1
                        nB = 128 if g < 2 else 64
                        eA = r + d * 128 * kA
                        eB = r + d * 128 * kB_
                        kslA = KT[:, eA:eA + d * 127 + 1:d] if d > 1 else KT[:, eA:eA + 128]
                        kslB = KT[:, eB:eB + d * (nB - 1) + 1:d] if d > 1 else KT[:, eB:eB + nB]
                        p = it % 2
                        it += 1
                        bS, bN, bD = p, 2 + p, 4 + p
                        psS = kb.psum[bS][:, 0:256].rearrange("p (a n) -> p a n", a=2)

                        def qk(e, kslA=kslA, kslB=kslB, qsl=qsl, psS=psS, nB=nB, QB=QB):
                            e.matmul(psS[:, 0, 0:QB], kslA, qsl, start=True, stop=True)
                            return e.matmul(psS[0:nB, 1, 0:QB], kslB, qsl, start=True, stop=True)
                        kb.op("pe", qk, reads=[b_KT[s], b_QT[s]], writes=[kb.psb[bS]])
                        pe_, pm_ = Pe[p], Pm[p]

                        def ex(e, psS=psS, pe_=pe_, nB=nB, QB=QB):
                            e.activation(out=pe_[:, 0, 0:QB], in_=psS[:, 0, 0:QB], func=AF.Exp, scale=SCALE)
                            return e.activation(out=pe_[0:nB, 1, 0:QB], in_=psS[0:nB, 1, 0:QB], func=AF.Exp, scale=SCALE)
                        kb.op("act", ex, writes=[kb.psb[bS], b_Pe[p]])
                        vc = VM_OFF[g] + r * nkb

                        def mk(e, pe_=pe_, pm_=pm_, nB=nB, QB=QB, vc=vc, kA=kA, kB_=kB_):
                            e.scalar_tensor_tensor(out=pm_[:, 0, 0:QB], in0=pe_[:, 0, 0:QB], scalar=vm[:, vc + kA:vc + kA + 1],
                                                   in1=masks[:, 0:QB], op0=ALU.mult, op1=ALU.mult)
                            return e.scalar_tensor_tensor(out=pm_[0:nB, 1, 0:QB], in0=pe_[0:nB, 1, 0:QB],
                                                          scalar=vm[0:nB, vc + kB_:vc + kB_ + 1], in1=masks[0:nB, 128:128 + QB],
                                                          op0=ALU.mult, op1=ALU.mult)
                        kb.op("dve", mk, reads=[b_Pe[p], b_const], writes=[b_Pm[p]])
                        flush()

                        def pv(e, pm_=pm_, Vt=Vt, r=r, kA=kA, kB_=kB_, nB=nB, QB=QB, bN=bN, bD=bD):
                            e.matmul(kb.psum[bN][:, 0:QB], Vt[:, r, kA, :], pm_[:, 0, 0:QB], start=True, stop=False)
                            e.matmul(kb.psum[bN][:, 0:QB], Vt[0:nB, r, kB_, :], pm_[0:nB, 1, 0:QB], start=False, stop=True)
                            e.matmul(kb.psum[bD][:, 0:QB], onesb[:, :], pm_[:, 0, 0:QB], start=True, stop=False)
                            return e.matmul(kb.psum[bD][:, 0:QB], onesb[0:nB, :], pm_[0:nB, 1, 0:QB], start=False, stop=True)
                        tsl = slice(q0, q0 + d * (QB - 1) + 1, d) if d > 1 else slice(q0, q0 + QB)
                        if g == 0:
                            def ac(e, tsl=tsl, bN=bN, bD=bD, QB=QB):
                                e.tensor_copy(out=accn[:, tsl], in_=kb.psum[bN][:, 0:QB])
                                return e.tensor_copy(out=accd[:, tsl], in_=kb.psum[bD][:, 0:QB])
                        else:
                            def ac(e, tsl=tsl, bN=bN, bD=bD, QB=QB):
                                e.tensor_tensor(out=accn[:, tsl], in0=kb.psum[bN][:, 0:QB], in1=accn[:, tsl], op=ALU.add)
                                return e.tensor_tensor(out=accd[:, tsl], in0=kb.psum[bD][:, 0:QB], in1=accd[:, tsl], op=ALU.add)

                        def stage2(pv=pv, ac=ac, p=p, s=s, bN=bN, bD=bD):
                            kb.op("pe", pv, reads=[b_Pm[p], b_Vt[s], b_const], writes=[kb.psb[bN], kb.psb[bD]])
                            kb.op("dve", ac, reads=[b_acc], writes=[kb.psb[bN], kb.psb[bD], b_acc])
                        pend.append(stage2)
            flush()
            kb.op("dve", lambda e: e.reciprocal(out=accd, in_=accd), reads=[b_acc], writes=[b_acc])
            kb.op("pool", lambda e, h=h: e.tensor_tensor(out=oT[:, h, :], in0=accn, in1=accd, op=ALU.mult),
                  reads=[b_acc], writes=[b_oT])
        kb.barrier()

        o = 80 * KBY
        sgt = [[V(o + (i * 3 + f) * 1024, 1024, BF16) for f in range(3)] for i in range(2)]; o += 6 * 1024
        mt = [[V(o + (i * 3 + f) * 2048, 2048, F32) for f in range(3)] for i in range(2)]; o += 6 * 2048
        mo = [V(o + i * 1024, 1024, BF16) for i in range(2)]; o += 2 * 1024
        assert o <= 100 * KBY
        wbase = [100 * KBY, 140 * KBY]
        wsA = [V(wbase[i], 16384, BF16, "p (k n) -> p k n", k=16) for i in range(2)]
        wsB = [V(wbase[i] + 16384, 16384, BF16, "p (k n) -> p k n", k=16) for i in range(2)]
        wsC = [V(wbase[i] + 32768, 8192, BF16, "p (k n) -> p k n", k=8) for i in range(2)]
        b_ws = [Buf(), Buf()]
        b_sg, b_mt, b_mo = ([Buf(), Buf()] for _ in range(3))

        def load3(cs):
            sl = cs % 2
            c0 = cs * 512
            load_wslab(kb, "pool", wsA[sl], b_ws[sl], woa, [(c0, 512)], 16)
            load_wslab(kb, "pool", wsB[sl], b_ws[sl], wob, [(c0, 512)], 16)
            load_wslab(kb, "pool", wsC[sl], b_ws[sl], woc, [(c0, 512)], 8)
        load3(0)
        load3(1)
        it = 0
        for cs in range(8):
            sl = cs % 2
            for j in range(4):
                crow = cs * 512 + j * 128
                for h in range(2):
                    hs = slice(h * 512, (h + 1) * 512)
                    p = it % 2
                    it += 1
                    bk = p * 3

                    def mm3(e, sl=sl, j=j, hs=hs, bk=bk):
                        ins = None
                        for kc in range(16):
                            ins = e.matmul(kb.psum[bk][:, :], wsA[sl][:, kc, j * 128:(j + 1) * 128], AinT[:, kc, hs],
                                           start=(kc == 0), stop=(kc == 15))
                        for kc in range(16):
                            ins = e.matmul(kb.psum[bk + 1][:, :], wsB[sl][:, kc, j * 128:(j + 1) * 128], ufT[:, kc, hs],
                                           start=(kc == 0), stop=(kc == 15))
                        for kc in range(8):
                            ins = e.matmul(kb.psum[bk + 2][:, :], wsC[sl][:, kc, j * 128:(j + 1) * 128], oT[:, kc, hs],
                                           start=(kc == 0), stop=(kc == 7))
                        return ins
                    kb.op("pe", mm3, reads=[b_ws[sl], b_Ain, b_uf, b_oT], writes=[kb.psb[bk], kb.psb[bk + 1], kb.psb[bk + 2]])
                    for f in range(3):
                        kb.dma("sp", sgt[p][f], sgT[f * D + crow:f * D + crow + 128, hs], writes=[b_sg[p]])
                    for f in range(3):
                        kb.op("dve", lambda e, p=p, f=f, bk=bk: e.tensor_tensor(out=mt[p][f], in0=kb.psum[bk + f][:, :], in1=sgt[p][f],
                                                                                op=ALU.mult),
                              reads=[b_sg[p]], writes=[kb.psb[bk + f], b_mt[p]])
                    kb.op("pool", lambda e, p=p: e.tensor_tensor(out=mt[p][0], in0=mt[p][0], in1=mt[p][1], op=ALU.add),
                          reads=[b_mt[p]], writes=[b_mt[p]])
                    kb.op("pool", lambda e, p=p: e.tensor_tensor(out=mo[p], in0=mt[p][0], in1=mt[p][2], op=ALU.add),
                          reads=[b_mt[p]], writes=[b_mo[p]])
                    kb.dma("act", mergedT[crow:crow + 128, hs], mo[p], reads=[b_mo[p]])
            if cs + 2 < 8:
                load3(cs + 2)
        kb.barrier()

        def tokmajor_residual(mTres, b_mT, wsrc, jgate, xin_d, xo_d, final):
            o = 64 * KBY
            wsl = [V(o + i * 32 * KBY, 32 * KBY, BF16, "p (k n) -> p k n", k=32) for i in range(2)]
            o = 128 * KBY
            gbc = V(o, 16384, F32); o += 16384
            xi = [V(o + i * 2048, 2048, F32) for i in range(3)]; o += 3 * 2048
            tm = [V(o + i * 2048, 2048, F32) for i in range(3)]; o += 3 * 2048
            b_g, b_w = Buf(), [Buf(), Buf()]
            b_xi, b_tm = [Buf() for _ in range(3)], [Buf() for _ in range(3)]
            kb.dma("sp", gbc, modv[jgate:jgate + 1, :].partition_broadcast(128), writes=[b_g])
            for s in range(2):
                load_wslab(kb, "pool", wsl[s], b_w[s], wsrc, [(s * 512, 512)], 32)
            it = 0
            for cs in range(8):
                sl = cs % 2
                cols = slice(cs * 512, (cs + 1) * 512)
                for tt in range(8):
                    rows = slice(tt * 128, (tt + 1) * 128)
                    pb = it % 8
                    q = it % 3
                    it += 1

                    def mm(e, sl=sl, rows=rows, pb=pb):
                        ins = None
                        for kc in range(32):
                            ins = e.matmul(kb.psum[pb][:, :], mTres[:, kc, rows], wsl[sl][:, kc, :], start=(kc == 0), stop=(kc == 31))
                        return ins
                    kb.op("pe", mm, reads=[b_w[sl], b_mT], writes=[kb.psb[pb]])
                    kb.dma("sp", xi[q], xin_d[rows, cols], writes=[b_xi[q]])
                    kb.op("dve", lambda e, pb=pb, q=q, cols=cols: e.tensor_tensor(out=tm[q], in0=kb.psum[pb][:, :], in1=gbc[:, cols], op=ALU.mult),
                          reads=[b_g], writes=[kb.psb[pb], b_tm[q]])
                    kb.op("pool", lambda e, q=q: e.tensor_tensor(out=tm[q], in0=tm[q], in1=xi[q], op=ALU.add),
                          reads=[b_xi[q], b_tm[q]], writes=[b_tm[q]])
                    kb.dma("act", xo_d[rows, cols], tm[q], reads=[b_tm[q]], final=final)
                if cs + 2 < 8:
                    load_wslab(kb, "pool", wsl[sl], b_w[sl], wsrc, [((cs + 2) * 512, 512)], 32)

        mT = V(0, 64 * KBY, BF16, "p (c n) -> p c n", c=32)
        b_mT = Buf()
        for kg in range(4):
            kb.dma("sp", mT[:, kg * 8:(kg + 1) * 8, :], mergedT[kg * 1024:(kg + 1) * 1024, :].rearrange("(k p) n -> p k n", p=128),
                   writes=[b_mT])
        tokmajor_residual(mT, b_mT, wo, 2, x, x1, False)
        kb.barrier()

        hT = V(0, 64 * KBY, BF16, "p (c n) -> p c n", c=32)
        o = 128 * KBY
        xt = [V(o + i * 16384, 16384, F32) for i in range(2)]
        xn = [V(o + 32768 + i * 8192, 8192, BF16) for i in range(2)]
        b_hT = [[Buf(), Buf()] for _ in range(8)]
        hT_all = [b for pp in b_hT for b in pp]
        tagbufs = ([Buf(), Buf()], [Buf(), Buf()], [Buf() for _ in range(8)])
        emit_norm_T(kb, x1, hT, b_hT, modT, 3, 4, gT, xt, xn, ss, {"a": a_t, "sh": None}, ident, b_const, tagbufs)
        kb.barrier()

        o = 64 * KBY
        wsl = [V(o + i * 32 * KBY, 32 * KBY, BF16, "p (k n) -> p k n", k=32) for i in range(2)]
        o = 128 * KBY
        rl = [V(o + i * 2048, 2048, F32) for i in range(4)]; o += 4 * 2048
        ao = [V(o + i * 1024, 1024, BF16) for i in range(4)]; o += 4 * 1024
        b_w, b_rl, b_ao = [Buf(), Buf()], [Buf() for _ in range(4)], [Buf() for _ in range(4)]
        for s in range(2):
            load_wslab(kb, "pool", wsl[s], b_w[s], w1, [(s * 512, 512)], 32)
        it = 0
        for cs in range(DFF // 512):
            sl = cs % 2
            for bi in range(4):
                pbk = (it % 4) * 2
                it += 1

                def mm(e, sl=sl, bi=bi, pbk=pbk):
                    ins = None
                    for kc in range(32):
                        for h in range(2):
                            ins = e.matmul(kb.psum[pbk + h][:, :], wsl[sl][:, kc, bi * 128:(bi + 1) * 128],
                                           hT[:, kc, h * 512:(h + 1) * 512], start=(kc == 0), stop=(kc == 31))
                    return ins
                kb.op("pe", mm, reads=[b_w[sl]] + hT_all, writes=[kb.psb[pbk], kb.psb[pbk + 1]])
                frow = cs * 512 + bi * 128
                for h in range(2):
                    q = (it * 2 + h) % 4
                    kb.op("act", lambda e, q=q, pbk=pbk, h=h: e.activation(out=rl[q], in_=kb.psum[pbk + h][:, :], func=AF.Relu),
                          writes=[kb.psb[pbk + h], b_rl[q]])
                    kb.op("pool", lambda e, q=q: e.tensor_tensor(out=ao[q], in0=rl[q], in1=rl[q], op=ALU.mult),
                          reads=[b_rl[q]], writes=[b_ao[q]])
                    kb.dma("sp", aT[frow:frow + 128, h * 512:(h + 1) * 512], ao[q], reads=[b_ao[q]])
            if cs + 2 < DFF // 512:
                load_wslab(kb, "pool", wsl[sl], b_w[sl], w1, [((cs + 2) * 512, 512)], 32)
        kb.barrier()

        o = 0
        w2s = [V(o + i * 8192, 8192, BF16, "p (k n) -> p k n", k=8) for i in range(3)]; o += 3 * 8192
        ag = [V(o + i * 16384, 16384, BF16, "p (k n) -> p k n", k=8) for i in range(3)]; o += 3 * 16384
        gbc = V(o, 16384, F32); o += 16384
        xi = [V(o + i * 2048, 2048, F32) for i in range(8)]; o += 8 * 2048
        tm = [V(o + i * 2048, 2048, F32) for i in range(4)]; o += 4 * 2048
        b_g, b_w2, b_ag = Buf(), [Buf() for _ in range(3)], [Buf() for _ in range(3)]
        b_xi, b_tm = [Buf() for _ in range(8)], [Buf() for _ in range(4)]
        kb.dma("sp", gbc, modv[5:6, :].partition_broadcast(128), writes=[b_g])
        NKG = DFF // 1024
        steps = [(cs, kg) for cs in range(8) for kg in range(NKG)]

        def issue_loads(i):
            cs, kg = steps[i]
            s3 = i % 3
            kb.dma("pool", w2s[s3], w2[kg * 1024:(kg + 1) * 1024, cs * 512:(cs + 1) * 512].rearrange("(k p) n -> p k n", p=128),
                   writes=[b_w2[s3]])
            kb.dma("sp", ag[s3], aT[kg * 1024:(kg + 1) * 1024, :].rearrange("(k p) n -> p k n", p=128), writes=[b_ag[s3]])
        for i in range(3):
            issue_loads(i)
        it2 = 0
        for i, (cs, kg) in enumerate(steps):
            cols = slice(cs * 512, (cs + 1) * 512)
            s3 = i % 3
            if kg == 0:
                for tt in range(8):
                    kb.dma("sp", xi[tt], x1[tt * 128:(tt + 1) * 128, cols], writes=[b_xi[tt]])
            for tt in range(8):
                def mm(e, s3=s3, kg=kg, tt=tt):
                    ins = None
                    for k in range(8):
                        ins = e.matmul(kb.psum[tt][:, :], ag[s3][:, k, tt * 128:(tt + 1) * 128], w2s[s3][:, k, :],
                                       start=(kg == 0 and k == 0), stop=(kg == NKG - 1 and k == 7))
                    return ins
                kb.op("pe", mm, reads=[b_w2[s3], b_ag[s3]], writes=[kb.psb[tt]])
            if i + 3 < len(steps):
                issue_loads(i + 3)
            if kg == NKG - 1:
                for tt in range(8):
                    rows = slice(tt * 128, (tt + 1) * 128)
                    q = it2 % 4
                    it2 += 1
                    kb.op("dve", lambda e, tt=tt, q=q, cols=cols: e.tensor_tensor(out=tm[q], in0=kb.psum[tt][:, :], in1=gbc[:, cols], op=ALU.mult),
                          reads=[b_g], writes=[kb.psb[tt], b_tm[q]])
                    kb.op("pool", lambda e, q=q, tt=tt: e.tensor_tensor(out=tm[q], in0=tm[q], in1=xi[tt], op=ALU.add),
                          reads=[b_xi[tt], b_tm[q]], writes=[b_tm[q]])
                    kb.dma("act", xout[rows, cols], tm[q], reads=[b_tm[q]], final=True)
        kb.barrier()

        o = 0
        gfb = V(o, 16384, F32); o += 16384
        xt = [V(o + i * 16384, 16384, F32) for i in range(2)]; o += 2 * 16384
        yt = [V(o + i * 16384, 16384, F32) for i in range(2)]; o += 2 * 16384
        b_gf, b_xt, b_yt, b_s2 = Buf(), [Buf(), Buf()], [Buf(), Buf()], [Buf() for _ in range(8)]
        kb.dma("sp", gfb, gfin.partition_broadcast(128), writes=[b_gf])
        for tt in range(8):
            s = tt % 2
            rows = slice(tt * 128, (tt + 1) * 128)
            kb.dma("sp", xt[s], xout[rows, :], writes=[b_xt[s]])
            kb.op("act", lambda e, s=s, tt=tt: e.activation(out=yt[s], in_=xt[s], func=AF.Square, accum_out=ss[:, tt:tt + 1]),
                  reads=[b_xt[s]], writes=[b_yt[s], b_s2[tt]])
            kb.op("dve", lambda e, tt=tt: e.tensor_scalar(out=ss[:, 8 + tt:9 + tt], in0=ss[:, tt:tt + 1], scalar1=1.0 / D, scalar2=EPS,
                                                         op0=ALU.mult, op1=ALU.add), reads=[b_s2[tt]], writes=[b_s2[tt]])
            kb.op("act", lambda e, tt=tt: e.activation(out=ss[:, 16 + tt:17 + tt], in_=ss[:, 8 + tt:9 + tt], func=AF.Sqrt),
                  reads=[b_s2[tt]], writes=[b_s2[tt]])
            kb.op("dve", lambda e, tt=tt: e.reciprocal(out=ss[:, 24 + tt:25 + tt], in_=ss[:, 16 + tt:17 + tt]),
                  reads=[b_s2[tt]], writes=[b_s2[tt]])
            kb.op("dve", lambda e, s=s, tt=tt: e.scalar_tensor_tensor(out=yt[s], in0=xt[s], scalar=ss[:, 24 + tt:25 + tt], in1=gfb,
                                                                     op0=ALU.mult, op1=ALU.mult),
                  reads=[b_xt[s], b_s2[tt], b_gf], writes=[b_yt[s]])
            kb.dma("sp", yout[rows, :], yt[s], reads=[b_yt[s]], final=True)
        kb.finish()
        with nc.Block() as block:
            kb.replay(block)
    return nc


_PROGS = {}


def _prog(name):
    if name not in _PROGS:
        if name == "A":
            _PROGS[name] = build_A(full_slabs_A(), IN_COLS)
        else:
            _PROGS[name] = build_B()
    return _PROGS[name]


def run_A(xcur, modl, g_mix_l, w_in_l, pos):
    nc = _prog("A")
    invf32, jm = rope_consts()
    modT = np.concatenate([t_layout(modl[j * D:(j + 1) * D], 32) for j in range(6)], axis=1)
    gT = t_layout(g_mix_l, 32)
    ident = np.eye(128, dtype=np.float32).astype(NPBF)
    w = np.ascontiguousarray(w_in_l)
    in_maps = []
    for c in range(NCORE):
        in_maps.append({"x": np.ascontiguousarray(xcur[c * TC:(c + 1) * TC]), "modT": modT, "gT": gT, "w_in": w,
                        "pos": np.ascontiguousarray(pos[c * TC:(c + 1) * TC]).reshape(1, TC).astype(np.int32),
                        "invf": invf32, "jm": jm, "ident": ident})
    res = run_bass_kernel_spmd(nc, in_maps, core_ids=list(range(NCORE)))
    return res.results


def glue_B(A_res):
    zT = np.concatenate([r["zT"] for r in A_res], axis=1)
    uT = np.concatenate([r["uT"] for r in A_res], axis=1)
    kT = np.concatenate([r["qkT"][DW_W:] for r in A_res], axis=1)
    v = np.concatenate([r["vtm"] for r in A_res], axis=0)
    PAD = 2048
    zpad = np.concatenate([np.zeros((SC_W, PAD), zT.dtype), zT], axis=1)
    upad = np.concatenate([np.zeros((CF_W, PAD), uT.dtype), uT], axis=1)
    kpad = np.concatenate([np.zeros((DW_W, PAD), kT.dtype), kT], axis=1)
    vpad = np.concatenate([np.zeros((PAD, DW_W), v.dtype), v], axis=0)
    out = []
    for c in range(NCORE):
        t0 = PAD + c * TC
        m = {"zTe": np.ascontiguousarray(zpad[:, t0 - 2:t0 + TC]),
             "uTe": np.ascontiguousarray(upad[:, t0 - 30:t0 + TC]),
             "scbT": A_res[c]["scbT"], "qT": np.ascontiguousarray(A_res[c]["qkT"][:DW_W]), "sgT": A_res[c]["sgT"]}
        vm = np.zeros((128, VM_COLS), np.float32)
        for g, (d, Lg, nkb, QB, nqb) in enumerate(GROUPS):
            W = 128 * d
            m[f"kTe{g}"] = np.ascontiguousarray(kpad[g * 1024:(g + 1) * 1024, t0 - W:t0 + TC])
            m[f"ve{g}"] = np.ascontiguousarray(vpad[t0 - W:t0 + TC, g * 1024:(g + 1) * 1024])
            for r in range(d):
                for kbi in range(nkb):
                    for mm in range(128):
                        e = r + d * (128 * kbi + mm)
                        if e < Lg and (c * TC - W + e) >= 0:
                            vm[mm, VM_OFF[g] + r * nkb + kbi] = 1.0
        m["vm"] = vm
        out.append(m)
    return out


def run_B(xcur, modl, glue, P, l):
    nc = _prog("B")
    modT = np.concatenate([t_layout(modl[j * D:(j + 1) * D], 32) for j in range(6)], axis=1)
    modv = np.ascontiguousarray(modl.reshape(6, D))
    gT = t_layout(P["g_mlp"][l], 32)
    gfin = np.ascontiguousarray(P["g_final"]).reshape(1, D)
    ca, cb = P["conv_a"][l], P["conv_b"][l]
    cva = np.ascontiguousarray(ca.reshape(3, 16, 128).transpose(2, 1, 0)).reshape(128, 48)
    cvb = np.ascontiguousarray(cb.reshape(31, 16, 128).transpose(2, 1, 0)).reshape(128, 16 * 31)
    cvec = np.concatenate([t_layout(P["conv_b_bias"][l], 16), t_layout(P["ln_cf_g"][l], 16), t_layout(P["ln_cf_b"][l], 16)], axis=1)
    mm_, ii_ = np.meshgrid(np.arange(128), np.arange(128), indexing="ij")
    masks = np.concatenate([(mm_ >= ii_), (mm_ <= ii_)], axis=1).astype(np.float32).astype(NPBF)
    shared = {"modT": modT, "modv": modv, "gT": gT, "gfin": gfin, "cva": cva, "cvb": cvb, "cvec": np.ascontiguousarray(cvec),
              "woa": np.ascontiguousarray(P["w_out_a"][l]), "wob": np.ascontiguousarray(P["w_out_b"][l]),
              "woc": np.ascontiguousarray(P["w_out_c"][l]), "wo": np.ascontiguousarray(P["w_o"][l]),
              "w1": np.ascontiguousarray(P["w_mlp1"][l]), "w2": np.ascontiguousarray(P["w_mlp2"][l]),
              "masks": masks, "onesb": np.ones((128, 128), np.float32).astype(NPBF), "onesf": np.ones((128, 128), np.float32),
              "ident": np.eye(128, dtype=np.float32).astype(NPBF)}
    in_maps = []
    for c in range(NCORE):
        m = dict(shared)
        m.update(glue[c])
        m["x"] = np.ascontiguousarray(xcur[c * TC:(c + 1) * TC])
        in_maps.append(m)
    res = run_bass_kernel_spmd(nc, in_maps, core_ids=list(range(NCORE)))
    xo = np.concatenate([r["xout"] for r in res.results], axis=0)
    yo = np.concatenate([r["yout"] for r in res.results], axis=0)
    return xo, yo


def kernel(**inp):
    P = {k: np.asarray(v) for k, v in inp.items()}
    x = P["x"][0].astype(np.float32)
    pos = P["positions"][0]
    mod = run_ada(P["c"].astype(np.float32), P["w_ada"], P["b_ada"])
    xcur = x
    yo = None
    for l in range(2):
        A_res = run_A(xcur, mod[l], P["g_mix"][l], P["w_in"][l], pos)
        glue = glue_B(A_res)
        del A_res
        xcur, yo = run_B(xcur, mod[l], glue, P, l)
    return yo.reshape(1, NCORE * TC, D).astype(np.float32)
```
